# Optimizing a Trainium2 kernel written in Bass

```python
import math
import jax
import jax.numpy as jnp
from jax import lax
import numpy as np


D_MODEL = 2048
BATCH = 1
SEQ = 16384
DEPTH = 2
DEC_BATCH = 8
DEC_SEQ = 2048
PAST_LEN = 128

N_MIXERS = 2
N_HGRN_LAYERS = (DEPTH + 1) // 2
N_DIFF_LAYERS = DEPTH // 2
HGRN_EXPAND = 128
HGRN_HEADS = D_MODEL // HGRN_EXPAND
HGRN_DK = HGRN_EXPAND
HGRN_DV = D_MODEL // HGRN_HEADS
HGRN_DF = HGRN_HEADS * HGRN_DK
HGRN_CHUNK = 64
DIFF_HEAD_DIM = 128
DIFF_HEADS = D_MODEL // (2 * DIFF_HEAD_DIM)
Q_BLOCK = 128
ROPE_THETA = 10000.0
D_FF = -(-8 * D_MODEL // (3 * 256)) * 256
NORM_EPS = 1e-5

kernel_name = 'hybrid_hgrn2_diffattn_encoder'


def rmsnorm(x, w):
    xf = x.astype(jnp.float32)
    y = xf * lax.rsqrt(jnp.mean(xf * xf, axis=-1, keepdims=True) + NORM_EPS) * w.astype(jnp.float32)
    return y.astype(x.dtype)


def swiglu(h, w_gate, w_up, w_down):
    return (jax.nn.silu(h @ w_gate) * (h @ w_up)) @ w_down


def rope_tables(T):
    inv_freq = ROPE_THETA ** (-jnp.arange(0, DIFF_HEAD_DIM, 2, dtype=jnp.float32) / DIFF_HEAD_DIM)
    ang = jnp.arange(T, dtype=jnp.float32)[:, None] * inv_freq[None, :]
    return jnp.cos(ang)[None, :, None, :], jnp.sin(ang)[None, :, None, :]


def apply_rope(x, cos, sin):
    xf = x.astype(jnp.float32)
    x1, x2 = jnp.split(xf, 2, axis=-1)
    return jnp.concatenate([x1 * cos - x2 * sin, x2 * cos + x1 * sin], axis=-1).astype(x.dtype)


def gla_chunked(q, k, v, log_f):
    B, T, H, DK = q.shape
    DV = v.shape[-1]
    n = T // HGRN_CHUNK

    def to_chunks(a):
        return a.reshape(B, n, HGRN_CHUNK, H, a.shape[-1]).transpose(1, 0, 3, 2, 4)

    mask = jnp.tril(jnp.ones((HGRN_CHUNK, HGRN_CHUNK), dtype=bool))[:, :, None]

    def step(S, inp):
        qi, ki, vi, gi = inp
        b = jnp.cumsum(gi, axis=2)
        o_inter = jnp.einsum('bhtd,bhde->bhte', qi * jnp.exp(b), S)
        diff = b[:, :, :, None, :] - b[:, :, None, :, :]
        decay = jnp.exp(jnp.where(mask, diff, -jnp.inf))
        scores = jnp.einsum('bhtd,bhsd,bhtsd->bhts', qi, ki, decay)
        o = o_inter + jnp.einsum('bhts,bhse->bhte', scores, vi)
        b_last = b[:, :, -1:, :]
        S_new = jnp.exp(b[:, :, -1, :])[..., None] * S + jnp.einsum('bhsd,bhse->bhde', ki * jnp.exp(b_last - b), vi)
        return S_new, o

    S0 = jnp.zeros((B, H, DK, DV), jnp.float32)
    _, o = lax.scan(step, S0, (to_chunks(q), to_chunks(k), to_chunks(v), to_chunks(log_f)))
    return o.transpose(1, 0, 3, 2, 4).reshape(B, T, H, DV)


def hgrn2_mixer(h, w_in, lb, w_gnorm, w_out):
    B, T, _ = h.shape
    q, z_fwd, z_bwd, v, g = jnp.split(h @ w_in, 5, axis=-1)
    q = (jax.nn.silu(q.astype(jnp.float32)) * HGRN_DK ** -0.5).reshape(B, T, HGRN_HEADS, HGRN_DK)
    v = v.astype(jnp.float32).reshape(B, T, HGRN_HEADS, HGRN_DV)

    def gates(z, lb_dir):
        f = lb_dir + (1.0 - lb_dir) * jax.nn.sigmoid(z.astype(jnp.float32))
        shp = (B, T, HGRN_HEADS, HGRN_DK)
        return (1.0 - f).reshape(shp), jnp.log(f).reshape(shp)

    k_fwd, logf_fwd = gates(z_fwd, lb[0])
    k_bwd, logf_bwd = gates(z_bwd, lb[1])
    flip = lambda a: jnp.flip(a, axis=1)
    o_fwd = gla_chunked(q, k_fwd, v, logf_fwd)
    o_bwd = flip(gla_chunked(flip(q), flip(k_bwd), flip(v), flip(logf_bwd)))
    o = rmsnorm(o_fwd + o_bwd, w_gnorm) * jax.nn.silu(g.astype(jnp.float32)).reshape(B, T, HGRN_HEADS, HGRN_DV)
    return o.reshape(B, T, D_MODEL).astype(h.dtype) @ w_out


def diff_attention_mixer(h, w_in, lq1, lk1, lq2, lk2, w_subln, w_out, lambda_init):
    B, T, _ = h.shape
    q, k, v = jnp.split(h @ w_in, 3, axis=-1)
    q = q.reshape(B, T, 2 * DIFF_HEADS, DIFF_HEAD_DIM)
    k = k.reshape(B, T, 2 * DIFF_HEADS, DIFF_HEAD_DIM)
    v = v.reshape(B, T, DIFF_HEADS, 2 * DIFF_HEAD_DIM).astype(jnp.float32)
    cos, sin = rope_tables(T)
    q = apply_rope(q, cos, sin) * DIFF_HEAD_DIM ** -0.5
    k = apply_rope(k, cos, sin)
    lam = (jnp.exp(jnp.sum(lq1.astype(jnp.float32) * lk1.astype(jnp.float32)))
           - jnp.exp(jnp.sum(lq2.astype(jnp.float32) * lk2.astype(jnp.float32))) + lambda_init)
    n_blocks = T // Q_BLOCK
    q_blocks = q.reshape(B, n_blocks, Q_BLOCK, 2 * DIFF_HEADS, DIFF_HEAD_DIM).transpose(1, 0, 2, 3, 4)

    def attend(qb):
        s = jnp.einsum('bqnd,bknd->bnqk', qb, k).astype(jnp.float32)
        p = jax.nn.softmax(s, axis=-1).reshape(B, DIFF_HEADS, 2, Q_BLOCK, T)
        a = p[:, :, 0] - lam * p[:, :, 1]
        return jnp.einsum('bhqk,bkhe->bqhe', a, v)

    o = lax.map(attend, q_blocks)
    o = o.transpose(1, 0, 2, 3, 4).reshape(B, T, DIFF_HEADS, 2 * DIFF_HEAD_DIM)
    o = rmsnorm(o, w_subln) * (1.0 - lambda_init)
    return o.reshape(B, T, D_MODEL).astype(h.dtype) @ w_out


def trunk(x, norm_mixer, norm_ffn, norm_final, hgrn_w_in, hgrn_lower_bound, hgrn_gnorm, hgrn_w_out,
          diff_w_in, diff_lambda_q1, diff_lambda_k1, diff_lambda_q2, diff_lambda_k2, diff_subln, diff_w_out,
          ffn_w_gate, ffn_w_up, ffn_w_down):
    lb_table = jnp.cumsum(jax.nn.softmax(hgrn_lower_bound.astype(jnp.float32), axis=1), axis=1)
    for i in range(DEPTH):
        h = rmsnorm(x, norm_mixer[i])
        j = i // N_MIXERS
        if i % N_MIXERS == 0:
            mix = hgrn2_mixer(h, hgrn_w_in[j], lb_table[:, i], hgrn_gnorm[j], hgrn_w_out[j])
        else:
            lambda_init = 0.8 - 0.6 * math.exp(-0.3 * i)
            mix = diff_attention_mixer(h, diff_w_in[j], diff_lambda_q1[j], diff_lambda_k1[j], diff_lambda_q2[j],
                                       diff_lambda_k2[j], diff_subln[j], diff_w_out[j], lambda_init)
        x = x + mix
        x = x + swiglu(rmsnorm(x, norm_ffn[i]), ffn_w_gate[i], ffn_w_up[i], ffn_w_down[i])
    return rmsnorm(x, norm_final)


def setup_inputs(seed: int = 0) -> dict:
    key = jax.random.key(seed)
    ks = jax.random.split(key, 20)
    f32 = jnp.float32

    def dense(k, shape):
        return jax.random.normal(k, shape, f32) * shape[-2] ** -0.5

    def gain(k, shape):
        return 1.0 + 0.02 * jax.random.normal(k, shape, f32)

    return {
        'x_prompt': jax.random.normal(ks[0], (BATCH, SEQ, D_MODEL), f32),
        'x_sample': jax.random.normal(ks[1], (DEC_BATCH, DEC_SEQ, D_MODEL), f32),
        'norm_mixer': gain(ks[2], (DEPTH, D_MODEL)),
        'norm_ffn': gain(ks[3], (DEPTH, D_MODEL)),
        'norm_final': gain(ks[4], (D_MODEL,)),
        'hgrn_w_in': dense(ks[5], (N_HGRN_LAYERS, D_MODEL, 3 * HGRN_DF + 2 * D_MODEL)),
        'hgrn_lower_bound': 0.5 * jax.random.normal(ks[6], (2, DEPTH + 1, HGRN_DF), f32),
        'hgrn_gnorm': gain(ks[7], (N_HGRN_LAYERS, HGRN_DV)),
        'hgrn_w_out': dense(ks[8], (N_HGRN_LAYERS, D_MODEL, D_MODEL)),
        'diff_w_in': dense(ks[9], (N_DIFF_LAYERS, D_MODEL, 3 * D_MODEL)),
        'diff_lambda_q1': 0.1 * jax.random.normal(ks[10], (N_DIFF_LAYERS, DIFF_HEAD_DIM), f32),
        'diff_lambda_k1': 0.1 * jax.random.normal(ks[11], (N_DIFF_LAYERS, DIFF_HEAD_DIM), f32),
        'diff_lambda_q2': 0.1 * jax.random.normal(ks[12], (N_DIFF_LAYERS, DIFF_HEAD_DIM), f32),
        'diff_lambda_k2': 0.1 * jax.random.normal(ks[13], (N_DIFF_LAYERS, DIFF_HEAD_DIM), f32),
        'diff_subln': gain(ks[14], (N_DIFF_LAYERS, 2 * DIFF_HEAD_DIM)),
        'diff_w_out': dense(ks[15], (N_DIFF_LAYERS, D_MODEL, D_MODEL)),
        'ffn_w_gate': dense(ks[16], (DEPTH, D_MODEL, D_FF)),
        'ffn_w_up': dense(ks[17], (DEPTH, D_MODEL, D_FF)),
        'ffn_w_down': dense(ks[18], (DEPTH, D_FF, D_MODEL)),
    }


def reference(x_prompt, x_sample, norm_mixer, norm_ffn, norm_final, hgrn_w_in, hgrn_lower_bound, hgrn_gnorm,
              hgrn_w_out, diff_w_in, diff_lambda_q1, diff_lambda_k1, diff_lambda_q2, diff_lambda_k2, diff_subln,
              diff_w_out, ffn_w_gate, ffn_w_up, ffn_w_down):
    y_prompt = trunk(x_prompt, norm_mixer, norm_ffn, norm_final, hgrn_w_in, hgrn_lower_bound, hgrn_gnorm,
                     hgrn_w_out, diff_w_in, diff_lambda_q1, diff_lambda_k1, diff_lambda_q2, diff_lambda_k2,
                     diff_subln, diff_w_out, ffn_w_gate, ffn_w_up, ffn_w_down)
    y_sample = trunk(x_sample, norm_mixer, norm_ffn, norm_final, hgrn_w_in, hgrn_lower_bound, hgrn_gnorm,
                     hgrn_w_out, diff_w_in, diff_lambda_q1, diff_lambda_k1, diff_lambda_q2, diff_lambda_k2,
                     diff_subln, diff_w_out, ffn_w_gate, ffn_w_up, ffn_w_down)
    return (y_prompt, y_sample)
```

```python
import math
from contextlib import ExitStack
import numpy as np
import concourse.bass as bass
import concourse.mybir as mybir
from concourse.bass_utils import run_bass_kernel_spmd

F32 = mybir.dt.float32
BF16 = mybir.dt.bfloat16
AF = mybir.ActivationFunctionType
ALU = mybir.AluOpType

NCORES = 8
D = 2048
KC = 16
TS = 2048
NT = TS // 128
NSTEP = TS // 32
DFF = 5632
NJ = DFF // 128
EPS = 1e-5
LAMBDA_INIT = 0.8 - 0.6 * math.exp(-0.3 * 1)
QSCALE = 128 ** -0.5


class Tracker:
    def __init__(self, nc):
        self.nc = nc
        self.eng = {'pe': nc.tensor, 'act': nc.scalar, 'dve': nc.vector, 'pool': nc.gpsimd, 'sp': nc.sync}
        self.sems = {e: None for e in self.eng}
        self.cnt = {e: 0 for e in self.eng}
        self.nsem = 0
        self.waited = {}
        self.lastw = {}
        self.readers = {}
        self.dsem = {}
        self.latest = {}
        self.ninst = 0

    def _new(self, tag):
        while True:
            self.nsem += 1
            h = self.nc.alloc_semaphore(f"{tag}_{self.nsem}")
            if not (160 <= h.num <= 199):
                return h

    def _deps(self, reads, writes):
        deps = []
        for r in reads:
            d = self.lastw.get(r)
            if d is not None:
                deps.append(d)
        for w in writes:
            d = self.lastw.get(w)
            if d is not None:
                deps.append(d)
            deps.extend(self.readers.get(w, {}).values())
        return deps

    def _waits(self, e, deps):
        best = {}
        for (sem, val, src) in deps:
            if src == 'pe' and e == 'pe':
                continue
            k = sem.name
            if val > best.get(k, (None, 0))[1]:
                best[k] = (sem, val)
        for k, (sem, val) in best.items():
            if self.waited.get((e, k), 0) >= val:
                continue
            self.waited[(e, k)] = val
            self.eng[e].wait_ge(sem, val)
            self.ninst += 1

    def _record(self, d, reads, writes):
        for r in reads:
            self.readers.setdefault(r, {})[d[0].name] = d
        for w in writes:
            self.lastw[w] = d
            self.readers[w] = {}
        self.latest[d[0].name] = (d[0], d[1])

    def op(self, e, fn, reads=(), writes=()):
        self._waits(e, self._deps(reads, writes))
        if self.sems[e] is None or self.cnt[e] >= 60000:
            self.sems[e] = self._new("s" + e)
            self.cnt[e] = 0
        ins = fn(self.eng[e])
        self.cnt[e] += 1
        ins.then_inc(self.sems[e], 1)
        self.ninst += 1
        self._record((self.sems[e], self.cnt[e], e), reads, writes)

    def dma(self, q, out, in_, reads=(), writes=(), stream="d", **kw):
        RING = 4
        st = self.dsem.setdefault(stream, {'i': 0, 'slots': [None] * RING})
        i = st['i'] % RING
        st['i'] += 1
        slot = st['slots'][i]
        deps = self._deps(reads, writes)
        if slot is not None:
            deps.append((slot[0], slot[1], 'dma'))
        self._waits(q, deps)
        if slot is None or slot[1] + 16 > 60000:
            slot = [self._new("d" + stream), 0]
            st['slots'][i] = slot
        ins = self.eng[q].dma_start(out=out, in_=in_, **kw)
        slot[1] += 16
        ins.then_inc(slot[0], 16)
        self.ninst += 1
        self._record((slot[0], slot[1], 'dma'), reads, writes)

    def barrier(self, engines=None):
        for e in (engines or self.eng):
            deps = [(sem, val, 'x') for (sem, val) in self.latest.values()]
            self._waits(e, deps)

    def final_wait(self, e='sp'):
        deps = [(sem, val, 'x') for (sem, val) in self.latest.values()]
        self._waits(e, deps)


def build_nc(stage=99, debug=False, fake_gather=False, slots=None, no_exchange=False):
    nc = bass.Bass("TRN2", target_bir_lowering=False)
    T = Tracker(nc)
    uid = [0]

    def uname(n):
        uid[0] += 1
        return f"{n}_{uid[0]}"

    def din(name, shape, dt=F32):
        return nc.dram_tensor(name, list(shape), dt, kind="ExternalInput").ap()

    x_in = din("x", [2, TS, D])
    norm_mixer = din("norm_mixer", [2, D])
    norm_ffn = din("norm_ffn", [2, D])
    norm_final = din("norm_final", [D])
    hgrn_w_in = din("hgrn_w_in", [1, D, 5 * D])
    hgrn_lb = din("hgrn_lower_bound", [2, 3, D])
    hgrn_gnorm = din("hgrn_gnorm", [1, 128])
    hgrn_w_out = din("hgrn_w_out", [1, D, D])
    diff_w_in = din("diff_w_in", [1, D, 3 * D])
    lq1 = din("diff_lambda_q1", [1, 128])
    lk1 = din("diff_lambda_k1", [1, 128])
    lq2 = din("diff_lambda_q2", [1, 128])
    lk2 = din("diff_lambda_k2", [1, 128])
    diff_subln = din("diff_subln", [1, 256])
    diff_w_out = din("diff_w_out", [1, D, D])
    ffn_w_gate = din("ffn_w_gate", [2, D, DFF])
    ffn_w_up = din("ffn_w_up", [2, D, DFF])
    ffn_w_down = din("ffn_w_down", [2, DFF, D])
    c_ident = din("c_ident", [128, 128])
    c_masks = din("c_masks", [128, 2, 32])
    c_rope = din("c_rope", [2, TS, 2, 64])
    c_cmask = din("c_cmask", [128, 2, NCORES])

    y_out = nc.dram_tensor("y", [2, TS, D], F32, kind="ExternalOutput").ap()

    onT_d = nc.dram_tensor("onT_d", [2, KC, 128, TS], BF16).ap()
    dbg = {}
    if debug:
        dbg['onT'] = nc.dram_tensor("dbg_onT", [2, KC, 128, TS], BF16, kind="ExternalOutput").ap()

    ident = nc.alloc_sbuf_tensor("ident", [128, 128], BF16)
    ones_bf = nc.alloc_sbuf_tensor("ones_bf", [128, 128], BF16)
    masks = nc.alloc_sbuf_tensor("masks", [128, 2, 32], F32)
    lbt = nc.alloc_sbuf_tensor("lbt", [128, 2, 16, 3], F32)
    lb0 = nc.alloc_sbuf_tensor("lb0", [128, 2, 16], F32)
    oml = nc.alloc_sbuf_tensor("oml", [128, 2, 16], F32)
    gw = nc.alloc_sbuf_tensor("gw", [128, 1], F32)
    epsb = nc.alloc_sbuf_tensor("epsb", [128, 1], F32)

    T.dma('pool', ident[:], c_ident, writes=['ident'], stream='c')
    T.dma('sp', masks[:], c_masks, writes=['masks'], stream='c')
    T.op('dve', lambda e: e.memset(ones_bf[:], 1.0), writes=['ones'])
    T.op('dve', lambda e: e.memset(epsb[:], EPS), writes=['epsb'])
    for r in range(2):
        for l in range(3):
            T.dma('sp', lbt[:, r, :, l], hgrn_lb[r, l, :].rearrange("(h d) -> d h", d=128), writes=['lbt'], stream='c',
                  allow_slow_non_contiguous=True)
    T.dma('sp', gw[:], hgrn_gnorm[0, :].rearrange("(d o) -> d o", o=1), writes=['gw'], stream='c',
          allow_slow_non_contiguous=True)
    T.op('act', lambda e: e.activation(out=lbt[:], in_=lbt[:], func=AF.Exp), reads=['lbt'], writes=['lbt'])
    T.op('dve', lambda e: e.tensor_reduce(out=oml[:], in_=lbt[:], axis=mybir.AxisListType.X, op=ALU.add), reads=['lbt'], writes=['oml'])
    T.op('dve', lambda e: e.reciprocal(out=oml[:], in_=oml[:]), reads=['oml'], writes=['oml'])
    T.op('dve', lambda e: e.tensor_tensor(out=lb0[:], in0=lbt[:, :, :, 0], in1=oml[:], op=ALU.mult), reads=['lbt', 'oml'], writes=['lb0'])
    T.op('dve', lambda e: e.tensor_scalar(out=oml[:], in0=lb0[:], scalar1=-1.0, scalar2=1.0, op0=ALU.mult, op1=ALU.add), reads=['lb0'], writes=['oml'])

    zeros = nc.alloc_sbuf_tensor("zeros", [128, 32], F32)
    T.op('dve', lambda e: e.memset(zeros[:], 0.0), writes=['zeros'])

    def finalize_head(s, h, Ob, gs, ke, E1, kdT, pp, st):
        oall = [('O', tt) for tt in range(NT)]
        T.op('act', lambda e: e.activation(out=ke[:], in_=Ob[:], func=AF.Square), reads=oall, writes=['ke'])
        for tg in range(4):
            pb = st['pp'] % 2
            st['pp'] += 1
            T.op('pe', lambda e: e.matmul(pp[:, pb, :], lhsT=ones_bf[:], rhs=ke[:, tg * 512:(tg + 1) * 512],
                                          start=True, stop=True),
                 reads=['ke', 'ones'], writes=[('pp', pb)])
            T.op('act', lambda e: e.activation(out=E1[:, tg * 512:(tg + 1) * 512], in_=pp[:, pb, :], func=AF.Ln,
                                               scale=1.0 / 128, bias=epsb[:]),
                 reads=[('pp', pb), 'epsb'], writes=['E1'])
        T.op('act', lambda e: e.activation(out=E1[:], in_=E1[:], func=AF.Exp, scale=-0.5), reads=['E1'], writes=['E1'])
        T.op('dve', lambda e: e.scalar_tensor_tensor(out=Ob[:], in0=Ob[:], scalar=gw[:, 0:1], in1=E1[:],
                                                     op0=ALU.mult, op1=ALU.mult),
             reads=oall + ['E1', 'gw'], writes=oall)
        T.op('pool', lambda e: e.tensor_tensor(out=kdT[:], in0=Ob[:], in1=gs[:], op=ALU.mult),
             reads=oall + ['gs'], writes=['kdT'])
        T.dma('sp', onT_d[s, h], kdT[:], reads=['kdT'], writes=[('onT_d', s, h)], stream='o')
        if debug:
            T.dma('sp', dbg['onT'][s, h], kdT[:], reads=['kdT'], writes=[('dbg_onT', s, h)], stream='o')

    Osave_d = nc.dram_tensor("Osave_d", [KC, 128, TS], F32).ap()
    gs_d = nc.dram_tensor("gs_d", [KC, 128, TS], BF16).ap()
    qeb_d = nc.dram_tensor("qeb_d", [KC, 2, 128, TS], BF16).ap()
    Send_d = nc.dram_tensor("Send_d", [KC * 2 * 128, 128], F32)
    SG = nc.dram_tensor("SG", [NCORES * KC * 2 * 128, 128], F32)
    Dtot_d = nc.dram_tensor("Dtot_d", [128, 256], F32)
    DG = nc.dram_tensor("DG", [NCORES * 128, 256], F32)
    Dtot_sb = nc.alloc_sbuf_tensor("Dtot_sb", [128, 256], F32)
    T.op('dve', lambda e: e.memset(Dtot_sb[:], 0.0), writes=['Dtot_sb'])
    zeros64 = nc.alloc_sbuf_tensor("zeros64", [128, NSTEP], F32)
    T.op('dve', lambda e: e.memset(zeros64[:], 0.0), writes=['zeros64'])

    def l0_mixer(s):
        defer = (s == 0) and not no_exchange
        with nc.sbuf_tensor(uname("hnT"), [128, KC, TS], BF16) as hnT:
            with ExitStack() as es:
                gam = es.enter_context(nc.sbuf_tensor(uname("gam"), [128, D], F32))
                xt = es.enter_context(nc.sbuf_tensor(uname("xt"), [128, 2, D], F32))
                hn = es.enter_context(nc.sbuf_tensor(uname("hn"), [128, 2, D], BF16))
                ss = es.enter_context(nc.sbuf_tensor(uname("ss"), [128, NT], F32))
                pst = es.enter_context(nc.psum_tensor(uname("pst"), [128, 2, KC, 128], BF16))
                T.dma('sp', gam[:], norm_mixer[0, :].partition_broadcast(128), writes=['gam'], stream='c')
                for tt in range(NT):
                    b = tt % 2
                    T.dma('sp', xt[:, b, :], x_in[s, tt * 128:(tt + 1) * 128, :], writes=[('xt', b)], stream='x')
                    T.op('act', lambda e: e.activation(out=hn[:, b, :], in_=xt[:, b, :], func=AF.Square,
                                                       accum_out=ss[:, tt:tt + 1]),
                         reads=[('xt', b)], writes=[('hn', b), ('ss', tt)])
                    T.op('act', lambda e: e.activation(out=ss[:, tt:tt + 1], in_=ss[:, tt:tt + 1], func=AF.Ln,
                                                       scale=1.0 / D, bias=epsb[:]),
                         reads=[('ss', tt), 'epsb'], writes=[('ss', tt)])
                    T.op('act', lambda e: e.activation(out=ss[:, tt:tt + 1], in_=ss[:, tt:tt + 1], func=AF.Exp, scale=-0.5),
                         reads=[('ss', tt)], writes=[('ss', tt)])
                    T.op('dve', lambda e: e.scalar_tensor_tensor(out=hn[:, b, :], in0=xt[:, b, :], scalar=ss[:, tt:tt + 1],
                                                                 in1=gam[:], op0=ALU.mult, op1=ALU.mult),
                         reads=[('xt', b), ('ss', tt), 'gam'], writes=[('hn', b)])

                    def tr(e):
                        for kc in range(KC):
                            ins = e.transpose(out=pst[:, b, kc, :], in_=hn[:, b, kc * 128:(kc + 1) * 128], identity=ident[:])
                        return ins
                    T.op('pe', tr, reads=[('hn', b), 'ident'], writes=[('pst', b)])
                    T.op('act', lambda e: e.copy(out=hnT[:, :, tt * 128:(tt + 1) * 128], in_=pst[:, b, :, :]),
                         reads=[('pst', b)], writes=['hnT'])
                T.barrier()
            with ExitStack() as es:
                W = es.enter_context(nc.sbuf_tensor(uname("W"), [128, 5, KC, 128], BF16))
                qs = es.enter_context(nc.sbuf_tensor(uname("qs"), [128, TS], F32))
                gs = es.enter_context(nc.sbuf_tensor(uname("gs"), [128, TS], BF16))
                vtm = es.enter_context(nc.sbuf_tensor(uname("vtm"), [128, NT, 128], BF16))
                Vbd = es.enter_context(nc.sbuf_tensor(uname("Vbd"), [128, NT, 4, 128], BF16))
                Fb = es.enter_context(nc.sbuf_tensor(uname("Fb"), [128, TS], F32))
                LF = es.enter_context(nc.sbuf_tensor(uname("LF"), [128, TS], F32))
                Bb = es.enter_context(nc.sbuf_tensor(uname("Bb"), [128, TS], F32))
                E1 = es.enter_context(nc.sbuf_tensor(uname("E1"), [128, TS], F32))
                qe = es.enter_context(nc.sbuf_tensor(uname("qe"), [128, 2, TS], BF16))
                ke = es.enter_context(nc.sbuf_tensor(uname("ke"), [128, TS], BF16))
                kdT = es.enter_context(nc.sbuf_tensor(uname("kdT"), [128, TS], BF16))
                kdtm = es.enter_context(nc.sbuf_tensor(uname("kdtm"), [128, 2, NT, 128], BF16))
                scT = es.enter_context(nc.sbuf_tensor(uname("scT"), [128, 2, NT, 128], BF16))
                Ob = es.enter_context(nc.sbuf_tensor(uname("Ob"), [128, TS], F32))
                TOT = es.enter_context(nc.sbuf_tensor(uname("TOT"), [128, 2, NSTEP], F32))
                Dm = es.enter_context(nc.sbuf_tensor(uname("Dm"), [128, 2, NSTEP], F32))
                Sst = es.enter_context(nc.sbuf_tensor(uname("Sst"), [128, 2, 128], F32))
                Sbf = es.enter_context(nc.sbuf_tensor(uname("Sbf"), [128, 2, 2, 128], BF16))
                pp = es.enter_context(nc.psum_tensor(uname("pp"), [128, 2, 512], F32))
                pm = es.enter_context(nc.psum_tensor(uname("pm"), [128, NT, 32], F32))
                ptr = es.enter_context(nc.psum_tensor(uname("ptr"), [128, 8, 128], BF16))
                pu = es.enter_context(nc.psum_tensor(uname("pu"), [128, 2, 4, 128], F32))
                po = es.enter_context(nc.psum_tensor(uname("po"), [128, 2, 4, 128], F32))
                st = {'pp': 0, 'pu': 0}
                T.op('pool', lambda e: e.memset(Vbd[:], 0.0), writes=['Vbd'])
                T.op('pool', lambda e: e.memset(scT[:], 0.0), writes=[('scT', 0), ('scT', 1)])
                B3 = Bb[:].rearrange("p (n t) -> p n t", t=32)
                LF3 = LF[:].rearrange("p (n t) -> p n t", t=32)

                def load_w(h):
                    for blk in range(5):
                        c0 = blk * D + h * 128
                        T.dma('pool', W[:, blk, :, :],
                              hgrn_w_in[0, :, c0:c0 + 128].rearrange("(kc p) c -> p kc c", p=128),
                              writes=[('W', blk)], stream='w')

                def proj_fm(blk, func, dst, dname):
                    for tg in range(4):
                        pb = st['pp'] % 2
                        st['pp'] += 1

                        def mm(e):
                            for kc in range(KC):
                                ins = e.matmul(pp[:, pb, :], lhsT=W[:, blk, kc, :], rhs=hnT[:, kc, tg * 512:(tg + 1) * 512],
                                               start=(kc == 0), stop=(kc == KC - 1))
                            return ins
                        T.op('pe', mm, reads=[('W', blk)], writes=[('pp', pb)])
                        T.op('act', lambda e: e.activation(out=dst[:, tg * 512:(tg + 1) * 512], in_=pp[:, pb, :], func=func),
                             reads=[('pp', pb)], writes=[dname])

                def proj_v():
                    for t4 in range(4):
                        pb = st['pp'] % 2
                        st['pp'] += 1

                        def mm(e):
                            for j in range(4):
                                tt = t4 * 4 + j
                                for kc in range(KC):
                                    ins = e.matmul(pp[:, pb, j * 128:(j + 1) * 128], lhsT=hnT[:, kc, tt * 128:(tt + 1) * 128],
                                                   rhs=W[:, 3, kc, :], start=(kc == 0), stop=(kc == KC - 1))
                            return ins
                        T.op('pe', mm, reads=[('W', 3)], writes=[('pp', pb)])
                        T.op('act', lambda e: e.copy(out=vtm[:, t4 * 4:(t4 + 1) * 4, :],
                                                     in_=pp[:, pb, :].rearrange("p (j e) -> p j e", j=4)),
                             reads=[('pp', pb)], writes=['vtm'])
                    for j in range(4):
                        T.op('pool', lambda e: e.tensor_copy(out=Vbd[32 * j:32 * j + 32, :, j, :], in_=vtm[32 * j:32 * j + 32, :, :]),
                             reads=['vtm'], writes=['Vbd'])

                def prep(h, r):
                    T.op('dve', lambda e: e.tensor_scalar(out=Fb[:], in0=Fb[:], scalar1=oml[:, r, h:h + 1], scalar2=lb0[:, r, h:h + 1],
                                                          op0=ALU.mult, op1=ALU.add),
                         reads=['F', 'oml', 'lb0'], writes=['F'])
                    T.op('act', lambda e: e.activation(out=LF[:], in_=Fb[:], func=AF.Ln), reads=['F'], writes=['LF'])
                    T.op('pool', lambda e: e.tensor_scalar(out=Fb[:], in0=Fb[:], scalar1=-1.0, scalar2=1.0, op0=ALU.mult, op1=ALU.add),
                         reads=['F'], writes=['F'])

                    def scans(e):
                        for n in range(NSTEP):
                            ins = e.tensor_tensor_scan(out=Bb[:, n * 32:(n + 1) * 32], data0=zeros[:], data1=LF[:, n * 32:(n + 1) * 32],
                                                       initial=0.0, op0=ALU.add, op1=ALU.add)
                        return ins
                    T.op('dve', scans, reads=['LF', 'zeros'], writes=['B'])
                    T.op('dve', lambda e: e.tensor_copy(out=TOT[:, r, :], in_=B3[:, :, 31]), reads=['B'], writes=[('TOT', r)])
                    if r == 1:
                        T.op('dve', lambda e: e.tensor_tensor(out=LF[:], in0=LF[:], in1=Bb[:], op=ALU.subtract),
                             reads=['LF', 'B'], writes=['LF'])
                        T.op('dve', lambda e: e.tensor_tensor(out=B3, in0=LF3,
                                                              in1=TOT[:, r, :].unsqueeze(2).to_broadcast([128, NSTEP, 32]), op=ALU.add),
                             reads=['LF', ('TOT', r)], writes=['B'])
                    T.op('act', lambda e: e.activation(out=LF[:], in_=Bb[:], func=AF.Exp), reads=['B'], writes=['LF'])
                    T.op('dve', lambda e: e.scalar_tensor_tensor(out=qe[:, r, :], in0=qs[:], scalar=QSCALE, in1=LF[:],
                                                                 op0=ALU.mult, op1=ALU.mult),
                         reads=['qs', 'LF'], writes=[('qe', r)])
                    T.op('act', lambda e: e.activation(out=E1[:], in_=Bb[:], func=AF.Exp, scale=-1.0), reads=['B'], writes=['E1'])
                    T.op('pool', lambda e: e.tensor_tensor(out=ke[:], in0=Fb[:], in1=E1[:], op=ALU.mult),
                         reads=['F', 'E1'], writes=['ke'])
                    T.op('dve', lambda e: e.tensor_tensor(out=B3, in0=TOT[:, r, :].unsqueeze(2).to_broadcast([128, NSTEP, 32]),
                                                          in1=B3, op=ALU.subtract),
                         reads=['B', ('TOT', r)], writes=['B'])
                    T.op('act', lambda e: e.activation(out=LF[:], in_=Bb[:], func=AF.Exp), reads=['B'], writes=['LF'])
                    T.op('pool', lambda e: e.tensor_tensor(out=kdT[:], in0=Fb[:], in1=LF[:], op=ALU.mult),
                         reads=['F', 'LF'], writes=['kdT'])
                    T.op('act', lambda e: e.activation(out=Dm[:, r, :], in_=TOT[:, r, :], func=AF.Exp),
                         reads=[('TOT', r)], writes=[('Dm', r)])
                    for g8 in range(2):
                        def trk(e):
                            for q in range(8):
                                tt = g8 * 8 + q
                                ins = e.transpose(out=ptr[:, q, :], in_=kdT[:, tt * 128:(tt + 1) * 128], identity=ident[:])
                            return ins
                        T.op('pe', trk, reads=['kdT'], writes=['ptr'])
                        T.op('act', lambda e: e.copy(out=kdtm[:, r, g8 * 8:(g8 + 1) * 8, :], in_=ptr[:]),
                             reads=['ptr'], writes=[('kdtm', r)])

                    def scores(e):
                        for n in range(NSTEP):
                            tt, j = divmod(n, 4)
                            ins = e.matmul(pm[32 * j:32 * j + 32, tt, :], lhsT=ke[:, n * 32:(n + 1) * 32],
                                           rhs=qe[:, r, n * 32:(n + 1) * 32], start=True, stop=True, tile_position=(0, 32 * j))
                        return ins
                    T.op('pe', scores, reads=['ke', ('qe', r)], writes=['pm'])
                    for j in range(4):
                        T.op('dve', lambda e: e.tensor_tensor(out=scT[32 * j:32 * j + 32, r, :, 32 * j:32 * j + 32],
                                                              in0=pm[32 * j:32 * j + 32, :, :],
                                                              in1=masks[32 * j:32 * j + 32, r, :].unsqueeze(1).to_broadcast([32, NT, 32]),
                                                              op=ALU.mult),
                             reads=['pm', 'masks'], writes=[('scT', r)])

                INC = es.enter_context(nc.sbuf_tensor(uname("INC"), [128, 2, NSTEP], F32))
                EO = es.enter_context(nc.sbuf_tensor(uname("EO"), [128, 2, NSTEP], F32))

                def qeb_part(h, r):
                    T.op('dve', lambda e: e.tensor_tensor_scan(out=INC[:, r, :], data0=zeros64[:], data1=TOT[:, r, :], initial=0.0,
                                                               op0=ALU.add, op1=ALU.add),
                         reads=[('TOT', r), 'zeros64'], writes=[('INC', r)])
                    if r == 0:
                        T.op('dve', lambda e: e.tensor_tensor(out=EO[:, r, :], in0=INC[:, r, :], in1=TOT[:, r, :], op=ALU.subtract),
                             reads=[('INC', r), ('TOT', r)], writes=[('EO', r)])
                    else:
                        T.op('dve', lambda e: e.tensor_scalar(out=EO[:, r, :], in0=INC[:, r, :], scalar1=-1.0,
                                                              scalar2=INC[:, r, NSTEP - 1:NSTEP], op0=ALU.mult, op1=ALU.add),
                             reads=[('INC', r)], writes=[('EO', r)])
                    T.op('act', lambda e: e.activation(out=EO[:, r, :], in_=EO[:, r, :], func=AF.Exp), reads=[('EO', r)], writes=[('EO', r)])
                    T.op('act', lambda e: e.activation(out=Dtot_sb[:, r * KC + h:r * KC + h + 1], in_=INC[:, r, NSTEP - 1:NSTEP], func=AF.Exp),
                         reads=[('INC', r)], writes=['Dtot_sb'])
                    T.op('dve', lambda e: e.tensor_tensor(out=ke[:].rearrange("p (n t) -> p n t", t=32),
                                                          in0=qe[:, r, :].rearrange("p (n t) -> p n t", t=32),
                                                          in1=EO[:, r, :].unsqueeze(2).to_broadcast([128, NSTEP, 32]), op=ALU.mult),
                         reads=[('qe', r), ('EO', r)], writes=['ke'])
                    T.dma('sp', qeb_d[h, r], ke[:], reads=['ke'], writes=[('qeb_d', h)], stream='o')

                def recur(h):
                    owritten = set()
                    uslot = {}
                    for i in range(NSTEP):
                        for r in range(2):
                            n = i if r == 0 else NSTEP - 1 - i
                            tt, j = divmod(n, 4)
                            first_of_tile = (j == 0) if r == 0 else (j == 3)
                            last_of_tile = (j == 3) if r == 0 else (j == 0)
                            osl = tt % 4
                            if first_of_tile:
                                us = st['pu'] % 2
                                st['pu'] += 1
                                uslot[r] = us
                                T.op('pe', lambda e: e.matmul(pu[:, us, :, :], lhsT=kdtm[:, r, tt, :], rhs=Vbd[:, tt, :, :],
                                                              start=True, stop=True),
                                     reads=[('kdtm', r), 'Vbd'], writes=[('pu', us)])
                                T.op('pe', lambda e: e.matmul(po[:, r, osl, :], lhsT=vtm[:, tt, :], rhs=scT[:, r, tt, :],
                                                              start=True, stop=False),
                                     reads=['vtm', ('scT', r)], writes=[('po', r, osl)])
                            us = uslot[r]
                            if i > 0:
                                T.op('pe', lambda e: e.matmul(po[:, r, osl, j * 32:(j + 1) * 32], lhsT=Sbf[:, r, i % 2, :],
                                                              rhs=qe[:, r, n * 32:(n + 1) * 32], start=False, stop=True),
                                     reads=[('Sbf', r, i % 2), ('qe', r)], writes=[('po', r, osl)])
                            if i < NSTEP - 1 or defer:
                                if i == 0:
                                    T.op('dve', lambda e: e.tensor_copy(out=Sst[:, r, :], in_=pu[:, us, j, :]),
                                         reads=[('pu', us)], writes=[('S', r)])
                                else:
                                    T.op('dve', lambda e: e.scalar_tensor_tensor(out=Sst[:, r, :], in0=Sst[:, r, :],
                                                                                 scalar=Dm[:, r, n:n + 1], in1=pu[:, us, j, :],
                                                                                 op0=ALU.mult, op1=ALU.add),
                                         reads=[('pu', us), ('S', r), ('Dm', r)], writes=[('S', r)])
                                if i < NSTEP - 1:
                                    T.op('act', lambda e: e.copy(out=Sbf[:, r, (i + 1) % 2, :], in_=Sst[:, r, :]),
                                         reads=[('S', r)], writes=[('Sbf', r, (i + 1) % 2)])
                            if last_of_tile:
                                if tt not in owritten:
                                    owritten.add(tt)
                                    T.op('dve', lambda e: e.tensor_copy(out=Ob[:, tt * 128:(tt + 1) * 128], in_=po[:, r, osl, :]),
                                         reads=[('po', r, osl)], writes=[('O', tt)])
                                else:
                                    T.op('dve', lambda e: e.tensor_tensor(out=Ob[:, tt * 128:(tt + 1) * 128],
                                                                          in0=Ob[:, tt * 128:(tt + 1) * 128], in1=po[:, r, osl, :],
                                                                          op=ALU.add),
                                         reads=[('po', r, osl), ('O', tt)], writes=[('O', tt)])

                load_w(0)
                for h in range(KC):
                    proj_fm(0, AF.Silu, qs, 'qs')
                    proj_fm(4, AF.Silu, gs, 'gs')
                    proj_v()
                    proj_fm(1, AF.Sigmoid, Fb, 'F')
                    prep(h, 0)
                    proj_fm(2, AF.Sigmoid, Fb, 'F')
                    if h + 1 < KC:
                        load_w(h + 1)
                    prep(h, 1)
                    if defer:
                        qeb_part(h, 0)
                        qeb_part(h, 1)
                    recur(h)
                    if defer:
                        oall = [('O', tt) for tt in range(NT)]
                        T.dma('sp', Osave_d[h], Ob[:], reads=oall, writes=[('Osave_d', h)], stream='o')
                        T.dma('sp', gs_d[h], gs[:], reads=['gs'], writes=[('gs_d', h)], stream='o')
                        for r in range(2):
                            T.dma('sp', Send_d.ap()[(h * 2 + r) * 128:(h * 2 + r + 1) * 128, :], Sst[:, r, :],
                                  reads=[('S', r)], writes=['Send_d'], stream='o')
                    else:
                        finalize_head(s, h, Ob, gs, ke, E1, kdT, pp, st)
                if defer:
                    T.dma('sp', Dtot_d.ap(), Dtot_sb[:], reads=['Dtot_sb'], writes=['Dtot_d'], stream='o')
                T.barrier()

    x1_d = nc.dram_tensor("x1_d", [2, TS, D], F32).ap()
    if debug:
        dbg['x1'] = nc.dram_tensor("dbg_x1", [2, TS, D], F32, kind="ExternalOutput").ap()
    G = 1024
    GT = G // 128

    def post_mixer(s, L, after_group=None):
        w_out_ap = hgrn_w_out[0] if L == 0 else diff_w_out[0]
        resid = x_in[s] if L == 0 else x1_d[s]
        for g in range(TS // G):
            t0 = g * G
            with ExitStack() as es0:
                yacc = es0.enter_context(nc.sbuf_tensor(uname("yacc"), [128, GT, D], F32))
                h2T = es0.enter_context(nc.sbuf_tensor(uname("h2T"), [128, KC, G], BF16))
                with ExitStack() as es:
                    onTg = es.enter_context(nc.sbuf_tensor(uname("onTg"), [128, KC, G], BF16))
                    wo = es.enter_context(nc.sbuf_tensor(uname("wo"), [128, 2, KC, 512], BF16))
                    hn2 = es.enter_context(nc.sbuf_tensor(uname("hn2"), [128, 2, D], BF16))
                    gam2 = es.enter_context(nc.sbuf_tensor(uname("gam2"), [128, D], F32))
                    ss2 = es.enter_context(nc.sbuf_tensor(uname("ss2"), [128, GT], F32))
                    ppa = es.enter_context(nc.psum_tensor(uname("ppa"), [128, 4, 512], F32))
                    pst2 = es.enter_context(nc.psum_tensor(uname("pst2"), [128, 2, KC, 128], BF16))
                    for tt in range(GT):
                        T.dma('sp', yacc[:, tt, :], resid[t0 + tt * 128:t0 + (tt + 1) * 128, :], writes=[('yacc', tt)], stream='x')
                    for kc in range(KC):
                        T.dma('sp', onTg[:, kc, :], onT_d[s, kc, :, t0:t0 + G], reads=[('onT_d', s, kc)], writes=['onTg'], stream='x')
                    T.dma('sp', gam2[:], norm_ffn[L, :].partition_broadcast(128), writes=['gam2'], stream='c')
                    k = 0
                    for cg in range(4):
                        wb = cg % 2
                        T.dma('pool', wo[:, wb, :, :], w_out_ap[:, cg * 512:(cg + 1) * 512].rearrange("(kc p) c -> p kc c", p=128),
                              writes=[('wo', wb)], stream='w')
                        for tt in range(GT):
                            slot = k % 4
                            k += 1

                            def mm(e):
                                for kc in range(KC):
                                    ins = e.matmul(ppa[:, slot, :], lhsT=onTg[:, kc, tt * 128:(tt + 1) * 128], rhs=wo[:, wb, kc, :],
                                                   start=(kc == 0), stop=(kc == KC - 1))
                                return ins
                            T.op('pe', mm, reads=['onTg', ('wo', wb)], writes=[('ppa', slot)])
                            T.op('dve', lambda e: e.tensor_tensor(out=yacc[:, tt, cg * 512:(cg + 1) * 512],
                                                                  in0=yacc[:, tt, cg * 512:(cg + 1) * 512], in1=ppa[:, slot, :], op=ALU.add),
                                 reads=[('ppa', slot), ('yacc', tt)], writes=[('yacc', tt)])
                    for tt in range(GT):
                        b = tt % 2
                        T.op('act', lambda e: e.activation(out=hn2[:, b, :], in_=yacc[:, tt, :], func=AF.Square,
                                                           accum_out=ss2[:, tt:tt + 1]),
                             reads=[('yacc', tt)], writes=[('hn2', b), ('ss2', tt)])
                        T.op('act', lambda e: e.activation(out=ss2[:, tt:tt + 1], in_=ss2[:, tt:tt + 1], func=AF.Ln,
                                                           scale=1.0 / D, bias=epsb[:]),
                             reads=[('ss2', tt), 'epsb'], writes=[('ss2', tt)])
                        T.op('act', lambda e: e.activation(out=ss2[:, tt:tt + 1], in_=ss2[:, tt:tt + 1], func=AF.Exp, scale=-0.5),
                             reads=[('ss2', tt)], writes=[('ss2', tt)])
                        T.op('dve', lambda e: e.scalar_tensor_tensor(out=hn2[:, b, :], in0=yacc[:, tt, :], scalar=ss2[:, tt:tt + 1],
                                                                     in1=gam2[:], op0=ALU.mult, op1=ALU.mult),
                             reads=[('yacc', tt), ('ss2', tt), 'gam2'], writes=[('hn2', b)])

                        def tr(e):
                            for kc in range(KC):
                                ins = e.transpose(out=pst2[:, b, kc, :], in_=hn2[:, b, kc * 128:(kc + 1) * 128], identity=ident[:])
                            return ins
                        T.op('pe', tr, reads=[('hn2', b), 'ident'], writes=[('pst2', b)])
                        T.op('act', lambda e: e.copy(out=h2T[:, :, tt * 128:(tt + 1) * 128], in_=pst2[:, b, :, :]),
                             reads=[('pst2', b)], writes=['h2T'])
                    T.barrier()
                with ExitStack() as es:
                    wg = es.enter_context(nc.sbuf_tensor(uname("wg"), [128, KC, 512], BF16))
                    wu = es.enter_context(nc.sbuf_tensor(uname("wu"), [128, KC, 512], BF16))
                    wd = es.enter_context(nc.sbuf_tensor(uname("wd"), [128, 4, D], BF16))
                    aT = es.enter_context(nc.sbuf_tensor(uname("aT"), [128, 2, 4, G], BF16))
                    sg = es.enter_context(nc.sbuf_tensor(uname("sg"), [128, 2, 512], F32))
                    pg = es.enter_context(nc.psum_tensor(uname("pg"), [128, 4, 512], F32))
                    pd = es.enter_context(nc.psum_tensor(uname("pd"), [128, 4, 512], F32))
                    kg = 0
                    kd_ = 0
                    ks = 0
                    for jb in range(NJ // 4):
                        ab = jb % 2
                        T.dma('pool', wg[:], ffn_w_gate[L, :, jb * 512:(jb + 1) * 512].rearrange("(kc p) c -> p kc c", p=128),
                              writes=['wg'], stream='w')
                        T.dma('pool', wu[:], ffn_w_up[L, :, jb * 512:(jb + 1) * 512].rearrange("(kc p) c -> p kc c", p=128),
                              writes=['wu'], stream='w')
                        T.dma('pool', wd[:], ffn_w_down[L, jb * 512:(jb + 1) * 512, :].rearrange("(j p) c -> p j c", p=128),
                              writes=['wd'], stream='w')
                        for c4 in range(4):
                            for tg in range(G // 512):
                                s1 = kg % 4
                                s2 = (kg + 1) % 4
                                kg += 2
                                sb = ks % 2
                                ks += 1

                                def mmg(e):
                                    for kc in range(KC):
                                        ins = e.matmul(pg[:, s1, :], lhsT=wg[:, kc, c4 * 128:(c4 + 1) * 128], rhs=h2T[:, kc, tg * 512:(tg + 1) * 512],
                                                       start=(kc == 0), stop=(kc == KC - 1))
                                    return ins

                                def mmu(e):
                                    for kc in range(KC):
                                        ins = e.matmul(pg[:, s2, :], lhsT=wu[:, kc, c4 * 128:(c4 + 1) * 128], rhs=h2T[:, kc, tg * 512:(tg + 1) * 512],
                                                       start=(kc == 0), stop=(kc == KC - 1))
                                    return ins
                                T.op('pe', mmg, reads=['wg'], writes=[('pg', s1)])
                                T.op('pe', mmu, reads=['wu'], writes=[('pg', s2)])
                                T.op('act', lambda e: e.activation(out=sg[:, sb, :], in_=pg[:, s1, :], func=AF.Silu),
                                     reads=[('pg', s1)], writes=[('sg', sb)])
                                T.op('dve', lambda e: e.tensor_tensor(out=aT[:, ab, c4, tg * 512:(tg + 1) * 512], in0=sg[:, sb, :],
                                                                      in1=pg[:, s2, :], op=ALU.mult),
                                     reads=[('sg', sb), ('pg', s2)], writes=[('aT', ab)])
                        for tt in range(GT):
                            for cg in range(4):
                                sd = kd_ % 4
                                kd_ += 1

                                def mmd(e):
                                    for c4 in range(4):
                                        ins = e.matmul(pd[:, sd, :], lhsT=aT[:, ab, c4, tt * 128:(tt + 1) * 128], rhs=wd[:, c4, cg * 512:(cg + 1) * 512],
                                                       start=(c4 == 0), stop=(c4 == 3))
                                    return ins
                                T.op('pe', mmd, reads=[('aT', ab), 'wd'], writes=[('pd', sd)])
                                T.op('dve', lambda e: e.tensor_tensor(out=yacc[:, tt, cg * 512:(cg + 1) * 512],
                                                                      in0=yacc[:, tt, cg * 512:(cg + 1) * 512], in1=pd[:, sd, :], op=ALU.add),
                                     reads=[('pd', sd), ('yacc', tt)], writes=[('yacc', tt)])
                    T.barrier()
                if after_group is not None:
                    after_group(s, g, t0, yacc, h2T)
                T.barrier()

    QT_d = nc.dram_tensor("QT_d", [2, 16, 128, TS], BF16).ap()
    kt_d = [[nc.dram_tensor(f"kt_d{s}_{i}", [512, TS], BF16) for i in range(4)] for s in range(2)]
    v_d = [[nc.dram_tensor(f"v_d{s}_{j}", [TS, 512], BF16) for j in range(4)] for s in range(2)]
    KTg = [nc.dram_tensor(f"KTg{i}", [NCORES * 512, TS], BF16) for i in range(4)]
    Vg = [nc.dram_tensor(f"Vg{j}", [NCORES * TS, 512], BF16) for j in range(4)]

    neglam = nc.alloc_sbuf_tensor("neglam", [128, 1], F32)
    swl = nc.alloc_sbuf_tensor("swl", [128, 256], F32)
    ebias = nc.alloc_sbuf_tensor("ebias", [128, 1], F32)
    lamt = nc.alloc_sbuf_tensor("lamt", [128, 4, 128], F32)
    lam2 = nc.alloc_sbuf_tensor("lam2", [128, 2, 128], F32)
    lams = nc.alloc_sbuf_tensor("lams", [128, 2], F32)
    for i, ap_ in enumerate((lq1, lk1, lq2, lk2)):
        T.dma('sp', lamt[:, i, :], ap_[0, :].partition_broadcast(128), writes=['lamt'], stream='c')
    T.dma('sp', swl[:], diff_subln[0, :].partition_broadcast(128), writes=['swl'], stream='c')
    T.op('dve', lambda e: e.tensor_tensor(out=lam2[:], in0=lamt[:, 0:4:2, :], in1=lamt[:, 1:4:2, :], op=ALU.mult), reads=['lamt'], writes=['lam2'])
    T.op('dve', lambda e: e.tensor_reduce(out=lams[:], in_=lam2[:], axis=mybir.AxisListType.X, op=ALU.add), reads=['lam2'], writes=['lams'])
    T.op('act', lambda e: e.activation(out=lams[:], in_=lams[:], func=AF.Exp), reads=['lams'], writes=['lams'])
    T.op('dve', lambda e: e.tensor_tensor(out=neglam[:], in0=lams[:, 1:2], in1=lams[:, 0:1], op=ALU.subtract), reads=['lams'], writes=['neglam'])
    T.op('dve', lambda e: e.tensor_scalar(out=neglam[:], in0=neglam[:], scalar1=-LAMBDA_INIT, scalar2=None, op0=ALU.add), reads=['neglam'], writes=['neglam'])
    T.op('dve', lambda e: e.tensor_scalar(out=swl[:], in0=swl[:], scalar1=1.0 - LAMBDA_INIT, scalar2=None, op0=ALU.mult), reads=['swl'], writes=['swl'])
    T.op('dve', lambda e: e.memset(ebias[:], 0.0), writes=['ebias'])

    def l0_after(s, g, t0, yacc, h2T):
        for tt in range(GT):
            T.dma('sp', x1_d[s, t0 + tt * 128:t0 + (tt + 1) * 128, :], yacc[:, tt, :], reads=[('yacc', tt)],
                  writes=[('x1_d', s, g)], stream='o')
            if debug:
                T.dma('sp', dbg['x1'][s, t0 + tt * 128:t0 + (tt + 1) * 128, :], yacc[:, tt, :], reads=[('yacc', tt)],
                      writes=[('dbg_x1', s, g)], stream='o')
        if stage < 2:
            return
        h1T = h2T
        with ExitStack() as es:
            hn1 = es.enter_context(nc.sbuf_tensor(uname("hn1"), [128, 2, D], BF16))
            gam1 = es.enter_context(nc.sbuf_tensor(uname("gam1"), [128, D], F32))
            ss1 = es.enter_context(nc.sbuf_tensor(uname("ss1"), [128, GT], F32))
            Wb = es.enter_context(nc.sbuf_tensor(uname("Wb"), [128, 2, KC, 512], BF16))
            rope = es.enter_context(nc.sbuf_tensor(uname("rope"), [128, GT, 2, 64], F32))
            xq = es.enter_context(nc.sbuf_tensor(uname("xq"), [128, 2, 512], F32))
            XC = es.enter_context(nc.sbuf_tensor(uname("XC"), [128, 2, 512], F32))
            XS = es.enter_context(nc.sbuf_tensor(uname("XS"), [128, 2, 512], F32))
            rq = es.enter_context(nc.sbuf_tensor(uname("rq"), [128, 2, 512], BF16))
            stq = es.enter_context(nc.sbuf_tensor(uname("stq"), [128, 2, 4, G], BF16))
            vst = es.enter_context(nc.sbuf_tensor(uname("vst"), [128, 2, GT, 512], BF16))
            pst1 = es.enter_context(nc.psum_tensor(uname("pst1"), [128, 2, KC, 128], BF16))
            ppq = es.enter_context(nc.psum_tensor(uname("ppq"), [128, 2, 512], F32))
            ptq = es.enter_context(nc.psum_tensor(uname("ptq"), [128, 2, 4, 128], BF16))
            T.dma('sp', gam1[:], norm_mixer[1, :].partition_broadcast(128), writes=['gam1'], stream='c')
            T.dma('sp', rope[:], c_rope[s, t0:t0 + G, :, :].rearrange("(tt p) a f -> p tt a f", p=128), writes=['rope'], stream='c')
            for tt in range(GT):
                b = tt % 2
                T.op('act', lambda e: e.activation(out=hn1[:, b, :], in_=yacc[:, tt, :], func=AF.Square,
                                                   accum_out=ss1[:, tt:tt + 1]),
                     reads=[('yacc', tt)], writes=[('hn1', b), ('ss1', tt)])
                T.op('act', lambda e: e.activation(out=ss1[:, tt:tt + 1], in_=ss1[:, tt:tt + 1], func=AF.Ln,
                                                   scale=1.0 / D, bias=epsb[:]),
                     reads=[('ss1', tt), 'epsb'], writes=[('ss1', tt)])
                T.op('act', lambda e: e.activation(out=ss1[:, tt:tt + 1], in_=ss1[:, tt:tt + 1], func=AF.Exp, scale=-0.5),
                     reads=[('ss1', tt)], writes=[('ss1', tt)])
                T.op('dve', lambda e: e.scalar_tensor_tensor(out=hn1[:, b, :], in0=yacc[:, tt, :], scalar=ss1[:, tt:tt + 1],
                                                             in1=gam1[:], op0=ALU.mult, op1=ALU.mult),
                     reads=[('yacc', tt), ('ss1', tt), 'gam1'], writes=[('hn1', b)])

                def tr(e):
                    for kc in range(KC):
                        ins = e.transpose(out=pst1[:, b, kc, :], in_=hn1[:, b, kc * 128:(kc + 1) * 128], identity=ident[:])
                    return ins
                T.op('pe', tr, reads=[('hn1', b), 'ident'], writes=[('pst1', b)])
                T.op('act', lambda e: e.copy(out=h1T[:, :, tt * 128:(tt + 1) * 128], in_=pst1[:, b, :, :]),
                     reads=[('pst1', b)], writes=['h1T'])
            k = 0
            for blk in range(12):
                wb = blk % 2
                T.dma('pool', Wb[:, wb, :, :], diff_w_in[0, :, blk * 512:(blk + 1) * 512].rearrange("(kc p) c -> p kc c", p=128),
                      writes=[('Wb', wb)], stream='w')
                for tt in range(GT):
                    sl = k % 2
                    k += 1

                    def mm(e):
                        for kc in range(KC):
                            ins = e.matmul(ppq[:, sl, :], lhsT=h1T[:, kc, tt * 128:(tt + 1) * 128], rhs=Wb[:, wb, kc, :],
                                           start=(kc == 0), stop=(kc == KC - 1))
                        return ins
                    T.op('pe', mm, reads=['h1T', ('Wb', wb)], writes=[('ppq', sl)])
                    if blk < 8:
                        T.op('act', lambda e: e.copy(out=xq[:, sl, :], in_=ppq[:, sl, :]), reads=[('ppq', sl)], writes=[('xq', sl)])
                        x4 = xq[:, sl, :].rearrange("p (m a f) -> p m a f", m=4, a=2)
                        c4_ = XC[:, sl, :].rearrange("p (m a f) -> p m a f", m=4, a=2)
                        s4_ = XS[:, sl, :].rearrange("p (m a f) -> p m a f", m=4, a=2)
                        r4_ = rq[:, sl, :].rearrange("p (m a f) -> p m a f", m=4, a=2)
                        cosb = rope[:, tt, 0, :].unsqueeze(1).unsqueeze(1).to_broadcast([128, 4, 2, 64])
                        sinb = rope[:, tt, 1, :].unsqueeze(1).unsqueeze(1).to_broadcast([128, 4, 2, 64])
                        T.op('pool', lambda e: e.tensor_tensor(out=c4_, in0=x4, in1=cosb, op=ALU.mult),
                             reads=[('xq', sl), 'rope'], writes=[('XC', sl)])
                        T.op('pool', lambda e: e.tensor_tensor(out=s4_, in0=x4, in1=sinb, op=ALU.mult),
                             reads=[('xq', sl), 'rope'], writes=[('XS', sl)])
                        T.op('dve', lambda e: e.tensor_tensor(out=r4_[:, :, 0, :], in0=c4_[:, :, 0, :], in1=s4_[:, :, 1, :], op=ALU.subtract),
                             reads=[('XC', sl), ('XS', sl)], writes=[('rq', sl)])
                        T.op('dve', lambda e: e.tensor_tensor(out=r4_[:, :, 1, :], in0=c4_[:, :, 1, :], in1=s4_[:, :, 0, :], op=ALU.add),
                             reads=[('XC', sl), ('XS', sl)], writes=[('rq', sl)])

                        def trq(e):
                            for m in range(4):
                                ins = e.transpose(out=ptq[:, sl, m, :], in_=rq[:, sl, m * 128:(m + 1) * 128], identity=ident[:])
                            return ins
                        T.op('pe', trq, reads=[('rq', sl), 'ident'], writes=[('ptq', sl)])
                        T.op('act', lambda e: e.activation(out=stq[:, wb, :, tt * 128:(tt + 1) * 128], in_=ptq[:, sl, :, :], func=AF.Copy,
                                                           scale=(QSCALE if blk < 4 else 1.0)),
                             reads=[('ptq', sl)], writes=[('stq', wb)])
                    else:
                        T.op('act', lambda e: e.copy(out=vst[:, wb, tt, :], in_=ppq[:, sl, :]), reads=[('ppq', sl)], writes=[('vst', wb)])
                if blk < 4:
                    for m in range(4):
                        T.dma('sp', QT_d[s, blk * 4 + m, :, t0:t0 + G], stq[:, wb, m, :], reads=[('stq', wb)],
                              writes=[('QT_d', s)], stream='o')
                elif blk < 8:
                    for m in range(4):
                        mp = (blk - 4) * 4 + m
                        T.dma('sp', kt_d[s][blk - 4].ap()[m * 128:(m + 1) * 128, t0:t0 + G], stq[:, wb, m, :], reads=[('stq', wb)],
                              writes=[('kt_d', s, blk - 4)], stream='o')
                else:
                    T.dma('sp', v_d[s][blk - 8].ap()[t0:t0 + G, :].rearrange("(tt p) c -> p tt c", p=128),
                          vst[:, wb, :, :], reads=[('vst', wb)], writes=[('v_d', s, blk - 8)], stream='o')
            T.barrier()

    def attention(s):
        nk = TS if s == 1 else NCORES * TS
        nkt = nk // 128
        with ExitStack() as es:
            Qh = es.enter_context(nc.sbuf_tensor(uname("Qh"), [128, 2, TS], BF16))
            Kh = es.enter_context(nc.sbuf_tensor(uname("Kh"), [128, 2, nk], BF16))
            Vh = es.enter_context(nc.sbuf_tensor(uname("Vh"), [128, nkt, 257], BF16))
            pT = es.enter_context(nc.sbuf_tensor(uname("pT"), [128, 3, 512], BF16))
            A1 = es.enter_context(nc.sbuf_tensor(uname("A1"), [128, 16, 256], F32))
            tmpA = es.enter_context(nc.sbuf_tensor(uname("tmpA"), [128, 2, 256], F32))
            rz = es.enter_context(nc.sbuf_tensor(uname("rz"), [128, 8], F32))
            ssn = es.enter_context(nc.sbuf_tensor(uname("ssn"), [128, 8], F32))
            onq = es.enter_context(nc.sbuf_tensor(uname("onq"), [128, 2, 256], BF16))
            ost = es.enter_context(nc.sbuf_tensor(uname("ost"), [128, 2, TS], BF16))
            psc = es.enter_context(nc.psum_tensor(uname("psc"), [128, 3, 512], F32))
            pav = es.enter_context(nc.psum_tensor(uname("pav"), [128, 4, 512], F32))
            pto = es.enter_context(nc.psum_tensor(uname("pto"), [128, 2, 2, 128], BF16))
            T.op('dve', lambda e: e.memset(Vh[:, :, 256:257], 1.0), writes=[('Vh', 0), ('Vh', 1)])
            ksc = 0
            kz = 0
            nvh = 2 if s == 1 else NCORES
            for hd in range(8):
                for m in range(2):
                    T.dma('sp', Qh[:, m, :], QT_d[s, 2 * hd + m, :, :], reads=[('QT_d', s)], writes=[('Qh', m)], stream='a')
                for m in range(2):
                    mp = 2 * hd + m
                    if s == 1:
                        T.dma('sp', Kh[:, m, :], kt_d[1][mp // 4].ap()[(mp % 4) * 128:(mp % 4 + 1) * 128, :], reads=[('kt_d', 1, mp // 4)],
                              writes=[('Kh', m, 0), ('Kh', m, 1)], stream='a')
                    else:
                        for rk in range(NCORES):
                            T.dma('sp', Kh[:, m, rk * TS:(rk + 1) * TS],
                                  KTg[mp // 4].ap()[rk * 512 + (mp % 4) * 128:rk * 512 + (mp % 4 + 1) * 128, :], reads=[('KTg', mp // 4)],
                                  writes=[('Kh', m, rk // 4)], stream='a')
                if s == 1:
                    T.dma('sp', Vh[:, :, 0:256], v_d[1][hd // 2].ap()[:, (hd % 2) * 256:(hd % 2 + 1) * 256].rearrange("(kt p) e -> p kt e", p=128),
                          reads=[('v_d', 1, hd // 2)], writes=[('Vh', 0), ('Vh', 1)], stream='a')
                else:
                    for rk in range(NCORES):
                        T.dma('sp', Vh[:, rk * NT:(rk + 1) * NT, 0:256],
                              Vg[hd // 2].ap()[rk * TS:(rk + 1) * TS, (hd % 2) * 256:(hd % 2 + 1) * 256].rearrange("(kt p) e -> p kt e", p=128),
                              reads=[('Vg', hd // 2)], writes=[('Vh', rk // 4)], stream='a')
                for m in range(2):
                    for qg in range(4):
                        for kt in range(nkt):
                            half = (kt * 2) // nkt
                            sl = ksc % 3
                            ksc += 1
                            T.op('pe', lambda e: e.matmul(psc[:, sl, :], lhsT=Kh[:, m, kt * 128:(kt + 1) * 128],
                                                          rhs=Qh[:, m, qg * 512:(qg + 1) * 512], start=True, stop=True),
                                 reads=[('Kh', m, half), ('Qh', m)], writes=[('psc', sl)])
                            T.op('act', lambda e: e.activation(out=pT[:, sl, :], in_=psc[:, sl, :], func=AF.Exp, bias=ebias[:]),
                                 reads=[('psc', sl), 'ebias'], writes=[('pT', sl)])

                            def av(e):
                                for qt in range(4):
                                    ins = e.matmul(pav[:, qt, 0:257], lhsT=pT[:, sl, qt * 128:(qt + 1) * 128], rhs=Vh[:, kt, :],
                                                   start=(kt == 0), stop=(kt == nkt - 1))
                                return ins
                            T.op('pe', av, reads=[('pT', sl), ('Vh', half)], writes=['pav'])
                        for qt in range(4):
                            zi = kz % 8
                            kz += 1
                            tq = qg * 4 + qt
                            T.op('dve', lambda e: e.reciprocal(out=rz[:, zi:zi + 1], in_=pav[:, qt, 256:257]), reads=['pav'], writes=[('rz', zi)])
                            if m == 0:
                                T.op('dve', lambda e: e.tensor_scalar(out=A1[:, tq, :], in0=pav[:, qt, 0:256], scalar1=rz[:, zi:zi + 1],
                                                                      scalar2=None, op0=ALU.mult),
                                     reads=['pav', ('rz', zi)], writes=[('A1', tq)])
                            else:
                                tb = qt % 2
                                T.op('dve', lambda e: e.tensor_scalar(out=tmpA[:, tb, :], in0=pav[:, qt, 0:256], scalar1=rz[:, zi:zi + 1],
                                                                      scalar2=neglam[:, 0:1], op0=ALU.mult, op1=ALU.mult),
                                     reads=['pav', ('rz', zi), 'neglam'], writes=[('tmpA', tb)])
                                T.op('dve', lambda e: e.tensor_tensor(out=tmpA[:, tb, :], in0=tmpA[:, tb, :], in1=A1[:, tq, :], op=ALU.add),
                                     reads=[('tmpA', tb), ('A1', tq)], writes=[('tmpA', tb)])
                                T.op('act', lambda e: e.activation(out=onq[:, tb, :], in_=tmpA[:, tb, :], func=AF.Square,
                                                                   accum_out=ssn[:, zi:zi + 1]),
                                     reads=[('tmpA', tb)], writes=[('onq', tb), ('ssn', zi)])
                                T.op('act', lambda e: e.activation(out=ssn[:, zi:zi + 1], in_=ssn[:, zi:zi + 1], func=AF.Ln,
                                                                   scale=1.0 / 256, bias=epsb[:]),
                                     reads=[('ssn', zi), 'epsb'], writes=[('ssn', zi)])
                                T.op('act', lambda e: e.activation(out=ssn[:, zi:zi + 1], in_=ssn[:, zi:zi + 1], func=AF.Exp, scale=-0.5),
                                     reads=[('ssn', zi)], writes=[('ssn', zi)])
                                T.op('dve', lambda e: e.scalar_tensor_tensor(out=onq[:, tb, :], in0=tmpA[:, tb, :], scalar=ssn[:, zi:zi + 1],
                                                                             in1=swl[:], op0=ALU.mult, op1=ALU.mult),
                                     reads=[('tmpA', tb), ('ssn', zi), 'swl'], writes=[('onq', tb)])

                                def tro(e):
                                    for c2 in range(2):
                                        ins = e.transpose(out=pto[:, tb, c2, :], in_=onq[:, tb, c2 * 128:(c2 + 1) * 128], identity=ident[:])
                                    return ins
                                T.op('pe', tro, reads=[('onq', tb), 'ident'], writes=[('pto', tb)])
                                T.op('act', lambda e: e.copy(out=ost[:, :, tq * 128:(tq + 1) * 128], in_=pto[:, tb, :, :]),
                                     reads=[('pto', tb)], writes=['ost'])
                for c2 in range(2):
                    T.dma('sp', onT_d[s, 2 * hd + c2], ost[:, c2, :], reads=['ost'], writes=[('onT_d', s, 2 * hd + c2)], stream='o')
            T.barrier()

    def final_after(s, g, t0, yacc, h2T):
        with ExitStack() as es:
            gamf = es.enter_context(nc.sbuf_tensor(uname("gamf"), [128, D], F32))
            ssf = es.enter_context(nc.sbuf_tensor(uname("ssf"), [128, GT], F32))
            junk = es.enter_context(nc.sbuf_tensor(uname("junk"), [128, 2, D], BF16))
            T.dma('sp', gamf[:], norm_final.partition_broadcast(128), writes=['gamf'], stream='c')
            for tt in range(GT):
                b = tt % 2
                T.op('act', lambda e: e.activation(out=junk[:, b, :], in_=yacc[:, tt, :], func=AF.Square,
                                                   accum_out=ssf[:, tt:tt + 1]),
                     reads=[('yacc', tt)], writes=[('junk', b), ('ssf', tt)])
                T.op('act', lambda e: e.activation(out=ssf[:, tt:tt + 1], in_=ssf[:, tt:tt + 1], func=AF.Ln,
                                                   scale=1.0 / D, bias=epsb[:]),
                     reads=[('ssf', tt), 'epsb'], writes=[('ssf', tt)])
                T.op('act', lambda e: e.activation(out=ssf[:, tt:tt + 1], in_=ssf[:, tt:tt + 1], func=AF.Exp, scale=-0.5),
                     reads=[('ssf', tt)], writes=[('ssf', tt)])
                T.op('dve', lambda e: e.scalar_tensor_tensor(out=yacc[:, tt, :], in0=yacc[:, tt, :], scalar=ssf[:, tt:tt + 1],
                                                             in1=gamf[:], op0=ALU.mult, op1=ALU.mult),
                     reads=[('yacc', tt), ('ssf', tt), 'gamf'], writes=[('yacc', tt)])
                T.dma('sp', y_out[s, t0 + tt * 128:t0 + (tt + 1) * 128, :], yacc[:, tt, :], reads=[('yacc', tt)],
                      writes=[('y', s, g, tt)], stream='y')
            T.barrier()

    def gather(src_t, dst_t, reads, writes):
        if fake_gather:
            n = src_t.ap().shape[0]
            for rk in range(NCORES):
                T.dma('sp', dst_t.ap()[rk * n:(rk + 1) * n, :], src_t.ap(), reads=reads, writes=writes, stream='g')
            return
        T._waits('pool', T._deps(reads, writes))
        sem = T._new('cc')
        ins = nc.gpsimd.collective_compute("AllGather", ALU.bypass, replica_groups=[list(range(NCORES))],
                                           ins=[src_t.ap().opt()], outs=[dst_t.ap().opt()])
        ins.then_inc(sem)
        T.ninst += 1
        T._record((sem, 1, 'cc'), reads, writes)

    def l0_fix():
        with ExitStack() as es:
            Ob = es.enter_context(nc.sbuf_tensor(uname("fOb"), [128, TS], F32))
            gsb = es.enter_context(nc.sbuf_tensor(uname("fgs"), [128, TS], BF16))
            qeb = es.enter_context(nc.sbuf_tensor(uname("fqeb"), [128, 2, TS], BF16))
            ke = es.enter_context(nc.sbuf_tensor(uname("fke"), [128, TS], BF16))
            E1 = es.enter_context(nc.sbuf_tensor(uname("fE1"), [128, TS], F32))
            kdT = es.enter_context(nc.sbuf_tensor(uname("fkdT"), [128, TS], BF16))
            Sg = es.enter_context(nc.sbuf_tensor(uname("fSg"), [128, 2, NCORES, 128], F32))
            DGt = es.enter_context(nc.sbuf_tensor(uname("fDGt"), [128, NCORES, 2 * KC], F32))
            acoef = es.enter_context(nc.sbuf_tensor(uname("facoef"), [128, 2, NCORES], F32))
            acc = es.enter_context(nc.sbuf_tensor(uname("facc"), [128, 2, 128], F32))
            accb = es.enter_context(nc.sbuf_tensor(uname("faccb"), [128, 2, 128], BF16))
            cm = es.enter_context(nc.sbuf_tensor(uname("fcm"), [128, 2, NCORES], F32))
            pp = es.enter_context(nc.psum_tensor(uname("fpp"), [128, 2, 512], F32))
            pc = es.enter_context(nc.psum_tensor(uname("fpc"), [128, 2, 512], F32))
            st = {'pp': 0}
            kpc = 0
            oall = [('O', tt) for tt in range(NT)]
            T.dma('sp', cm[:], c_cmask, writes=['cm'], stream='c')
            SGv = SG.ap().rearrange("(k x) e -> x k e", k=NCORES)
            T.dma('sp', DGt[:], DG.ap().rearrange("(k d) c -> d k c", k=NCORES)[:, :, 0:2 * KC], reads=['DG'], writes=['DGt'], stream='c')
            for h in range(KC):
                T.dma('sp', Ob[:], Osave_d[h], reads=[('Osave_d', h)], writes=oall, stream='x')
                T.dma('sp', gsb[:], gs_d[h], reads=[('gs_d', h)], writes=['gs'], stream='x')
                for r in range(2):
                    T.dma('sp', qeb[:, r, :], qeb_d[h, r], reads=[('qeb_d', h)], writes=[('qeb', r)], stream='x')
                    T.dma('sp', Sg[:, r, :, :], SGv[(h * 2 + r) * 128:(h * 2 + r + 1) * 128, :, :],
                          reads=['SG'], writes=[('Sg', r)], stream='x')
                    col = r * KC + h
                    T.op('dve', lambda e: e.tensor_scalar(out=acoef[:, r, :], in0=DGt[:, :, col], scalar1=-1.0, scalar2=None, op0=ALU.add),
                         reads=['DGt'], writes=[('acoef', r)])
                    T.op('dve', lambda e: e.tensor_tensor(out=acoef[:, r, :], in0=acoef[:, r, :], in1=cm[:, r, :], op=ALU.mult),
                         reads=[('acoef', r), 'cm'], writes=[('acoef', r)])
                    T.op('dve', lambda e: e.tensor_scalar(out=acoef[:, r, :], in0=acoef[:, r, :], scalar1=1.0, scalar2=None, op0=ALU.add),
                         reads=[('acoef', r)], writes=[('acoef', r)])
                    T.op('dve', lambda e: e.tensor_tensor(out=Sg[:, r, :, :], in0=Sg[:, r, :, :],
                                                          in1=cm[:, r, :].unsqueeze(2).to_broadcast([128, NCORES, 128]), op=ALU.mult),
                         reads=[('Sg', r), 'cm'], writes=[('Sg', r)])
                    order = list(range(NCORES)) if r == 0 else list(range(NCORES - 1, -1, -1))
                    for idx, cp in enumerate(order):
                        if idx == 0:
                            T.op('dve', lambda e: e.tensor_copy(out=acc[:, r, :], in_=Sg[:, r, cp, :]), reads=[('Sg', r)], writes=[('acc', r)])
                        else:
                            T.op('dve', lambda e: e.scalar_tensor_tensor(out=acc[:, r, :], in0=acc[:, r, :], scalar=acoef[:, r, cp:cp + 1],
                                                                         in1=Sg[:, r, cp, :], op0=ALU.mult, op1=ALU.add),
                                 reads=[('Sg', r), ('acc', r), ('acoef', r)], writes=[('acc', r)])
                    T.op('act', lambda e: e.copy(out=accb[:, r, :], in_=acc[:, r, :]), reads=[('acc', r)], writes=[('accb', r)])
                    for tg in range(4):
                        sl = kpc % 2
                        kpc += 1
                        T.op('pe', lambda e: e.matmul(pc[:, sl, :], lhsT=accb[:, r, :], rhs=qeb[:, r, tg * 512:(tg + 1) * 512],
                                                      start=True, stop=True),
                             reads=[('accb', r), ('qeb', r)], writes=[('pc', sl)])
                        T.op('dve', lambda e: e.tensor_tensor(out=Ob[:, tg * 512:(tg + 1) * 512], in0=Ob[:, tg * 512:(tg + 1) * 512],
                                                              in1=pc[:, sl, :], op=ALU.add),
                             reads=[('pc', sl)] + oall, writes=oall)
                finalize_head(0, h, Ob, gsb, ke, E1, kdT, pp, st)
            T.barrier()

    if slots is None:
        slots = [1] if debug else [0, 1]
    exch = (0 in slots) and not no_exchange
    if 0 in slots:
        l0_mixer(0)
        if exch:
            T.barrier()
            gather(Send_d, SG, ['Send_d'], ['SG'])
            gather(Dtot_d, DG, ['Dtot_d'], ['DG'])
            T.barrier(['pool'])
    if 1 in slots:
        l0_mixer(1)
    if exch:
        l0_fix()
    for s in slots:
        if stage >= 1:
            post_mixer(s, 0, l0_after)
        if s == 0 and stage >= 3:
            T.barrier()
            for i in range(4):
                gather(kt_d[0][i], KTg[i], [('kt_d', 0, i)], [('KTg', i)])
                gather(v_d[0][i], Vg[i], [('v_d', 0, i)], [('Vg', i)])
            T.barrier(['pool'])
    if stage >= 3:
        for s in reversed(slots):
            attention(s)
            post_mixer(s, 1, final_after)
    T.final_wait('sp')
    return nc, T


def host_consts():
    ident = np.eye(128, dtype=np.float32)
    p = np.arange(128) % 32
    t = np.arange(32)
    m = np.zeros((128, 2, 32), np.float32)
    m[:, 0, :] = (p[:, None] <= t[None, :])
    m[:, 1, :] = (p[:, None] >= t[None, :])
    return ident, m


def make_in_maps(inputs):
    ident, m = host_consts()
    xp = np.asarray(inputs['x_prompt'], np.float32)
    xs = np.asarray(inputs['x_sample'], np.float32)
    inv_freq = (10000.0 ** (-np.arange(0, 128, 2, dtype=np.float32) / 128)).astype(np.float32)
    maps = []
    shared = {k: np.ascontiguousarray(np.asarray(v, np.float32)) for k, v in inputs.items()
              if k not in ('x_prompt', 'x_sample')}
    for c in range(NCORES):
        x = np.stack([xp[0, c * TS:(c + 1) * TS, :], xs[c]], axis=0)
        rope = np.zeros((2, TS, 2, 64), np.float32)
        for slot, pos0 in ((0, c * TS), (1, 0)):
            ang = (np.arange(pos0, pos0 + TS, dtype=np.float32)[:, None] * inv_freq[None, :]).astype(np.float32)
            rope[slot, :, 0, :] = np.cos(ang)
            rope[slot, :, 1, :] = np.sin(ang)
        cm = np.zeros((128, 2, NCORES), np.float32)
        cm[:, 0, :c] = 1.0
        cm[:, 1, c + 1:] = 1.0
        d = dict(shared)
        d.update({'x': np.ascontiguousarray(x), 'c_ident': ident, 'c_masks': m, 'c_rope': rope, 'c_cmask': cm})
        maps.append(d)
    return maps


_NC_CACHE = {}


def kernel(**inputs):
    if 'nc' not in _NC_CACHE:
        _NC_CACHE['nc'] = build_nc()[0]
    nc = _NC_CACHE['nc']
    maps = make_in_maps(inputs)
    res = run_bass_kernel_spmd(nc, maps, core_ids=list(range(NCORES)))
    ys = [np.asarray(r['y'], dtype=np.float32) for r in res.results]
    y_prompt = np.concatenate([y[0] for y in ys], axis=0)[None]
    y_sample = np.stack([y[1] for y in ys], axis=0)
    return (y_prompt, y_sample)
```

```python
import math
from contextlib import ExitStack
import numpy as np
import concourse.bass as bass
import concourse.mybir as mybir
from concourse.bass_utils import run_bass_kernel_spmd

F32 = mybir.dt.float32
BF16 = mybir.dt.bfloat16
AF = mybir.ActivationFunctionType
ALU = mybir.AluOpType

NCORES = 8
D = 2048
KC = 16
TS = 2048
NT = TS // 128
NSTEP = TS // 32
DFF = 5632
NJ = DFF // 128
EPS = 1e-5
LAMBDA_INIT = 0.8 - 0.6 * math.exp(-0.3 * 1)
QSCALE = 128 ** -0.5


class Tracker:
    def __init__(self, nc):
        self.nc = nc
        self.eng = {'pe': nc.tensor, 'act': nc.scalar, 'dve': nc.vector, 'pool': nc.gpsimd, 'sp': nc.sync}
        self.sems = {e: None for e in self.eng}
        self.cnt = {e: 0 for e in self.eng}
        self.nsem = 0
        self.waited = {}
        self.lastw = {}
        self.readers = {}
        self.dsem = {}
        self.latest = {}
        self.ninst = 0

    def _new(self, tag):
        while True:
            self.nsem += 1
            h = self.nc.alloc_semaphore(f"{tag}_{self.nsem}")
            if not (160 <= h.num <= 199):
                return h

    def _deps(self, reads, writes):
        deps = []
        for r in reads:
            d = self.lastw.get(r)
            if d is not None:
                deps.append(d)
        for w in writes:
            d = self.lastw.get(w)
            if d is not None:
                deps.append(d)
            deps.extend(self.readers.get(w, {}).values())
        return deps

    def _waits(self, e, deps):
        best = {}
        for (sem, val, src) in deps:
            if src == 'pe' and e == 'pe':
                continue
            k = sem.name
            if val > best.get(k, (None, 0))[1]:
                best[k] = (sem, val)
        for k, (sem, val) in best.items():
            if self.waited.get((e, k), 0) >= val:
                continue
            self.waited[(e, k)] = val
            self.eng[e].wait_ge(sem, val)
            self.ninst += 1

    def _record(self, d, reads, writes):
        for r in reads:
            self.readers.setdefault(r, {})[d[0].name] = d
        for w in writes:
            self.lastw[w] = d
            self.readers[w] = {}
        self.latest[d[0].name] = (d[0], d[1])

    def op(self, e, fn, reads=(), writes=()):
        self._waits(e, self._deps(reads, writes))
        if self.sems[e] is None or self.cnt[e] >= 60000:
            self.sems[e] = self._new("s" + e)
            self.cnt[e] = 0
        ins = fn(self.eng[e])
        self.cnt[e] += 1
        ins.then_inc(self.sems[e], 1)
        self.ninst += 1
        self._record((self.sems[e], self.cnt[e], e), reads, writes)

    def dma(self, q, out, in_, reads=(), writes=(), stream="d", **kw):
        RING = 4
        st = self.dsem.setdefault(stream, {'i': 0, 'slots': [None] * RING})
        i = st['i'] % RING
        st['i'] += 1
        slot = st['slots'][i]
        deps = self._deps(reads, writes)
        if slot is not None:
            deps.append((slot[0], slot[1], 'dma'))
        self._waits(q, deps)
        if slot is None or slot[1] + 16 > 60000:
            slot = [self._new("d" + stream), 0]
            st['slots'][i] = slot
        ins = self.eng[q].dma_start(out=out, in_=in_, **kw)
        slot[1] += 16
        ins.then_inc(slot[0], 16)
        self.ninst += 1
        self._record((slot[0], slot[1], 'dma'), reads, writes)

    def barrier(self, engines=None):
        for e in (engines or self.eng):
            deps = [(sem, val, 'x') for (sem, val) in self.latest.values()]
            self._waits(e, deps)

    def final_wait(self, e='sp'):
        deps = [(sem, val, 'x') for (sem, val) in self.latest.values()]
        self._waits(e, deps)


def build_nc(stage=99, debug=False, fake_gather=False, slots=None, no_exchange=False):
    nc = bass.Bass("TRN2", target_bir_lowering=False)
    T = Tracker(nc)
    uid = [0]

    def uname(n):
        uid[0] += 1
        return f"{n}_{uid[0]}"

    def din(name, shape, dt=F32):
        return nc.dram_tensor(name, list(shape), dt, kind="ExternalInput").ap()

    x_in = din("x", [2, TS, D])
    norm_mixer = din("norm_mixer", [2, D])
    norm_ffn = din("norm_ffn", [2, D])
    norm_final = din("norm_final", [D])
    hgrn_w_in = din("hgrn_w_in", [1, D, 5 * D])
    hgrn_lb = din("hgrn_lower_bound", [2, 3, D])
    hgrn_gnorm = din("hgrn_gnorm", [1, 128])
    hgrn_w_out = din("hgrn_w_out", [1, D, D])
    diff_w_in = din("diff_w_in", [1, D, 3 * D])
    lq1 = din("diff_lambda_q1", [1, 128])
    lk1 = din("diff_lambda_k1", [1, 128])
    lq2 = din("diff_lambda_q2", [1, 128])
    lk2 = din("diff_lambda_k2", [1, 128])
    diff_subln = din("diff_subln", [1, 256])
    diff_w_out = din("diff_w_out", [1, D, D])
    ffn_w_gate = din("ffn_w_gate", [2, D, DFF])
    ffn_w_up = din("ffn_w_up", [2, D, DFF])
    ffn_w_down = din("ffn_w_down", [2, DFF, D])
    c_ident = din("c_ident", [128, 128])
    c_masks = din("c_masks", [128, 2, 32])
    c_rope = din("c_rope", [2, TS, 2, 64])
    c_cmask = din("c_cmask", [128, 2, NCORES])

    y_out = nc.dram_tensor("y", [2, TS, D], F32, kind="ExternalOutput").ap()

    onT_d = nc.dram_tensor("onT_d", [2, KC, 128, TS], BF16).ap()
    dbg = {}
    if debug:
        dbg['onT'] = nc.dram_tensor("dbg_onT", [2, KC, 128, TS], BF16, kind="ExternalOutput").ap()

    ident = nc.alloc_sbuf_tensor("ident", [128, 128], BF16)
    ones_bf = nc.alloc_sbuf_tensor("ones_bf", [128, 128], BF16)
    masks = nc.alloc_sbuf_tensor("masks", [128, 2, 32], F32)
    lbt = nc.alloc_sbuf_tensor("lbt", [128, 2, 16, 3], F32)
    lb0 = nc.alloc_sbuf_tensor("lb0", [128, 2, 16], F32)
    oml = nc.alloc_sbuf_tensor("oml", [128, 2, 16], F32)
    gw = nc.alloc_sbuf_tensor("gw", [128, 1], F32)
    epsb = nc.alloc_sbuf_tensor("epsb", [128, 1], F32)

    T.dma('pool', ident[:], c_ident, writes=['ident'], stream='c')
    T.dma('sp', masks[:], c_masks, writes=['masks'], stream='c')
    T.op('dve', lambda e: e.memset(ones_bf[:], 1.0), writes=['ones'])
    T.op('dve', lambda e: e.memset(epsb[:], EPS), writes=['epsb'])
    for r in range(2):
        for l in range(3):
            T.dma('sp', lbt[:, r, :, l], hgrn_lb[r, l, :].rearrange("(h d) -> d h", d=128), writes=['lbt'], stream='c',
                  allow_slow_non_contiguous=True)
    T.dma('sp', gw[:], hgrn_gnorm[0, :].rearrange("(d o) -> d o", o=1), writes=['gw'], stream='c',
          allow_slow_non_contiguous=True)
    T.op('act', lambda e: e.activation(out=lbt[:], in_=lbt[:], func=AF.Exp), reads=['lbt'], writes=['lbt'])
    T.op('dve', lambda e: e.tensor_reduce(out=oml[:], in_=lbt[:], axis=mybir.AxisListType.X, op=ALU.add), reads=['lbt'], writes=['oml'])
    T.op('dve', lambda e: e.reciprocal(out=oml[:], in_=oml[:]), reads=['oml'], writes=['oml'])
    T.op('dve', lambda e: e.tensor_tensor(out=lb0[:], in0=lbt[:, :, :, 0], in1=oml[:], op=ALU.mult), reads=['lbt', 'oml'], writes=['lb0'])
    T.op('dve', lambda e: e.tensor_scalar(out=oml[:], in0=lb0[:], scalar1=-1.0, scalar2=1.0, op0=ALU.mult, op1=ALU.add), reads=['lb0'], writes=['oml'])

    zeros = nc.alloc_sbuf_tensor("zeros", [128, 32], F32)
    T.op('dve', lambda e: e.memset(zeros[:], 0.0), writes=['zeros'])

    def finalize_head(s, h, Ob, gs, ke, E1, kdT, pp, st):
        oall = [('O', tt) for tt in range(NT)]
        T.op('act', lambda e: e.activation(out=ke[:], in_=Ob[:], func=AF.Square), reads=oall, writes=['ke'])
        for tg in range(4):
            pb = st['pp'] % 2
            st['pp'] += 1
            T.op('pe', lambda e: e.matmul(pp[:, pb, :], lhsT=ones_bf[:], rhs=ke[:, tg * 512:(tg + 1) * 512],
                                          start=True, stop=True),
                 reads=['ke', 'ones'], writes=[('pp', pb)])
            T.op('act', lambda e: e.activation(out=E1[:, tg * 512:(tg + 1) * 512], in_=pp[:, pb, :], func=AF.Ln,
                                               scale=1.0 / 128, bias=epsb[:]),
                 reads=[('pp', pb), 'epsb'], writes=['E1'])
        T.op('act', lambda e: e.activation(out=E1[:], in_=E1[:], func=AF.Exp, scale=-0.5), reads=['E1'], writes=['E1'])
        T.op('dve', lambda e: e.scalar_tensor_tensor(out=Ob[:], in0=Ob[:], scalar=gw[:, 0:1], in1=E1[:],
                                                     op0=ALU.mult, op1=ALU.mult),
             reads=oall + ['E1', 'gw'], writes=oall)
        T.op('pool', lambda e: e.tensor_tensor(out=kdT[:], in0=Ob[:], in1=gs[:], op=ALU.mult),
             reads=oall + ['gs'], writes=['kdT'])
        T.dma('sp', onT_d[s, h], kdT[:], reads=['kdT'], writes=[('onT_d', s, h)], stream='o')
        if debug:
            T.dma('sp', dbg['onT'][s, h], kdT[:], reads=['kdT'], writes=[('dbg_onT', s, h)], stream='o')

    Osave_d = nc.dram_tensor("Osave_d", [KC, 128, TS], F32).ap()
    gs_d = nc.dram_tensor("gs_d", [KC, 128, TS], BF16).ap()
    qeb_d = nc.dram_tensor("qeb_d", [KC, 2, 128, TS], BF16).ap()
    Send_d = nc.dram_tensor("Send_d", [KC * 2 * 128, 128], F32)
    SG = nc.dram_tensor("SG", [NCORES * KC * 2 * 128, 128], F32)
    Dtot_d = nc.dram_tensor("Dtot_d", [128, 256], F32)
    DG = nc.dram_tensor("DG", [NCORES * 128, 256], F32)
    Dtot_sb = nc.alloc_sbuf_tensor("Dtot_sb", [128, 256], F32)
    T.op('dve', lambda e: e.memset(Dtot_sb[:], 0.0), writes=['Dtot_sb'])
    zeros64 = nc.alloc_sbuf_tensor("zeros64", [128, NSTEP], F32)
    T.op('dve', lambda e: e.memset(zeros64[:], 0.0), writes=['zeros64'])

    def l0_mixer(s):
        defer = (s == 0) and not no_exchange
        with nc.sbuf_tensor(uname("hnT"), [128, KC, TS], BF16) as hnT:
            with ExitStack() as es:
                gam = es.enter_context(nc.sbuf_tensor(uname("gam"), [128, D], F32))
                xt = es.enter_context(nc.sbuf_tensor(uname("xt"), [128, 2, D], F32))
                hn = es.enter_context(nc.sbuf_tensor(uname("hn"), [128, 2, D], BF16))
                ss = es.enter_context(nc.sbuf_tensor(uname("ss"), [128, NT], F32))
                pst = es.enter_context(nc.psum_tensor(uname("pst"), [128, 2, KC, 128], BF16))
                T.dma('sp', gam[:], norm_mixer[0, :].partition_broadcast(128), writes=['gam'], stream='c')
                for tt in range(NT):
                    b = tt % 2
                    T.dma('sp', xt[:, b, :], x_in[s, tt * 128:(tt + 1) * 128, :], writes=[('xt', b)], stream='x')
                    T.op('act', lambda e: e.activation(out=hn[:, b, :], in_=xt[:, b, :], func=AF.Square,
                                                       accum_out=ss[:, tt:tt + 1]),
                         reads=[('xt', b)], writes=[('hn', b), ('ss', tt)])
                    T.op('act', lambda e: e.activation(out=ss[:, tt:tt + 1], in_=ss[:, tt:tt + 1], func=AF.Ln,
                                                       scale=1.0 / D, bias=epsb[:]),
                         reads=[('ss', tt), 'epsb'], writes=[('ss', tt)])
                    T.op('act', lambda e: e.activation(out=ss[:, tt:tt + 1], in_=ss[:, tt:tt + 1], func=AF.Exp, scale=-0.5),
                         reads=[('ss', tt)], writes=[('ss', tt)])
                    T.op('dve', lambda e: e.scalar_tensor_tensor(out=hn[:, b, :], in0=xt[:, b, :], scalar=ss[:, tt:tt + 1],
                                                                 in1=gam[:], op0=ALU.mult, op1=ALU.mult),
                         reads=[('xt', b), ('ss', tt), 'gam'], writes=[('hn', b)])

                    def tr(e):
                        for kc in range(KC):
                            ins = e.transpose(out=pst[:, b, kc, :], in_=hn[:, b, kc * 128:(kc + 1) * 128], identity=ident[:])
                        return ins
                    T.op('pe', tr, reads=[('hn', b), 'ident'], writes=[('pst', b)])
                    T.op('act', lambda e: e.copy(out=hnT[:, :, tt * 128:(tt + 1) * 128], in_=pst[:, b, :, :]),
                         reads=[('pst', b)], writes=['hnT'])
                T.barrier()
            with ExitStack() as es:
                W = es.enter_context(nc.sbuf_tensor(uname("W"), [128, 5, KC, 128], BF16))
                qs = es.enter_context(nc.sbuf_tensor(uname("qs"), [128, TS], F32))
                gs = es.enter_context(nc.sbuf_tensor(uname("gs"), [128, TS], BF16))
                vtm = es.enter_context(nc.sbuf_tensor(uname("vtm"), [128, NT, 128], BF16))
                Vbd = es.enter_context(nc.sbuf_tensor(uname("Vbd"), [128, NT, 4, 128], BF16))
                Fb = es.enter_context(nc.sbuf_tensor(uname("Fb"), [128, TS], F32))
                LF = es.enter_context(nc.sbuf_tensor(uname("LF"), [128, TS], F32))
                Bb = es.enter_context(nc.sbuf_tensor(uname("Bb"), [128, TS], F32))
                E1 = es.enter_context(nc.sbuf_tensor(uname("E1"), [128, TS], F32))
                qe = es.enter_context(nc.sbuf_tensor(uname("qe"), [128, 2, TS], BF16))
                ke = es.enter_context(nc.sbuf_tensor(uname("ke"), [128, TS], BF16))
                kdT = es.enter_context(nc.sbuf_tensor(uname("kdT"), [128, TS], BF16))
                kdtm = es.enter_context(nc.sbuf_tensor(uname("kdtm"), [128, 2, NT, 128], BF16))
                scT = es.enter_context(nc.sbuf_tensor(uname("scT"), [128, 2, NT, 128], BF16))
                Ob = es.enter_context(nc.sbuf_tensor(uname("Ob"), [128, TS], F32))
                TOT = es.enter_context(nc.sbuf_tensor(uname("TOT"), [128, 2, NSTEP], F32))
                Dm = es.enter_context(nc.sbuf_tensor(uname("Dm"), [128, 2, NSTEP], F32))
                Sst = es.enter_context(nc.sbuf_tensor(uname("Sst"), [128, 2, 128], F32))
                Sbf = es.enter_context(nc.sbuf_tensor(uname("Sbf"), [128, 2, 2, 128], BF16))
                pp = es.enter_context(nc.psum_tensor(uname("pp"), [128, 2, 512], F32))
                pm = es.enter_context(nc.psum_tensor(uname("pm"), [128, NT, 32], F32))
                ptr = es.enter_context(nc.psum_tensor(uname("ptr"), [128, 8, 128], BF16))
                pu = es.enter_context(nc.psum_tensor(uname("pu"), [128, 2, 4, 128], F32))
                po = es.enter_context(nc.psum_tensor(uname("po"), [128, 2, 4, 128], F32))
                st = {'pp': 0, 'pu': 0}
                T.op('pool', lambda e: e.memset(Vbd[:], 0.0), writes=['Vbd'])
                T.op('pool', lambda e: e.memset(scT[:], 0.0), writes=[('scT', 0), ('scT', 1)])
                B3 = Bb[:].rearrange("p (n t) -> p n t", t=32)
                LF3 = LF[:].rearrange("p (n t) -> p n t", t=32)

                def load_w(h):
                    for blk in range(5):
                        c0 = blk * D + h * 128
                        T.dma('pool', W[:, blk, :, :],
                              hgrn_w_in[0, :, c0:c0 + 128].rearrange("(kc p) c -> p kc c", p=128),
                              writes=[('W', blk)], stream='w')

                def proj_fm(blk, func, dst, dname):
                    for tg in range(4):
                        pb = st['pp'] % 2
                        st['pp'] += 1

                        def mm(e):
                            for kc in range(KC):
                                ins = e.matmul(pp[:, pb, :], lhsT=W[:, blk, kc, :], rhs=hnT[:, kc, tg * 512:(tg + 1) * 512],
                                               start=(kc == 0), stop=(kc == KC - 1))
                            return ins
                        T.op('pe', mm, reads=[('W', blk)], writes=[('pp', pb)])
                        T.op('act', lambda e: e.activation(out=dst[:, tg * 512:(tg + 1) * 512], in_=pp[:, pb, :], func=func),
                             reads=[('pp', pb)], writes=[dname])

                def proj_v():
                    for t4 in range(4):
                        pb = st['pp'] % 2
                        st['pp'] += 1

                        def mm(e):
                            for j in range(4):
                                tt = t4 * 4 + j
                                for kc in range(KC):
                                    ins = e.matmul(pp[:, pb, j * 128:(j + 1) * 128], lhsT=hnT[:, kc, tt * 128:(tt + 1) * 128],
                                                   rhs=W[:, 3, kc, :], start=(kc == 0), stop=(kc == KC - 1))
                            return ins
                        T.op('pe', mm, reads=[('W', 3)], writes=[('pp', pb)])
                        T.op('act', lambda e: e.copy(out=vtm[:, t4 * 4:(t4 + 1) * 4, :],
                                                     in_=pp[:, pb, :].rearrange("p (j e) -> p j e", j=4)),
                             reads=[('pp', pb)], writes=['vtm'])
                    for j in range(4):
                        T.op('pool', lambda e: e.tensor_copy(out=Vbd[32 * j:32 * j + 32, :, j, :], in_=vtm[32 * j:32 * j + 32, :, :]),
                             reads=['vtm'], writes=['Vbd'])

                def prep(h, r):
                    T.op('dve', lambda e: e.tensor_scalar(out=Fb[:], in0=Fb[:], scalar1=oml[:, r, h:h + 1], scalar2=lb0[:, r, h:h + 1],
                                                          op0=ALU.mult, op1=ALU.add),
                         reads=['F', 'oml', 'lb0'], writes=['F'])
                    T.op('act', lambda e: e.activation(out=LF[:], in_=Fb[:], func=AF.Ln), reads=['F'], writes=['LF'])
                    T.op('pool', lambda e: e.tensor_scalar(out=Fb[:], in0=Fb[:], scalar1=-1.0, scalar2=1.0, op0=ALU.mult, op1=ALU.add),
                         reads=['F'], writes=['F'])

                    def scans(e):
                        for n in range(NSTEP):
                            ins = e.tensor_tensor_scan(out=Bb[:, n * 32:(n + 1) * 32], data0=zeros[:], data1=LF[:, n * 32:(n + 1) * 32],
                                                       initial=0.0, op0=ALU.add, op1=ALU.add)
                        return ins
                    T.op('dve', scans, reads=['LF', 'zeros'], writes=['B'])
                    T.op('dve', lambda e: e.tensor_copy(out=TOT[:, r, :], in_=B3[:, :, 31]), reads=['B'], writes=[('TOT', r)])
                    if r == 1:
                        T.op('dve', lambda e: e.tensor_tensor(out=LF[:], in0=LF[:], in1=Bb[:], op=ALU.subtract),
                             reads=['LF', 'B'], writes=['LF'])
                        T.op('dve', lambda e: e.tensor_tensor(out=B3, in0=LF3,
                                                              in1=TOT[:, r, :].unsqueeze(2).to_broadcast([128, NSTEP, 32]), op=ALU.add),
                             reads=['LF', ('TOT', r)], writes=['B'])
                    T.op('act', lambda e: e.activation(out=LF[:], in_=Bb[:], func=AF.Exp), reads=['B'], writes=['LF'])
                    T.op('dve', lambda e: e.scalar_tensor_tensor(out=qe[:, r, :], in0=qs[:], scalar=QSCALE, in1=LF[:],
                                                                 op0=ALU.mult, op1=ALU.mult),
                         reads=['qs', 'LF'], writes=[('qe', r)])
                    T.op('act', lambda e: e.activation(out=E1[:], in_=Bb[:], func=AF.Exp, scale=-1.0), reads=['B'], writes=['E1'])
                    T.op('pool', lambda e: e.tensor_tensor(out=ke[:], in0=Fb[:], in1=E1[:], op=ALU.mult),
                         reads=['F', 'E1'], writes=['ke'])
                    T.op('dve', lambda e: e.tensor_tensor(out=B3, in0=TOT[:, r, :].unsqueeze(2).to_broadcast([128, NSTEP, 32]),
                                                          in1=B3, op=ALU.subtract),
                         reads=['B', ('TOT', r)], writes=['B'])
                    T.op('act', lambda e: e.activation(out=LF[:], in_=Bb[:], func=AF.Exp), reads=['B'], writes=['LF'])
                    T.op('pool', lambda e: e.tensor_tensor(out=kdT[:], in0=Fb[:], in1=LF[:], op=ALU.mult),
                         reads=['F', 'LF'], writes=['kdT'])
                    T.op('act', lambda e: e.activation(out=Dm[:, r, :], in_=TOT[:, r, :], func=AF.Exp),
                         reads=[('TOT', r)], writes=[('Dm', r)])
                    for g8 in range(2):
                        def trk(e):
                            for q in range(8):
                                tt = g8 * 8 + q
                                ins = e.transpose(out=ptr[:, q, :], in_=kdT[:, tt * 128:(tt + 1) * 128], identity=ident[:])
                            return ins
                        T.op('pe', trk, reads=['kdT'], writes=['ptr'])
                        T.op('act', lambda e: e.copy(out=kdtm[:, r, g8 * 8:(g8 + 1) * 8, :], in_=ptr[:]),
                             reads=['ptr'], writes=[('kdtm', r)])

                    def scores(e):
                        for n in range(NSTEP):
                            tt, j = divmod(n, 4)
                            ins = e.matmul(pm[32 * j:32 * j + 32, tt, :], lhsT=ke[:, n * 32:(n + 1) * 32],
                                           rhs=qe[:, r, n * 32:(n + 1) * 32], start=True, stop=True, tile_position=(0, 32 * j))
                        return ins
                    T.op('pe', scores, reads=['ke', ('qe', r)], writes=['pm'])
                    for j in range(4):
                        T.op('dve', lambda e: e.tensor_tensor(out=scT[32 * j:32 * j + 32, r, :, 32 * j:32 * j + 32],
                                                              in0=pm[32 * j:32 * j + 32, :, :],
                                                              in1=masks[32 * j:32 * j + 32, r, :].unsqueeze(1).to_broadcast([32, NT, 32]),
                                                              op=ALU.mult),
                             reads=['pm', 'masks'], writes=[('scT', r)])

                INC = es.enter_context(nc.sbuf_tensor(uname("INC"), [128, 2, NSTEP], F32))
                EO = es.enter_context(nc.sbuf_tensor(uname("EO"), [128, 2, NSTEP], F32))

                def qeb_part(h, r):
                    T.op('dve', lambda e: e.tensor_tensor_scan(out=INC[:, r, :], data0=zeros64[:], data1=TOT[:, r, :], initial=0.0,
                                                               op0=ALU.add, op1=ALU.add),
                         reads=[('TOT', r), 'zeros64'], writes=[('INC', r)])
                    if r == 0:
                        T.op('dve', lambda e: e.tensor_tensor(out=EO[:, r, :], in0=INC[:, r, :], in1=TOT[:, r, :], op=ALU.subtract),
                             reads=[('INC', r), ('TOT', r)], writes=[('EO', r)])
                    else:
                        T.op('dve', lambda e: e.tensor_scalar(out=EO[:, r, :], in0=INC[:, r, :], scalar1=-1.0,
                                                              scalar2=INC[:, r, NSTEP - 1:NSTEP], op0=ALU.mult, op1=ALU.add),
                             reads=[('INC', r)], writes=[('EO', r)])
                    T.op('act', lambda e: e.activation(out=EO[:, r, :], in_=EO[:, r, :], func=AF.Exp), reads=[('EO', r)], writes=[('EO', r)])
                    T.op('act', lambda e: e.activation(out=Dtot_sb[:, r * KC + h:r * KC + h + 1], in_=INC[:, r, NSTEP - 1:NSTEP], func=AF.Exp),
                         reads=[('INC', r)], writes=['Dtot_sb'])
                    T.op('dve', lambda e: e.tensor_tensor(out=ke[:].rearrange("p (n t) -> p n t", t=32),
                                                          in0=qe[:, r, :].rearrange("p (n t) -> p n t", t=32),
                                                          in1=EO[:, r, :].unsqueeze(2).to_broadcast([128, NSTEP, 32]), op=ALU.mult),
                         reads=[('qe', r), ('EO', r)], writes=['ke'])
                    T.dma('sp', qeb_d[h, r], ke[:], reads=['ke'], writes=[('qeb_d', h)], stream='o')

                def recur(h):
                    owritten = set()
                    uslot = {}
                    for i in range(NSTEP):
                        for r in range(2):
                            n = i if r == 0 else NSTEP - 1 - i
                            tt, j = divmod(n, 4)
                            first_of_tile = (j == 0) if r == 0 else (j == 3)
                            last_of_tile = (j == 3) if r == 0 else (j == 0)
                            osl = tt % 4
                            if first_of_tile:
                                us = st['pu'] % 2
                                st['pu'] += 1
                                uslot[r] = us
                                T.op('pe', lambda e: e.matmul(pu[:, us, :, :], lhsT=kdtm[:, r, tt, :], rhs=Vbd[:, tt, :, :],
                                                              start=True, stop=True),
                                     reads=[('kdtm', r), 'Vbd'], writes=[('pu', us)])
                                T.op('pe', lambda e: e.matmul(po[:, r, osl, :], lhsT=vtm[:, tt, :], rhs=scT[:, r, tt, :],
                                                              start=True, stop=False),
                                     reads=['vtm', ('scT', r)], writes=[('po', r, osl)])
                            us = uslot[r]
                            if i > 0:
                                T.op('pe', lambda e: e.matmul(po[:, r, osl, j * 32:(j + 1) * 32], lhsT=Sbf[:, r, i % 2, :],
                                                              rhs=qe[:, r, n * 32:(n + 1) * 32], start=False, stop=True),
                                     reads=[('Sbf', r, i % 2), ('qe', r)], writes=[('po', r, osl)])
                            if i < NSTEP - 1 or defer:
                                if i == 0:
                                    T.op('dve', lambda e: e.tensor_copy(out=Sst[:, r, :], in_=pu[:, us, j, :]),
                                         reads=[('pu', us)], writes=[('S', r)])
                                else:
                                    T.op('dve', lambda e: e.scalar_tensor_tensor(out=Sst[:, r, :], in0=Sst[:, r, :],
                                                                                 scalar=Dm[:, r, n:n + 1], in1=pu[:, us, j, :],
                                                                                 op0=ALU.mult, op1=ALU.add),
                                         reads=[('pu', us), ('S', r), ('Dm', r)], writes=[('S', r)])
                                if i < NSTEP - 1:
                                    T.op('act', lambda e: e.copy(out=Sbf[:, r, (i + 1) % 2, :], in_=Sst[:, r, :]),
                                         reads=[('S', r)], writes=[('Sbf', r, (i + 1) % 2)])
                            if last_of_tile:
                                if tt not in owritten:
                                    owritten.add(tt)
                                    T.op('dve', lambda e: e.tensor_copy(out=Ob[:, tt * 128:(tt + 1) * 128], in_=po[:, r, osl, :]),
                                         reads=[('po', r, osl)], writes=[('O', tt)])
                                else:
                                    T.op('dve', lambda e: e.tensor_tensor(out=Ob[:, tt * 128:(tt + 1) * 128],
                                                                          in0=Ob[:, tt * 128:(tt + 1) * 128], in1=po[:, r, osl, :],
                                                                          op=ALU.add),
                                         reads=[('po', r, osl), ('O', tt)], writes=[('O', tt)])

                load_w(0)
                for h in range(KC):
                    proj_fm(0, AF.Silu, qs, 'qs')
                    proj_fm(4, AF.Silu, gs, 'gs')
                    proj_v()
                    proj_fm(1, AF.Sigmoid, Fb, 'F')
                    prep(h, 0)
                    proj_fm(2, AF.Sigmoid, Fb, 'F')
                    if h + 1 < KC:
                        load_w(h + 1)
                    prep(h, 1)
                    if defer:
                        qeb_part(h, 0)
                        qeb_part(h, 1)
                    recur(h)
                    if defer:
                        oall = [('O', tt) for tt in range(NT)]
                        T.dma('sp', Osave_d[h], Ob[:], reads=oall, writes=[('Osave_d', h)], stream='o')
                        T.dma('sp', gs_d[h], gs[:], reads=['gs'], writes=[('gs_d', h)], stream='o')
                        for r in range(2):
                            T.dma('sp', Send_d.ap()[(h * 2 + r) * 128:(h * 2 + r + 1) * 128, :], Sst[:, r, :],
                                  reads=[('S', r)], writes=['Send_d'], stream='o')
                    else:
                        finalize_head(s, h, Ob, gs, ke, E1, kdT, pp, st)
                if defer:
                    T.dma('sp', Dtot_d.ap(), Dtot_sb[:], reads=['Dtot_sb'], writes=['Dtot_d'], stream='o')
                T.barrier()

    x1_d = nc.dram_tensor("x1_d", [2, TS, D], F32).ap()
    if debug:
        dbg['x1'] = nc.dram_tensor("dbg_x1", [2, TS, D], F32, kind="ExternalOutput").ap()
    G = 1024
    GT = G // 128

    def post_mixer(s, L, after_group=None):
        w_out_ap = hgrn_w_out[0] if L == 0 else diff_w_out[0]
        resid = x_in[s] if L == 0 else x1_d[s]
        for g in range(TS // G):
            t0 = g * G
            with ExitStack() as es0:
                yacc = es0.enter_context(nc.sbuf_tensor(uname("yacc"), [128, GT, D], F32))
                h2T = es0.enter_context(nc.sbuf_tensor(uname("h2T"), [128, KC, G], BF16))
                with ExitStack() as es:
                    onTg = es.enter_context(nc.sbuf_tensor(uname("onTg"), [128, KC, G], BF16))
                    wo = es.enter_context(nc.sbuf_tensor(uname("wo"), [128, 2, KC, 512], BF16))
                    hn2 = es.enter_context(nc.sbuf_tensor(uname("hn2"), [128, 2, D], BF16))
                    gam2 = es.enter_context(nc.sbuf_tensor(uname("gam2"), [128, D], F32))
                    ss2 = es.enter_context(nc.sbuf_tensor(uname("ss2"), [128, GT], F32))
                    ppa = es.enter_context(nc.psum_tensor(uname("ppa"), [128, 4, 512], F32))
                    pst2 = es.enter_context(nc.psum_tensor(uname("pst2"), [128, 2, KC, 128], BF16))
                    for tt in range(GT):
                        T.dma('sp', yacc[:, tt, :], resid[t0 + tt * 128:t0 + (tt + 1) * 128, :], writes=[('yacc', tt)], stream='x')
                    for kc in range(KC):
                        T.dma('sp', onTg[:, kc, :], onT_d[s, kc, :, t0:t0 + G], reads=[('onT_d', s, kc)], writes=['onTg'], stream='x')
                    T.dma('sp', gam2[:], norm_ffn[L, :].partition_broadcast(128), writes=['gam2'], stream='c')
                    k = 0
                    for cg in range(4):
                        wb = cg % 2
                        T.dma('pool', wo[:, wb, :, :], w_out_ap[:, cg * 512:(cg + 1) * 512].rearrange("(kc p) c -> p kc c", p=128),
                              writes=[('wo', wb)], stream='w')
                        for tt in range(GT):
                            slot = k % 4
                            k += 1

                            def mm(e):
                                for kc in range(KC):
                                    ins = e.matmul(ppa[:, slot, :], lhsT=onTg[:, kc, tt * 128:(tt + 1) * 128], rhs=wo[:, wb, kc, :],
                                                   start=(kc == 0), stop=(kc == KC - 1))
                                return ins
                            T.op('pe', mm, reads=['onTg', ('wo', wb)], writes=[('ppa', slot)])
                            T.op('dve', lambda e: e.tensor_tensor(out=yacc[:, tt, cg * 512:(cg + 1) * 512],
                                                                  in0=yacc[:, tt, cg * 512:(cg + 1) * 512], in1=ppa[:, slot, :], op=ALU.add),
                                 reads=[('ppa', slot), ('yacc', tt)], writes=[('yacc', tt)])
                    for tt in range(GT):
                        b = tt % 2
                        T.op('act', lambda e: e.activation(out=hn2[:, b, :], in_=yacc[:, tt, :], func=AF.Square,
                                                           accum_out=ss2[:, tt:tt + 1]),
                             reads=[('yacc', tt)], writes=[('hn2', b), ('ss2', tt)])
                        T.op('act', lambda e: e.activation(out=ss2[:, tt:tt + 1], in_=ss2[:, tt:tt + 1], func=AF.Ln,
                                                           scale=1.0 / D, bias=epsb[:]),
                             reads=[('ss2', tt), 'epsb'], writes=[('ss2', tt)])
                        T.op('act', lambda e: e.activation(out=ss2[:, tt:tt + 1], in_=ss2[:, tt:tt + 1], func=AF.Exp, scale=-0.5),
                             reads=[('ss2', tt)], writes=[('ss2', tt)])
                        T.op('dve', lambda e: e.scalar_tensor_tensor(out=hn2[:, b, :], in0=yacc[:, tt, :], scalar=ss2[:, tt:tt + 1],
                                                                     in1=gam2[:], op0=ALU.mult, op1=ALU.mult),
                             reads=[('yacc', tt), ('ss2', tt), 'gam2'], writes=[('hn2', b)])

                        def tr(e):
                            for kc in range(KC):
                                ins = e.transpose(out=pst2[:, b, kc, :], in_=hn2[:, b, kc * 128:(kc + 1) * 128], identity=ident[:])
                            return ins
                        T.op('pe', tr, reads=[('hn2', b), 'ident'], writes=[('pst2', b)])
                        T.op('act', lambda e: e.copy(out=h2T[:, :, tt * 128:(tt + 1) * 128], in_=pst2[:, b, :, :]),
                             reads=[('pst2', b)], writes=['h2T'])
                    T.barrier()
                with ExitStack() as es:
                    wg = es.enter_context(nc.sbuf_tensor(uname("wg"), [128, KC, 512], BF16))
                    wu = es.enter_context(nc.sbuf_tensor(uname("wu"), [128, KC, 512], BF16))
                    wd = es.enter_context(nc.sbuf_tensor(uname("wd"), [128, 4, D], BF16))
                    aT = es.enter_context(nc.sbuf_tensor(uname("aT"), [128, 2, 4, G], BF16))
                    sg = es.enter_context(nc.sbuf_tensor(uname("sg"), [128, 2, 512], F32))
                    pg = es.enter_context(nc.psum_tensor(uname("pg"), [128, 4, 512], F32))
                    pd = es.enter_context(nc.psum_tensor(uname("pd"), [128, 4, 512], F32))
                    kg = 0
                    kd_ = 0
                    ks = 0
                    for jb in range(NJ // 4):
                        ab = jb % 2
                        T.dma('pool', wg[:], ffn_w_gate[L, :, jb * 512:(jb + 1) * 512].rearrange("(kc p) c -> p kc c", p=128),
                              writes=['wg'], stream='w')
                        T.dma('pool', wu[:], ffn_w_up[L, :, jb * 512:(jb + 1) * 512].rearrange("(kc p) c -> p kc c", p=128),
                              writes=['wu'], stream='w')
                        T.dma('pool', wd[:], ffn_w_down[L, jb * 512:(jb + 1) * 512, :].rearrange("(j p) c -> p j c", p=128),
                              writes=['wd'], stream='w')
                        for c4 in range(4):
                            for tg in range(G // 512):
                                s1 = kg % 4
                                s2 = (kg + 1) % 4
                                kg += 2
                                sb = ks % 2
                                ks += 1

                                def mmg(e):
                                    for kc in range(KC):
                                        ins = e.matmul(pg[:, s1, :], lhsT=wg[:, kc, c4 * 128:(c4 + 1) * 128], rhs=h2T[:, kc, tg * 512:(tg + 1) * 512],
                                                       start=(kc == 0), stop=(kc == KC - 1))
                                    return ins

                                def mmu(e):
                                    for kc in range(KC):
                                        ins = e.matmul(pg[:, s2, :], lhsT=wu[:, kc, c4 * 128:(c4 + 1) * 128], rhs=h2T[:, kc, tg * 512:(tg + 1) * 512],
                                                       start=(kc == 0), stop=(kc == KC - 1))
                                    return ins
                                T.op('pe', mmg, reads=['wg'], writes=[('pg', s1)])
                                T.op('pe', mmu, reads=['wu'], writes=[('pg', s2)])
                                T.op('act', lambda e: e.activation(out=sg[:, sb, :], in_=pg[:, s1, :], func=AF.Silu),
                                     reads=[('pg', s1)], writes=[('sg', sb)])
                                T.op('dve', lambda e: e.tensor_tensor(out=aT[:, ab, c4, tg * 512:(tg + 1) * 512], in0=sg[:, sb, :],
                                                                      in1=pg[:, s2, :], op=ALU.mult),
                                     reads=[('sg', sb), ('pg', s2)], writes=[('aT', ab)])
                        for tt in range(GT):
                            for cg in range(4):
                                sd = kd_ % 4
                                kd_ += 1

                                def mmd(e):
                                    for c4 in range(4):
                                        ins = e.matmul(pd[:, sd, :], lhsT=aT[:, ab, c4, tt * 128:(tt + 1) * 128], rhs=wd[:, c4, cg * 512:(cg + 1) * 512],
                                                       start=(c4 == 0), stop=(c4 == 3))
                                    return ins
                                T.op('pe', mmd, reads=[('aT', ab), 'wd'], writes=[('pd', sd)])
                                T.op('dve', lambda e: e.tensor_tensor(out=yacc[:, tt, cg * 512:(cg + 1) * 512],
                                                                      in0=yacc[:, tt, cg * 512:(cg + 1) * 512], in1=pd[:, sd, :], op=ALU.add),
                                     reads=[('pd', sd), ('yacc', tt)], writes=[('yacc', tt)])
                    T.barrier()
                if after_group is not None:
                    after_group(s, g, t0, yacc, h2T)
                T.barrier()

    QT_d = nc.dram_tensor("QT_d", [2, 16, 128, TS], BF16).ap()
    kt_d = [[nc.dram_tensor(f"kt_d{s}_{i}", [512, TS], BF16) for i in range(4)] for s in range(2)]
    v_d = [[nc.dram_tensor(f"v_d{s}_{j}", [TS, 512], BF16) for j in range(4)] for s in range(2)]
    KTg = [nc.dram_tensor(f"KTg{i}", [NCORES * 512, TS], BF16) for i in range(4)]
    Vg = [nc.dram_tensor(f"Vg{j}", [NCORES * TS, 512], BF16) for j in range(4)]

    neglam = nc.alloc_sbuf_tensor("neglam", [128, 1], F32)
    swl = nc.alloc_sbuf_tensor("swl", [128, 256], F32)
    ebias = nc.alloc_sbuf_tensor("ebias", [128, 1], F32)
    lamt = nc.alloc_sbuf_tensor("lamt", [128, 4, 128], F32)
    lam2 = nc.alloc_sbuf_tensor("lam2", [128, 2, 128], F32)
    lams = nc.alloc_sbuf_tensor("lams", [128, 2], F32)
    for i, ap_ in enumerate((lq1, lk1, lq2, lk2)):
        T.dma('sp', lamt[:, i, :], ap_[0, :].partition_broadcast(128), writes=['lamt'], stream='c')
    T.dma('sp', swl[:], diff_subln[0, :].partition_broadcast(128), writes=['swl'], stream='c')
    T.op('dve', lambda e: e.tensor_tensor(out=lam2[:], in0=lamt[:, 0:4:2, :], in1=lamt[:, 1:4:2, :], op=ALU.mult), reads=['lamt'], writes=['lam2'])
    T.op('dve', lambda e: e.tensor_reduce(out=lams[:], in_=lam2[:], axis=mybir.AxisListType.X, op=ALU.add), reads=['lam2'], writes=['lams'])
    T.op('act', lambda e: e.activation(out=lams[:], in_=lams[:], func=AF.Exp), reads=['lams'], writes=['lams'])
    T.op('dve', lambda e: e.tensor_tensor(out=neglam[:], in0=lams[:, 1:2], in1=lams[:, 0:1], op=ALU.subtract), reads=['lams'], writes=['neglam'])
    T.op('dve', lambda e: e.tensor_scalar(out=neglam[:], in0=neglam[:], scalar1=-LAMBDA_INIT, scalar2=None, op0=ALU.add), reads=['neglam'], writes=['neglam'])
    T.op('dve', lambda e: e.tensor_scalar(out=swl[:], in0=swl[:], scalar1=1.0 - LAMBDA_INIT, scalar2=None, op0=ALU.mult), reads=['swl'], writes=['swl'])
    T.op('dve', lambda e: e.memset(ebias[:], 0.0), writes=['ebias'])

    def l0_after(s, g, t0, yacc, h2T):
        for tt in range(GT):
            T.dma('sp', x1_d[s, t0 + tt * 128:t0 + (tt + 1) * 128, :], yacc[:, tt, :], reads=[('yacc', tt)],
                  writes=[('x1_d', s, g)], stream='o')
            if debug:
                T.dma('sp', dbg['x1'][s, t0 + tt * 128:t0 + (tt + 1) * 128, :], yacc[:, tt, :], reads=[('yacc', tt)],
                      writes=[('dbg_x1', s, g)], stream='o')
        if stage < 2:
            return
        h1T = h2T
        with ExitStack() as es:
            hn1 = es.enter_context(nc.sbuf_tensor(uname("hn1"), [128, 2, D], BF16))
            gam1 = es.enter_context(nc.sbuf_tensor(uname("gam1"), [128, D], F32))
            ss1 = es.enter_context(nc.sbuf_tensor(uname("ss1"), [128, GT], F32))
            Wb = es.enter_context(nc.sbuf_tensor(uname("Wb"), [128, 2, KC, 512], BF16))
            rope = es.enter_context(nc.sbuf_tensor(uname("rope"), [128, GT, 2, 64], F32))
            xq = es.enter_context(nc.sbuf_tensor(uname("xq"), [128, 2, 512], F32))
            XC = es.enter_context(nc.sbuf_tensor(uname("XC"), [128, 2, 512], F32))
            XS = es.enter_context(nc.sbuf_tensor(uname("XS"), [128, 2, 512], F32))
            rq = es.enter_context(nc.sbuf_tensor(uname("rq"), [128, 2, 512], BF16))
            stq = es.enter_context(nc.sbuf_tensor(uname("stq"), [128, 2, 4, G], BF16))
            vst = es.enter_context(nc.sbuf_tensor(uname("vst"), [128, 2, GT, 512], BF16))
            pst1 = es.enter_context(nc.psum_tensor(uname("pst1"), [128, 2, KC, 128], BF16))
            ppq = es.enter_context(nc.psum_tensor(uname("ppq"), [128, 2, 512], F32))
            ptq = es.enter_context(nc.psum_tensor(uname("ptq"), [128, 2, 4, 128], BF16))
            T.dma('sp', gam1[:], norm_mixer[1, :].partition_broadcast(128), writes=['gam1'], stream='c')
            T.dma('sp', rope[:], c_rope[s, t0:t0 + G, :, :].rearrange("(tt p) a f -> p tt a f", p=128), writes=['rope'], stream='c')
            for tt in range(GT):
                b = tt % 2
                T.op('act', lambda e: e.activation(out=hn1[:, b, :], in_=yacc[:, tt, :], func=AF.Square,
                                                   accum_out=ss1[:, tt:tt + 1]),
                     reads=[('yacc', tt)], writes=[('hn1', b), ('ss1', tt)])
                T.op('act', lambda e: e.activation(out=ss1[:, tt:tt + 1], in_=ss1[:, tt:tt + 1], func=AF.Ln,
                                                   scale=1.0 / D, bias=epsb[:]),
                     reads=[('ss1', tt), 'epsb'], writes=[('ss1', tt)])
                T.op('act', lambda e: e.activation(out=ss1[:, tt:tt + 1], in_=ss1[:, tt:tt + 1], func=AF.Exp, scale=-0.5),
                     reads=[('ss1', tt)], writes=[('ss1', tt)])
                T.op('dve', lambda e: e.scalar_tensor_tensor(out=hn1[:, b, :], in0=yacc[:, tt, :], scalar=ss1[:, tt:tt + 1],
                                                             in1=gam1[:], op0=ALU.mult, op1=ALU.mult),
                     reads=[('yacc', tt), ('ss1', tt), 'gam1'], writes=[('hn1', b)])

                def tr(e):
                    for kc in range(KC):
                        ins = e.transpose(out=pst1[:, b, kc, :], in_=hn1[:, b, kc * 128:(kc + 1) * 128], identity=ident[:])
                    return ins
                T.op('pe', tr, reads=[('hn1', b), 'ident'], writes=[('pst1', b)])
                T.op('act', lambda e: e.copy(out=h1T[:, :, tt * 128:(tt + 1) * 128], in_=pst1[:, b, :, :]),
                     reads=[('pst1', b)], writes=['h1T'])
            k = 0
            for blk in range(12):
                wb = blk % 2
                T.dma('pool', Wb[:, wb, :, :], diff_w_in[0, :, blk * 512:(blk + 1) * 512].rearrange("(kc p) c -> p kc c", p=128),
                      writes=[('Wb', wb)], stream='w')
                for tt in range(GT):
                    sl = k % 2
                    k += 1

                    def mm(e):
                        for kc in range(KC):
                            ins = e.matmul(ppq[:, sl, :], lhsT=h1T[:, kc, tt * 128:(tt + 1) * 128], rhs=Wb[:, wb, kc, :],
                                           start=(kc == 0), stop=(kc == KC - 1))
                        return ins
                    T.op('pe', mm, reads=['h1T', ('Wb', wb)], writes=[('ppq', sl)])
                    if blk < 8:
                        T.op('act', lambda e: e.copy(out=xq[:, sl, :], in_=ppq[:, sl, :]), reads=[('ppq', sl)], writes=[('xq', sl)])
                        x4 = xq[:, sl, :].rearrange("p (m a f) -> p m a f", m=4, a=2)
                        c4_ = XC[:, sl, :].rearrange("p (m a f) -> p m a f", m=4, a=2)
                        s4_ = XS[:, sl, :].rearrange("p (m a f) -> p m a f", m=4, a=2)
                        r4_ = rq[:, sl, :].rearrange("p (m a f) -> p m a f", m=4, a=2)
                        cosb = rope[:, tt, 0, :].unsqueeze(1).unsqueeze(1).to_broadcast([128, 4, 2, 64])
                        sinb = rope[:, tt, 1, :].unsqueeze(1).unsqueeze(1).to_broadcast([128, 4, 2, 64])
                        T.op('pool', lambda e: e.tensor_tensor(out=c4_, in0=x4, in1=cosb, op=ALU.mult),
                             reads=[('xq', sl), 'rope'], writes=[('XC', sl)])
                        T.op('pool', lambda e: e.tensor_tensor(out=s4_, in0=x4, in1=sinb, op=ALU.mult),
                             reads=[('xq', sl), 'rope'], writes=[('XS', sl)])
                        T.op('dve', lambda e: e.tensor_tensor(out=r4_[:, :, 0, :], in0=c4_[:, :, 0, :], in1=s4_[:, :, 1, :], op=ALU.subtract),
                             reads=[('XC', sl), ('XS', sl)], writes=[('rq', sl)])
                        T.op('dve', lambda e: e.tensor_tensor(out=r4_[:, :, 1, :], in0=c4_[:, :, 1, :], in1=s4_[:, :, 0, :], op=ALU.add),
                             reads=[('XC', sl), ('XS', sl)], writes=[('rq', sl)])

                        def trq(e):
                            for m in range(4):
                                ins = e.transpose(out=ptq[:, sl, m, :], in_=rq[:, sl, m * 128:(m + 1) * 128], identity=ident[:])
                            return ins
                        T.op('pe', trq, reads=[('rq', sl), 'ident'], writes=[('ptq', sl)])
                        T.op('act', lambda e: e.activation(out=stq[:, wb, :, tt * 128:(tt + 1) * 128], in_=ptq[:, sl, :, :], func=AF.Copy,
                                                           scale=(QSCALE if blk < 4 else 1.0)),
                             reads=[('ptq', sl)], writes=[('stq', wb)])
                    else:
                        T.op('act', lambda e: e.copy(out=vst[:, wb, tt, :], in_=ppq[:, sl, :]), reads=[('ppq', sl)], writes=[('vst', wb)])
                if blk < 4:
                    for m in range(4):
                        T.dma('sp', QT_d[s, blk * 4 + m, :, t0:t0 + G], stq[:, wb, m, :], reads=[('stq', wb)],
                              writes=[('QT_d', s)], stream='o')
                elif blk < 8:
                    for m in range(4):
                        mp = (blk - 4) * 4 + m
                        T.dma('sp', kt_d[s][blk - 4].ap()[m * 128:(m + 1) * 128, t0:t0 + G], stq[:, wb, m, :], reads=[('stq', wb)],
                              writes=[('kt_d', s, blk - 4)], stream='o')
                else:
                    T.dma('sp', v_d[s][blk - 8].ap()[t0:t0 + G, :].rearrange("(tt p) c -> p tt c", p=128),
                          vst[:, wb, :, :], reads=[('vst', wb)], writes=[('v_d', s, blk - 8)], stream='o')
            T.barrier()

    def attention(s):
        nk = TS if s == 1 else NCORES * TS
        nkt = nk // 128
        with ExitStack() as es:
            Qh = es.enter_context(nc.sbuf_tensor(uname("Qh"), [128, 2, TS], BF16))
            Kh = es.enter_context(nc.sbuf_tensor(uname("Kh"), [128, 2, nk], BF16))
            Vh = es.enter_context(nc.sbuf_tensor(uname("Vh"), [128, nkt, 257], BF16))
            pT = es.enter_context(nc.sbuf_tensor(uname("pT"), [128, 3, 512], BF16))
            A1 = es.enter_context(nc.sbuf_tensor(uname("A1"), [128, 16, 256], F32))
            tmpA = es.enter_context(nc.sbuf_tensor(uname("tmpA"), [128, 2, 256], F32))
            rz = es.enter_context(nc.sbuf_tensor(uname("rz"), [128, 8], F32))
            ssn = es.enter_context(nc.sbuf_tensor(uname("ssn"), [128, 8], F32))
            onq = es.enter_context(nc.sbuf_tensor(uname("onq"), [128, 2, 256], BF16))
            ost = es.enter_context(nc.sbuf_tensor(uname("ost"), [128, 2, TS], BF16))
            psc = es.enter_context(nc.psum_tensor(uname("psc"), [128, 3, 512], F32))
            pav = es.enter_context(nc.psum_tensor(uname("pav"), [128, 4, 512], F32))
            pto = es.enter_context(nc.psum_tensor(uname("pto"), [128, 2, 2, 128], BF16))
            T.op('dve', lambda e: e.memset(Vh[:, :, 256:257], 1.0), writes=[('Vh', 0), ('Vh', 1)])
            ksc = 0
            kz = 0
            nvh = 2 if s == 1 else NCORES
            for hd in range(8):
                for m in range(2):
                    T.dma('sp', Qh[:, m, :], QT_d[s, 2 * hd + m, :, :], reads=[('QT_d', s)], writes=[('Qh', m)], stream='a')
                for m in range(2):
                    mp = 2 * hd + m
                    if s == 1:
                        T.dma('sp', Kh[:, m, :], kt_d[1][mp // 4].ap()[(mp % 4) * 128:(mp % 4 + 1) * 128, :], reads=[('kt_d', 1, mp // 4)],
                              writes=[('Kh', m, 0), ('Kh', m, 1)], stream='a')
                    else:
                        for rk in range(NCORES):
                            T.dma('sp', Kh[:, m, rk * TS:(rk + 1) * TS],
                                  KTg[mp // 4].ap()[rk * 512 + (mp % 4) * 128:rk * 512 + (mp % 4 + 1) * 128, :], reads=[('KTg', mp // 4)],
                                  writes=[('Kh', m, rk // 4)], stream='a')
                if s == 1:
                    T.dma('sp', Vh[:, :, 0:256], v_d[1][hd // 2].ap()[:, (hd % 2) * 256:(hd % 2 + 1) * 256].rearrange("(kt p) e -> p kt e", p=128),
                          reads=[('v_d', 1, hd // 2)], writes=[('Vh', 0), ('Vh', 1)], stream='a')
                else:
                    for rk in range(NCORES):
                        T.dma('sp', Vh[:, rk * NT:(rk + 1) * NT, 0:256],
                              Vg[hd // 2].ap()[rk * TS:(rk + 1) * TS, (hd % 2) * 256:(hd % 2 + 1) * 256].rearrange("(kt p) e -> p kt e", p=128),
                              reads=[('Vg', hd // 2)], writes=[('Vh', rk // 4)], stream='a')
                items = [(m, qg, kt) for m in range(2) for qg in range(4) for kt in range(nkt)]
                slots_of = {}

                def emit_qk(idx):
                    nonlocal ksc
                    m, qg, kt = items[idx]
                    half = (kt * 2) // nkt
                    sl = ksc % 3
                    ksc += 1
                    slots_of[idx] = sl
                    T.op('pe', lambda e: e.matmul(psc[:, sl, :], lhsT=Kh[:, m, kt * 128:(kt + 1) * 128],
                                                  rhs=Qh[:, m, qg * 512:(qg + 1) * 512], start=True, stop=True),
                         reads=[('Kh', m, half), ('Qh', m)], writes=[('psc', sl)])
                    T.op('act', lambda e: e.activation(out=pT[:, sl, :], in_=psc[:, sl, :], func=AF.Exp, bias=ebias[:]),
                         reads=[('psc', sl), 'ebias'], writes=[('pT', sl)])

                def emit_av(idx):
                    m, qg, kt = items[idx]
                    half = (kt * 2) // nkt
                    sl = slots_of.pop(idx)

                    def av(e):
                        for qt in range(4):
                            ins = e.matmul(pav[:, qt, 0:257], lhsT=pT[:, sl, qt * 128:(qt + 1) * 128], rhs=Vh[:, kt, :],
                                           start=(kt == 0), stop=(kt == nkt - 1))
                        return ins
                    T.op('pe', av, reads=[('pT', sl), ('Vh', half)], writes=['pav'])

                def emit_evac(m, qg):
                    nonlocal kz
                    for qt in range(4):
                        zi = kz % 8
                        kz += 1
                        tq = qg * 4 + qt
                        T.op('dve', lambda e: e.reciprocal(out=rz[:, zi:zi + 1], in_=pav[:, qt, 256:257]), reads=['pav'], writes=[('rz', zi)])
                        if m == 0:
                            T.op('dve', lambda e: e.tensor_scalar(out=A1[:, tq, :], in0=pav[:, qt, 0:256], scalar1=rz[:, zi:zi + 1],
                                                                  scalar2=None, op0=ALU.mult),
                                 reads=['pav', ('rz', zi)], writes=[('A1', tq)])
                        else:
                            tb = qt % 2
                            T.op('dve', lambda e: e.tensor_scalar(out=tmpA[:, tb, :], in0=pav[:, qt, 0:256], scalar1=rz[:, zi:zi + 1],
                                                                  scalar2=neglam[:, 0:1], op0=ALU.mult, op1=ALU.mult),
                                 reads=['pav', ('rz', zi), 'neglam'], writes=[('tmpA', tb)])
                            T.op('dve', lambda e: e.tensor_tensor(out=tmpA[:, tb, :], in0=tmpA[:, tb, :], in1=A1[:, tq, :], op=ALU.add),
                                 reads=[('tmpA', tb), ('A1', tq)], writes=[('tmpA', tb)])
                            T.op('act', lambda e: e.activation(out=onq[:, tb, :], in_=tmpA[:, tb, :], func=AF.Square,
                                                               accum_out=ssn[:, zi:zi + 1]),
                                 reads=[('tmpA', tb)], writes=[('onq', tb), ('ssn', zi)])
                            T.op('act', lambda e: e.activation(out=ssn[:, zi:zi + 1], in_=ssn[:, zi:zi + 1], func=AF.Ln,
                                                               scale=1.0 / 256, bias=epsb[:]),
                                 reads=[('ssn', zi), 'epsb'], writes=[('ssn', zi)])
                            T.op('act', lambda e: e.activation(out=ssn[:, zi:zi + 1], in_=ssn[:, zi:zi + 1], func=AF.Exp, scale=-0.5),
                                 reads=[('ssn', zi)], writes=[('ssn', zi)])
                            T.op('dve', lambda e: e.scalar_tensor_tensor(out=onq[:, tb, :], in0=tmpA[:, tb, :], scalar=ssn[:, zi:zi + 1],
                                                                         in1=swl[:], op0=ALU.mult, op1=ALU.mult),
                                 reads=[('tmpA', tb), ('ssn', zi), 'swl'], writes=[('onq', tb)])

                            def tro(e):
                                for c2 in range(2):
                                    ins = e.transpose(out=pto[:, tb, c2, :], in_=onq[:, tb, c2 * 128:(c2 + 1) * 128], identity=ident[:])
                                return ins
                            T.op('pe', tro, reads=[('onq', tb), 'ident'], writes=[('pto', tb)])
                            T.op('act', lambda e: e.copy(out=ost[:, :, tq * 128:(tq + 1) * 128], in_=pto[:, tb, :, :]),
                                 reads=[('pto', tb)], writes=['ost'])

                emit_qk(0)
                for idx in range(len(items)):
                    if idx + 1 < len(items):
                        emit_qk(idx + 1)
                    emit_av(idx)
                    m_, qg_, kt_ = items[idx]
                    if kt_ == nkt - 1:
                        emit_evac(m_, qg_)
                for c2 in range(2):
                    T.dma('sp', onT_d[s, 2 * hd + c2], ost[:, c2, :], reads=['ost'], writes=[('onT_d', s, 2 * hd + c2)], stream='o')
            T.barrier()

    def final_after(s, g, t0, yacc, h2T):
        with ExitStack() as es:
            gamf = es.enter_context(nc.sbuf_tensor(uname("gamf"), [128, D], F32))
            ssf = es.enter_context(nc.sbuf_tensor(uname("ssf"), [128, GT], F32))
            junk = es.enter_context(nc.sbuf_tensor(uname("junk"), [128, 2, D], BF16))
            T.dma('sp', gamf[:], norm_final.partition_broadcast(128), writes=['gamf'], stream='c')
            for tt in range(GT):
                b = tt % 2
                T.op('act', lambda e: e.activation(out=junk[:, b, :], in_=yacc[:, tt, :], func=AF.Square,
                                                   accum_out=ssf[:, tt:tt + 1]),
                     reads=[('yacc', tt)], writes=[('junk', b), ('ssf', tt)])
                T.op('act', lambda e: e.activation(out=ssf[:, tt:tt + 1], in_=ssf[:, tt:tt + 1], func=AF.Ln,
                                                   scale=1.0 / D, bias=epsb[:]),
                     reads=[('ssf', tt), 'epsb'], writes=[('ssf', tt)])
                T.op('act', lambda e: e.activation(out=ssf[:, tt:tt + 1], in_=ssf[:, tt:tt + 1], func=AF.Exp, scale=-0.5),
                     reads=[('ssf', tt)], writes=[('ssf', tt)])
                T.op('dve', lambda e: e.scalar_tensor_tensor(out=yacc[:, tt, :], in0=yacc[:, tt, :], scalar=ssf[:, tt:tt + 1],
                                                             in1=gamf[:], op0=ALU.mult, op1=ALU.mult),
                     reads=[('yacc', tt), ('ssf', tt), 'gamf'], writes=[('yacc', tt)])
                T.dma('sp', y_out[s, t0 + tt * 128:t0 + (tt + 1) * 128, :], yacc[:, tt, :], reads=[('yacc', tt)],
                      writes=[('y', s, g, tt)], stream='y')
            T.barrier()

    def gather(src_t, dst_t, reads, writes):
        if fake_gather:
            n = src_t.ap().shape[0]
            for rk in range(NCORES):
                T.dma('sp', dst_t.ap()[rk * n:(rk + 1) * n, :], src_t.ap(), reads=reads, writes=writes, stream='g')
            return
        T._waits('pool', T._deps(reads, writes))
        sem = T._new('cc')
        ins = nc.gpsimd.collective_compute("AllGather", ALU.bypass, replica_groups=[list(range(NCORES))],
                                           ins=[src_t.ap().opt()], outs=[dst_t.ap().opt()])
        ins.then_inc(sem)
        T.ninst += 1
        T._record((sem, 1, 'cc'), reads, writes)

    def l0_fix():
        with ExitStack() as es:
            Ob = es.enter_context(nc.sbuf_tensor(uname("fOb"), [128, TS], F32))
            gsb = es.enter_context(nc.sbuf_tensor(uname("fgs"), [128, TS], BF16))
            qeb = es.enter_context(nc.sbuf_tensor(uname("fqeb"), [128, 2, TS], BF16))
            ke = es.enter_context(nc.sbuf_tensor(uname("fke"), [128, TS], BF16))
            E1 = es.enter_context(nc.sbuf_tensor(uname("fE1"), [128, TS], F32))
            kdT = es.enter_context(nc.sbuf_tensor(uname("fkdT"), [128, TS], BF16))
            Sg = es.enter_context(nc.sbuf_tensor(uname("fSg"), [128, 2, NCORES, 128], F32))
            DGt = es.enter_context(nc.sbuf_tensor(uname("fDGt"), [128, NCORES, 2 * KC], F32))
            acoef = es.enter_context(nc.sbuf_tensor(uname("facoef"), [128, 2, NCORES], F32))
            acc = es.enter_context(nc.sbuf_tensor(uname("facc"), [128, 2, 128], F32))
            accb = es.enter_context(nc.sbuf_tensor(uname("faccb"), [128, 2, 128], BF16))
            cm = es.enter_context(nc.sbuf_tensor(uname("fcm"), [128, 2, NCORES], F32))
            pp = es.enter_context(nc.psum_tensor(uname("fpp"), [128, 2, 512], F32))
            pc = es.enter_context(nc.psum_tensor(uname("fpc"), [128, 2, 512], F32))
            st = {'pp': 0}
            kpc = 0
            oall = [('O', tt) for tt in range(NT)]
            T.dma('sp', cm[:], c_cmask, writes=['cm'], stream='c')
            SGv = SG.ap().rearrange("(k x) e -> x k e", k=NCORES)
            T.dma('sp', DGt[:], DG.ap().rearrange("(k d) c -> d k c", k=NCORES)[:, :, 0:2 * KC], reads=['DG'], writes=['DGt'], stream='c')
            for h in range(KC):
                T.dma('sp', Ob[:], Osave_d[h], reads=[('Osave_d', h)], writes=oall, stream='x')
                T.dma('sp', gsb[:], gs_d[h], reads=[('gs_d', h)], writes=['gs'], stream='x')
                for r in range(2):
                    T.dma('sp', qeb[:, r, :], qeb_d[h, r], reads=[('qeb_d', h)], writes=[('qeb', r)], stream='x')
                    T.dma('sp', Sg[:, r, :, :], SGv[(h * 2 + r) * 128:(h * 2 + r + 1) * 128, :, :],
                          reads=['SG'], writes=[('Sg', r)], stream='x')
                    col = r * KC + h
                    T.op('dve', lambda e: e.tensor_scalar(out=acoef[:, r, :], in0=DGt[:, :, col], scalar1=-1.0, scalar2=None, op0=ALU.add),
                         reads=['DGt'], writes=[('acoef', r)])
                    T.op('dve', lambda e: e.tensor_tensor(out=acoef[:, r, :], in0=acoef[:, r, :], in1=cm[:, r, :], op=ALU.mult),
                         reads=[('acoef', r), 'cm'], writes=[('acoef', r)])
                    T.op('dve', lambda e: e.tensor_scalar(out=acoef[:, r, :], in0=acoef[:, r, :], scalar1=1.0, scalar2=None, op0=ALU.add),
                         reads=[('acoef', r)], writes=[('acoef', r)])
                    T.op('dve', lambda e: e.tensor_tensor(out=Sg[:, r, :, :], in0=Sg[:, r, :, :],
                                                          in1=cm[:, r, :].unsqueeze(2).to_broadcast([128, NCORES, 128]), op=ALU.mult),
                         reads=[('Sg', r), 'cm'], writes=[('Sg', r)])
                    order = list(range(NCORES)) if r == 0 else list(range(NCORES - 1, -1, -1))
                    for idx, cp in enumerate(order):
                        if idx == 0:
                            T.op('dve', lambda e: e.tensor_copy(out=acc[:, r, :], in_=Sg[:, r, cp, :]), reads=[('Sg', r)], writes=[('acc', r)])
                        else:
                            T.op('dve', lambda e: e.scalar_tensor_tensor(out=acc[:, r, :], in0=acc[:, r, :], scalar=acoef[:, r, cp:cp + 1],
                                                                         in1=Sg[:, r, cp, :], op0=ALU.mult, op1=ALU.add),
                                 reads=[('Sg', r), ('acc', r), ('acoef', r)], writes=[('acc', r)])
                    T.op('act', lambda e: e.copy(out=accb[:, r, :], in_=acc[:, r, :]), reads=[('acc', r)], writes=[('accb', r)])
                    for tg in range(4):
                        sl = kpc % 2
                        kpc += 1
                        T.op('pe', lambda e: e.matmul(pc[:, sl, :], lhsT=accb[:, r, :], rhs=qeb[:, r, tg * 512:(tg + 1) * 512],
                                                      start=True, stop=True),
                             reads=[('accb', r), ('qeb', r)], writes=[('pc', sl)])
                        T.op('dve', lambda e: e.tensor_tensor(out=Ob[:, tg * 512:(tg + 1) * 512], in0=Ob[:, tg * 512:(tg + 1) * 512],
                                                              in1=pc[:, sl, :], op=ALU.add),
                             reads=[('pc', sl)] + oall, writes=oall)
                finalize_head(0, h, Ob, gsb, ke, E1, kdT, pp, st)
            T.barrier()

    if slots is None:
        slots = [1] if debug else [0, 1]
    exch = (0 in slots) and not no_exchange
    if 0 in slots:
        l0_mixer(0)
        if exch:
            T.barrier()
            gather(Send_d, SG, ['Send_d'], ['SG'])
            gather(Dtot_d, DG, ['Dtot_d'], ['DG'])
            T.barrier(['pool'])
    if 1 in slots:
        l0_mixer(1)
    if exch:
        l0_fix()
    for s in slots:
        if stage >= 1:
            post_mixer(s, 0, l0_after)
        if s == 0 and stage >= 3:
            T.barrier()
            for i in range(4):
                gather(kt_d[0][i], KTg[i], [('kt_d', 0, i)], [('KTg', i)])
                gather(v_d[0][i], Vg[i], [('v_d', 0, i)], [('Vg', i)])
            T.barrier(['pool'])
    if stage >= 3:
        for s in reversed(slots):
            attention(s)
            post_mixer(s, 1, final_after)
    T.final_wait('sp')
    return nc, T


def host_consts():
    ident = np.eye(128, dtype=np.float32)
    p = np.arange(128) % 32
    t = np.arange(32)
    m = np.zeros((128, 2, 32), np.float32)
    m[:, 0, :] = (p[:, None] <= t[None, :])
    m[:, 1, :] = (p[:, None] >= t[None, :])
    return ident, m


def make_in_maps(inputs):
    ident, m = host_consts()
    xp = np.asarray(inputs['x_prompt'], np.float32)
    xs = np.asarray(inputs['x_sample'], np.float32)
    inv_freq = (10000.0 ** (-np.arange(0, 128, 2, dtype=np.float32) / 128)).astype(np.float32)
    maps = []
    shared = {k: np.ascontiguousarray(np.asarray(v, np.float32)) for k, v in inputs.items()
              if k not in ('x_prompt', 'x_sample')}
    for c in range(NCORES):
        x = np.stack([xp[0, c * TS:(c + 1) * TS, :], xs[c]], axis=0)
        rope = np.zeros((2, TS, 2, 64), np.float32)
        for slot, pos0 in ((0, c * TS), (1, 0)):
            ang = (np.arange(pos0, pos0 + TS, dtype=np.float32)[:, None] * inv_freq[None, :]).astype(np.float32)
            rope[slot, :, 0, :] = np.cos(ang)
            rope[slot, :, 1, :] = np.sin(ang)
        cm = np.zeros((128, 2, NCORES), np.float32)
        cm[:, 0, :c] = 1.0
        cm[:, 1, c + 1:] = 1.0
        d = dict(shared)
        d.update({'x': np.ascontiguousarray(x), 'c_ident': ident, 'c_masks': m, 'c_rope': rope, 'c_cmask': cm})
        maps.append(d)
    return maps


_NC_CACHE = {}


def kernel(**inputs):
    if 'nc' not in _NC_CACHE:
        _NC_CACHE['nc'] = build_nc()[0]
    nc = _NC_CACHE['nc']
    maps = make_in_maps(inputs)
    res = run_bass_kernel_spmd(nc, maps, core_ids=list(range(NCORES)))
    ys = [np.asarray(r['y'], dtype=np.float32) for r in res.results]
    y_prompt = np.concatenate([y[0] for y in ys], axis=0)[None]
    y_sample = np.stack([y[1] for y in ys], axis=0)
    return (y_prompt, y_sample)
```

```python
import math
from contextlib import ExitStack
import numpy as np
import concourse.bass as bass
import concourse.mybir as mybir
from concourse.bass_utils import run_bass_kernel_spmd

F32 = mybir.dt.float32
BF16 = mybir.dt.bfloat16
AF = mybir.ActivationFunctionType
ALU = mybir.AluOpType

NCORES = 8
D = 2048
KC = 16
TS = 2048
NT = TS // 128
NSTEP = TS // 32
DFF = 5632
NJ = DFF // 128
EPS = 1e-5
LAMBDA_INIT = 0.8 - 0.6 * math.exp(-0.3 * 1)
QSCALE = 128 ** -0.5


class Tracker:
    def __init__(self, nc):
        self.nc = nc
        self.eng = {'pe': nc.tensor, 'act': nc.scalar, 'dve': nc.vector, 'pool': nc.gpsimd, 'sp': nc.sync}
        self.sems = {e: None for e in self.eng}
        self.cnt = {e: 0 for e in self.eng}
        self.nsem = 0
        self.waited = {}
        self.lastw = {}
        self.readers = {}
        self.dsem = {}
        self.latest = {}
        self.ninst = 0

    def _new(self, tag):
        while True:
            self.nsem += 1
            h = self.nc.alloc_semaphore(f"{tag}_{self.nsem}")
            if not (160 <= h.num <= 199):
                return h

    def _deps(self, reads, writes):
        deps = []
        for r in reads:
            d = self.lastw.get(r)
            if d is not None:
                deps.append(d)
        for w in writes:
            d = self.lastw.get(w)
            if d is not None:
                deps.append(d)
            deps.extend(self.readers.get(w, {}).values())
        return deps

    def _waits(self, e, deps):
        best = {}
        for (sem, val, src) in deps:
            if src == 'pe' and e == 'pe':
                continue
            k = sem.name
            if val > best.get(k, (None, 0))[1]:
                best[k] = (sem, val)
        for k, (sem, val) in best.items():
            if self.waited.get((e, k), 0) >= val:
                continue
            self.waited[(e, k)] = val
            self.eng[e].wait_ge(sem, val)
            self.ninst += 1

    def _record(self, d, reads, writes):
        for r in reads:
            self.readers.setdefault(r, {})[d[0].name] = d
        for w in writes:
            self.lastw[w] = d
            self.readers[w] = {}
        self.latest[d[0].name] = (d[0], d[1])

    def op(self, e, fn, reads=(), writes=()):
        self._waits(e, self._deps(reads, writes))
        if self.sems[e] is None or self.cnt[e] >= 60000:
            self.sems[e] = self._new("s" + e)
            self.cnt[e] = 0
        ins = fn(self.eng[e])
        self.cnt[e] += 1
        ins.then_inc(self.sems[e], 1)
        self.ninst += 1
        self._record((self.sems[e], self.cnt[e], e), reads, writes)

    def dma(self, q, out, in_, reads=(), writes=(), stream="d", **kw):
        RING = 4
        st = self.dsem.setdefault(stream, {'i': 0, 'slots': [None] * RING})
        i = st['i'] % RING
        st['i'] += 1
        slot = st['slots'][i]
        deps = self._deps(reads, writes)
        if slot is not None:
            deps.append((slot[0], slot[1], 'dma'))
        self._waits(q, deps)
        if slot is None or slot[1] + 16 > 60000:
            slot = [self._new("d" + stream), 0]
            st['slots'][i] = slot
        ins = self.eng[q].dma_start(out=out, in_=in_, **kw)
        slot[1] += 16
        ins.then_inc(slot[0], 16)
        self.ninst += 1
        self._record((slot[0], slot[1], 'dma'), reads, writes)

    def barrier(self, engines=None):
        for e in (engines or self.eng):
            deps = [(sem, val, 'x') for (sem, val) in self.latest.values()]
            self._waits(e, deps)

    def final_wait(self, e='sp'):
        deps = [(sem, val, 'x') for (sem, val) in self.latest.values()]
        self._waits(e, deps)


def build_nc(stage=99, debug=False, fake_gather=False, slots=None, no_exchange=False):
    nc = bass.Bass("TRN2", target_bir_lowering=False)
    T = Tracker(nc)
    uid = [0]

    def uname(n):
        uid[0] += 1
        return f"{n}_{uid[0]}"

    def din(name, shape, dt=F32):
        return nc.dram_tensor(name, list(shape), dt, kind="ExternalInput").ap()

    x_in = din("x", [2, TS, D])
    norm_mixer = din("norm_mixer", [2, D])
    norm_ffn = din("norm_ffn", [2, D])
    norm_final = din("norm_final", [D])
    hgrn_w_in = din("hgrn_w_in", [1, D, 5 * D])
    hgrn_lb = din("hgrn_lower_bound", [2, 3, D])
    hgrn_gnorm = din("hgrn_gnorm", [1, 128])
    hgrn_w_out = din("hgrn_w_out", [1, D, D])
    diff_w_in = din("diff_w_in", [1, D, 3 * D])
    lq1 = din("diff_lambda_q1", [1, 128])
    lk1 = din("diff_lambda_k1", [1, 128])
    lq2 = din("diff_lambda_q2", [1, 128])
    lk2 = din("diff_lambda_k2", [1, 128])
    diff_subln = din("diff_subln", [1, 256])
    diff_w_out = din("diff_w_out", [1, D, D])
    ffn_w_gate = din("ffn_w_gate", [2, D, DFF])
    ffn_w_up = din("ffn_w_up", [2, D, DFF])
    ffn_w_down = din("ffn_w_down", [2, DFF, D])
    c_ident = din("c_ident", [128, 128])
    c_masks = din("c_masks", [128, 2, 32])
    c_rope = din("c_rope", [2, TS, 2, 64])
    c_cmask = din("c_cmask", [128, 2, NCORES])

    y_out = nc.dram_tensor("y", [2, TS, D], F32, kind="ExternalOutput").ap()

    onT_d = nc.dram_tensor("onT_d", [2, KC, 128, TS], BF16).ap()
    dbg = {}
    if debug:
        dbg['onT'] = nc.dram_tensor("dbg_onT", [2, KC, 128, TS], BF16, kind="ExternalOutput").ap()

    ident = nc.alloc_sbuf_tensor("ident", [128, 128], BF16)
    ones_bf = nc.alloc_sbuf_tensor("ones_bf", [128, 128], BF16)
    masks = nc.alloc_sbuf_tensor("masks", [128, 2, 32], F32)
    lbt = nc.alloc_sbuf_tensor("lbt", [128, 2, 16, 3], F32)
    lb0 = nc.alloc_sbuf_tensor("lb0", [128, 2, 16], F32)
    oml = nc.alloc_sbuf_tensor("oml", [128, 2, 16], F32)
    gw = nc.alloc_sbuf_tensor("gw", [128, 1], F32)
    epsb = nc.alloc_sbuf_tensor("epsb", [128, 1], F32)

    T.dma('pool', ident[:], c_ident, writes=['ident'], stream='ci')
    T.dma('sp', masks[:], c_masks, writes=['masks'], stream='c')
    T.op('dve', lambda e: e.memset(ones_bf[:], 1.0), writes=['ones'])
    T.op('dve', lambda e: e.memset(epsb[:], EPS), writes=['epsb'])
    for r in range(2):
        for l in range(3):
            T.dma('sp', lbt[:, r, :, l], hgrn_lb[r, l, :].rearrange("(h d) -> d h", d=128), writes=['lbt'], stream='c',
                  allow_slow_non_contiguous=True)
    T.dma('sp', gw[:], hgrn_gnorm[0, :].rearrange("(d o) -> d o", o=1), writes=['gw'], stream='c',
          allow_slow_non_contiguous=True)
    T.op('act', lambda e: e.activation(out=lbt[:], in_=lbt[:], func=AF.Exp), reads=['lbt'], writes=['lbt'])
    T.op('dve', lambda e: e.tensor_reduce(out=oml[:], in_=lbt[:], axis=mybir.AxisListType.X, op=ALU.add), reads=['lbt'], writes=['oml'])
    T.op('dve', lambda e: e.reciprocal(out=oml[:], in_=oml[:]), reads=['oml'], writes=['oml'])
    T.op('dve', lambda e: e.tensor_tensor(out=lb0[:], in0=lbt[:, :, :, 0], in1=oml[:], op=ALU.mult), reads=['lbt', 'oml'], writes=['lb0'])
    T.op('dve', lambda e: e.tensor_scalar(out=oml[:], in0=lb0[:], scalar1=-1.0, scalar2=1.0, op0=ALU.mult, op1=ALU.add), reads=['lb0'], writes=['oml'])

    zeros = nc.alloc_sbuf_tensor("zeros", [128, 32], F32)
    T.op('dve', lambda e: e.memset(zeros[:], 0.0), writes=['zeros'])

    def finalize_head(s, h, Ob, gs, ke, E1, kdT, pp, st):
        oall = [('O', tt) for tt in range(NT)]
        T.op('act', lambda e: e.activation(out=ke[:], in_=Ob[:], func=AF.Square), reads=oall, writes=['ke'])
        for tg in range(4):
            pb = st['pp'] % 2
            st['pp'] += 1
            T.op('pe', lambda e: e.matmul(pp[:, pb, :], lhsT=ones_bf[:], rhs=ke[:, tg * 512:(tg + 1) * 512],
                                          start=True, stop=True),
                 reads=['ke', 'ones'], writes=[('pp', pb)])
            T.op('act', lambda e: e.activation(out=E1[:, tg * 512:(tg + 1) * 512], in_=pp[:, pb, :], func=AF.Ln,
                                               scale=1.0 / 128, bias=epsb[:]),
                 reads=[('pp', pb), 'epsb'], writes=['E1'])
        T.op('act', lambda e: e.activation(out=E1[:], in_=E1[:], func=AF.Exp, scale=-0.5), reads=['E1'], writes=['E1'])
        T.op('dve', lambda e: e.scalar_tensor_tensor(out=Ob[:], in0=Ob[:], scalar=gw[:, 0:1], in1=E1[:],
                                                     op0=ALU.mult, op1=ALU.mult),
             reads=oall + ['E1', 'gw'], writes=oall)
        T.op('pool', lambda e: e.tensor_tensor(out=kdT[:], in0=Ob[:], in1=gs[:], op=ALU.mult),
             reads=oall + ['gs'], writes=['kdT'])
        T.dma('sp', onT_d[s, h], kdT[:], reads=['kdT'], writes=[('onT_d', s, h)], stream='o')
        if debug:
            T.dma('sp', dbg['onT'][s, h], kdT[:], reads=['kdT'], writes=[('dbg_onT', s, h)], stream='o')

    Osave_d = nc.dram_tensor("Osave_d", [KC, 128, TS], F32).ap()
    gs_d = nc.dram_tensor("gs_d", [KC, 128, TS], BF16).ap()
    qeb_d = nc.dram_tensor("qeb_d", [KC, 2, 128, TS], BF16).ap()
    Send_d = nc.dram_tensor("Send_d", [KC * 2 * 128, 128], F32)
    SG = nc.dram_tensor("SG", [NCORES * KC * 2 * 128, 128], F32)
    Dtot_d = nc.dram_tensor("Dtot_d", [128, 256], F32)
    DG = nc.dram_tensor("DG", [NCORES * 128, 256], F32)
    Dtot_sb = nc.alloc_sbuf_tensor("Dtot_sb", [128, 256], F32)
    T.op('dve', lambda e: e.memset(Dtot_sb[:], 0.0), writes=['Dtot_sb'])
    zeros64 = nc.alloc_sbuf_tensor("zeros64", [128, NSTEP], F32)
    T.op('dve', lambda e: e.memset(zeros64[:], 0.0), writes=['zeros64'])

    P2_FREE = []

    def l0_mixer(s):
        defer = (s == 0) and not no_exchange
        with nc.sbuf_tensor(uname("hnT"), [128, KC, TS], BF16) as hnT:
            with ExitStack() as es:
                gam = es.enter_context(nc.sbuf_tensor(uname("gam"), [128, D], F32))
                xt = es.enter_context(nc.sbuf_tensor(uname("xt"), [128, 2, D], F32))
                hn = es.enter_context(nc.sbuf_tensor(uname("hn"), [128, 2, D], BF16))
                ss = es.enter_context(nc.sbuf_tensor(uname("ss"), [128, NT], F32))
                pst = es.enter_context(nc.psum_tensor(uname("pst"), [128, 2, KC, 128], BF16))
                T.dma('sp', gam[:], norm_mixer[0, :].partition_broadcast(128), writes=['gam'], stream='c')
                for tt in range(NT):
                    b = tt % 2
                    T.dma('sp', xt[:, b, :], x_in[s, tt * 128:(tt + 1) * 128, :], writes=[('xt', b)], stream='x')
                    T.op('act', lambda e: e.activation(out=hn[:, b, :], in_=xt[:, b, :], func=AF.Square,
                                                       accum_out=ss[:, tt:tt + 1]),
                         reads=[('xt', b)], writes=[('hn', b), ('ss', tt)])
                    T.op('act', lambda e: e.activation(out=ss[:, tt:tt + 1], in_=ss[:, tt:tt + 1], func=AF.Ln,
                                                       scale=1.0 / D, bias=epsb[:]),
                         reads=[('ss', tt), 'epsb'], writes=[('ss', tt)])
                    T.op('act', lambda e: e.activation(out=ss[:, tt:tt + 1], in_=ss[:, tt:tt + 1], func=AF.Exp, scale=-0.5),
                         reads=[('ss', tt)], writes=[('ss', tt)])
                    T.op('dve', lambda e: e.scalar_tensor_tensor(out=hn[:, b, :], in0=xt[:, b, :], scalar=ss[:, tt:tt + 1],
                                                                 in1=gam[:], op0=ALU.mult, op1=ALU.mult),
                         reads=[('xt', b), ('ss', tt), 'gam'], writes=[('hn', b)])

                    def tr(e):
                        for kc in range(KC):
                            ins = e.transpose(out=pst[:, b, kc, :], in_=hn[:, b, kc * 128:(kc + 1) * 128], identity=ident[:])
                        return ins
                    T.op('pe', tr, reads=[('hn', b), 'ident'], writes=[('pst', b)])
                    T.op('act', lambda e: e.copy(out=hnT[:, :, tt * 128:(tt + 1) * 128], in_=pst[:, b, :, :]),
                         reads=[('pst', b)], writes=['hnT'])
                T.barrier()
            with ExitStack() as es:
                W = es.enter_context(nc.sbuf_tensor(uname("W"), [128, 5, KC, 128], BF16))
                qs = es.enter_context(nc.sbuf_tensor(uname("qs"), [128, TS], F32))
                gs = es.enter_context(nc.sbuf_tensor(uname("gs"), [128, TS], BF16))
                vtm = es.enter_context(nc.sbuf_tensor(uname("vtm"), [128, NT, 128], BF16))
                Vbd = es.enter_context(nc.sbuf_tensor(uname("Vbd"), [128, NT, 4, 128], BF16))
                Fb = es.enter_context(nc.sbuf_tensor(uname("Fb"), [128, TS], F32))
                Fb2 = es.enter_context(nc.sbuf_tensor(uname("Fb2"), [128, TS], F32))
                LF = es.enter_context(nc.sbuf_tensor(uname("LF"), [128, TS], F32))
                Bb = es.enter_context(nc.sbuf_tensor(uname("Bb"), [128, TS], F32))
                E1 = es.enter_context(nc.sbuf_tensor(uname("E1"), [128, TS], F32))
                qe = es.enter_context(nc.sbuf_tensor(uname("qe"), [128, 2, TS], BF16))
                ke = es.enter_context(nc.sbuf_tensor(uname("ke"), [128, TS], BF16))
                kdT = es.enter_context(nc.sbuf_tensor(uname("kdT"), [128, TS], BF16))
                kdtm = es.enter_context(nc.sbuf_tensor(uname("kdtm"), [128, 2, NT, 128], BF16))
                scT = es.enter_context(nc.sbuf_tensor(uname("scT"), [128, 2, NT, 128], BF16))
                Ob = es.enter_context(nc.sbuf_tensor(uname("Ob"), [128, TS], F32))
                TOT = es.enter_context(nc.sbuf_tensor(uname("TOT"), [128, 2, NSTEP], F32))
                Dm = es.enter_context(nc.sbuf_tensor(uname("Dm"), [128, 2, NSTEP], F32))
                Sst = es.enter_context(nc.sbuf_tensor(uname("Sst"), [128, 2, 128], F32))
                Sbf = es.enter_context(nc.sbuf_tensor(uname("Sbf"), [128, 2, 2, 128], BF16))
                pp = es.enter_context(nc.psum_tensor(uname("pp"), [128, 2, 512], F32))
                pm = es.enter_context(nc.psum_tensor(uname("pm"), [128, NT, 32], F32))
                ptr = es.enter_context(nc.psum_tensor(uname("ptr"), [128, 8, 128], BF16))
                pu = es.enter_context(nc.psum_tensor(uname("pu"), [128, 2, 4, 128], F32))
                po = es.enter_context(nc.psum_tensor(uname("po"), [128, 2, 4, 128], F32))
                st = {'pp': 0, 'pu': 0}
                P2_FREE.append(nc.sbuf_bytes_remaining)
                T.op('pool', lambda e: e.memset(Vbd[:], 0.0), writes=['Vbd'])
                T.op('pool', lambda e: e.memset(scT[:], 0.0), writes=[('scT', 0), ('scT', 1)])
                B3 = Bb[:].rearrange("p (n t) -> p n t", t=32)
                LF3 = LF[:].rearrange("p (n t) -> p n t", t=32)

                def load_w(h):
                    for blk in range(5):
                        c0 = blk * D + h * 128
                        T.dma('pool', W[:, blk, :, :],
                              hgrn_w_in[0, :, c0:c0 + 128].rearrange("(kc p) c -> p kc c", p=128),
                              writes=[('W', blk)], stream='w')

                def proj_fm(blk, func, dst, dname):
                    for tg in range(4):
                        pb = st['pp'] % 2
                        st['pp'] += 1

                        def mm(e):
                            for kc in range(KC):
                                ins = e.matmul(pp[:, pb, :], lhsT=W[:, blk, kc, :], rhs=hnT[:, kc, tg * 512:(tg + 1) * 512],
                                               start=(kc == 0), stop=(kc == KC - 1))
                            return ins
                        T.op('pe', mm, reads=[('W', blk)], writes=[('pp', pb)])
                        T.op('act', lambda e: e.activation(out=dst[:, tg * 512:(tg + 1) * 512], in_=pp[:, pb, :], func=func),
                             reads=[('pp', pb)], writes=[dname])

                def proj_v():
                    for t4 in range(4):
                        pb = st['pp'] % 2
                        st['pp'] += 1

                        def mm(e):
                            for j in range(4):
                                tt = t4 * 4 + j
                                for kc in range(KC):
                                    ins = e.matmul(pp[:, pb, j * 128:(j + 1) * 128], lhsT=hnT[:, kc, tt * 128:(tt + 1) * 128],
                                                   rhs=W[:, 3, kc, :], start=(kc == 0), stop=(kc == KC - 1))
                            return ins
                        T.op('pe', mm, reads=[('W', 3)], writes=[('pp', pb)])
                        T.op('act', lambda e: e.copy(out=vtm[:, t4 * 4:(t4 + 1) * 4, :],
                                                     in_=pp[:, pb, :].rearrange("p (j e) -> p j e", j=4)),
                             reads=[('pp', pb)], writes=['vtm'])
                    for j in range(4):
                        T.op('pool', lambda e: e.tensor_copy(out=Vbd[32 * j:32 * j + 32, :, j, :], in_=vtm[32 * j:32 * j + 32, :, :]),
                             reads=['vtm'], writes=['Vbd'])

                def prep(h, r):
                    Fd = Fb if r == 0 else Fb2
                    fname = ('F', r)
                    T.op('dve', lambda e: e.tensor_scalar(out=Fd[:], in0=Fd[:], scalar1=oml[:, r, h:h + 1], scalar2=lb0[:, r, h:h + 1],
                                                          op0=ALU.mult, op1=ALU.add),
                         reads=[fname, 'oml', 'lb0'], writes=[fname])
                    T.op('act', lambda e: e.activation(out=LF[:], in_=Fd[:], func=AF.Ln), reads=[fname], writes=['LF'])
                    T.op('pool', lambda e: e.tensor_scalar(out=Fd[:], in0=Fd[:], scalar1=-1.0, scalar2=1.0, op0=ALU.mult, op1=ALU.add),
                         reads=[fname], writes=[fname])

                    def scans(e):
                        for n in range(NSTEP):
                            ins = e.tensor_tensor_scan(out=Bb[:, n * 32:(n + 1) * 32], data0=zeros[:], data1=LF[:, n * 32:(n + 1) * 32],
                                                       initial=0.0, op0=ALU.add, op1=ALU.add)
                        return ins
                    T.op('dve', scans, reads=['LF', 'zeros'], writes=['B'])
                    T.op('dve', lambda e: e.tensor_copy(out=TOT[:, r, :], in_=B3[:, :, 31]), reads=['B'], writes=[('TOT', r)])
                    if r == 1:
                        T.op('dve', lambda e: e.tensor_tensor(out=LF[:], in0=LF[:], in1=Bb[:], op=ALU.subtract),
                             reads=['LF', 'B'], writes=['LF'])
                        T.op('dve', lambda e: e.tensor_tensor(out=B3, in0=LF3,
                                                              in1=TOT[:, r, :].unsqueeze(2).to_broadcast([128, NSTEP, 32]), op=ALU.add),
                             reads=['LF', ('TOT', r)], writes=['B'])
                    T.op('act', lambda e: e.activation(out=LF[:], in_=Bb[:], func=AF.Exp), reads=['B'], writes=['LF'])
                    T.op('dve', lambda e: e.scalar_tensor_tensor(out=qe[:, r, :], in0=qs[:], scalar=QSCALE, in1=LF[:],
                                                                 op0=ALU.mult, op1=ALU.mult),
                         reads=['qs', 'LF'], writes=[('qe', r)])
                    T.op('act', lambda e: e.activation(out=E1[:], in_=Bb[:], func=AF.Exp, scale=-1.0), reads=['B'], writes=['E1'])
                    T.op('pool', lambda e: e.tensor_tensor(out=ke[:], in0=Fd[:], in1=E1[:], op=ALU.mult),
                         reads=[fname, 'E1'], writes=['ke'])
                    T.op('dve', lambda e: e.tensor_tensor(out=B3, in0=TOT[:, r, :].unsqueeze(2).to_broadcast([128, NSTEP, 32]),
                                                          in1=B3, op=ALU.subtract),
                         reads=['B', ('TOT', r)], writes=['B'])
                    T.op('act', lambda e: e.activation(out=LF[:], in_=Bb[:], func=AF.Exp), reads=['B'], writes=['LF'])
                    T.op('pool', lambda e: e.tensor_tensor(out=kdT[:], in0=Fd[:], in1=LF[:], op=ALU.mult),
                         reads=[fname, 'LF'], writes=['kdT'])
                    T.op('act', lambda e: e.activation(out=Dm[:, r, :], in_=TOT[:, r, :], func=AF.Exp),
                         reads=[('TOT', r)], writes=[('Dm', r)])
                    for g8 in range(2):
                        def trk(e):
                            for q in range(8):
                                tt = g8 * 8 + q
                                ins = e.transpose(out=ptr[:, q, :], in_=kdT[:, tt * 128:(tt + 1) * 128], identity=ident[:])
                            return ins
                        T.op('pe', trk, reads=['kdT'], writes=['ptr'])
                        T.op('act', lambda e: e.copy(out=kdtm[:, r, g8 * 8:(g8 + 1) * 8, :], in_=ptr[:]),
                             reads=['ptr'], writes=[('kdtm', r)])

                    def scores(e):
                        for n in range(NSTEP):
                            tt, j = divmod(n, 4)
                            ins = e.matmul(pm[32 * j:32 * j + 32, tt, :], lhsT=ke[:, n * 32:(n + 1) * 32],
                                           rhs=qe[:, r, n * 32:(n + 1) * 32], start=True, stop=True, tile_position=(0, 32 * j))
                        return ins
                    T.op('pe', scores, reads=['ke', ('qe', r)], writes=['pm'])
                    for j in range(4):
                        T.op('dve', lambda e: e.tensor_tensor(out=scT[32 * j:32 * j + 32, r, :, 32 * j:32 * j + 32],
                                                              in0=pm[32 * j:32 * j + 32, :, :],
                                                              in1=masks[32 * j:32 * j + 32, r, :].unsqueeze(1).to_broadcast([32, NT, 32]),
                                                              op=ALU.mult),
                             reads=['pm', 'masks'], writes=[('scT', r)])

                INC = es.enter_context(nc.sbuf_tensor(uname("INC"), [128, 2, NSTEP], F32))
                EO = es.enter_context(nc.sbuf_tensor(uname("EO"), [128, 2, NSTEP], F32))

                def qeb_part(h, r):
                    T.op('dve', lambda e: e.tensor_tensor_scan(out=INC[:, r, :], data0=zeros64[:], data1=TOT[:, r, :], initial=0.0,
                                                               op0=ALU.add, op1=ALU.add),
                         reads=[('TOT', r), 'zeros64'], writes=[('INC', r)])
                    if r == 0:
                        T.op('dve', lambda e: e.tensor_tensor(out=EO[:, r, :], in0=INC[:, r, :], in1=TOT[:, r, :], op=ALU.subtract),
                             reads=[('INC', r), ('TOT', r)], writes=[('EO', r)])
                    else:
                        T.op('dve', lambda e: e.tensor_scalar(out=EO[:, r, :], in0=INC[:, r, :], scalar1=-1.0,
                                                              scalar2=INC[:, r, NSTEP - 1:NSTEP], op0=ALU.mult, op1=ALU.add),
                             reads=[('INC', r)], writes=[('EO', r)])
                    T.op('act', lambda e: e.activation(out=EO[:, r, :], in_=EO[:, r, :], func=AF.Exp), reads=[('EO', r)], writes=[('EO', r)])
                    T.op('act', lambda e: e.activation(out=Dtot_sb[:, r * KC + h:r * KC + h + 1], in_=INC[:, r, NSTEP - 1:NSTEP], func=AF.Exp),
                         reads=[('INC', r)], writes=['Dtot_sb'])
                    T.op('dve', lambda e: e.tensor_tensor(out=ke[:].rearrange("p (n t) -> p n t", t=32),
                                                          in0=qe[:, r, :].rearrange("p (n t) -> p n t", t=32),
                                                          in1=EO[:, r, :].unsqueeze(2).to_broadcast([128, NSTEP, 32]), op=ALU.mult),
                         reads=[('qe', r), ('EO', r)], writes=['ke'])
                    T.dma('sp', qeb_d[h, r], ke[:], reads=['ke'], writes=[('qeb_d', h)], stream='o')

                def recur(h):
                    owritten = set()
                    uslot = {}
                    for i in range(NSTEP):
                        for r in range(2):
                            n = i if r == 0 else NSTEP - 1 - i
                            tt, j = divmod(n, 4)
                            first_of_tile = (j == 0) if r == 0 else (j == 3)
                            last_of_tile = (j == 3) if r == 0 else (j == 0)
                            osl = tt % 4
                            if first_of_tile:
                                us = st['pu'] % 2
                                st['pu'] += 1
                                uslot[r] = us
                                T.op('pe', lambda e: e.matmul(pu[:, us, :, :], lhsT=kdtm[:, r, tt, :], rhs=Vbd[:, tt, :, :],
                                                              start=True, stop=True),
                                     reads=[('kdtm', r), 'Vbd'], writes=[('pu', us)])
                                T.op('pe', lambda e: e.matmul(po[:, r, osl, :], lhsT=vtm[:, tt, :], rhs=scT[:, r, tt, :],
                                                              start=True, stop=False),
                                     reads=['vtm', ('scT', r)], writes=[('po', r, osl)])
                            us = uslot[r]
                            if i > 0:
                                T.op('pe', lambda e: e.matmul(po[:, r, osl, j * 32:(j + 1) * 32], lhsT=Sbf[:, r, i % 2, :],
                                                              rhs=qe[:, r, n * 32:(n + 1) * 32], start=False, stop=True),
                                     reads=[('Sbf', r, i % 2), ('qe', r)], writes=[('po', r, osl)])
                            if i < NSTEP - 1 or defer:
                                if i == 0:
                                    T.op('dve', lambda e: e.tensor_copy(out=Sst[:, r, :], in_=pu[:, us, j, :]),
                                         reads=[('pu', us)], writes=[('S', r)])
                                else:
                                    T.op('dve', lambda e: e.scalar_tensor_tensor(out=Sst[:, r, :], in0=Sst[:, r, :],
                                                                                 scalar=Dm[:, r, n:n + 1], in1=pu[:, us, j, :],
                                                                                 op0=ALU.mult, op1=ALU.add),
                                         reads=[('pu', us), ('S', r), ('Dm', r)], writes=[('S', r)])
                                if i < NSTEP - 1:
                                    T.op('act', lambda e: e.copy(out=Sbf[:, r, (i + 1) % 2, :], in_=Sst[:, r, :]),
                                         reads=[('S', r)], writes=[('Sbf', r, (i + 1) % 2)])
                            if last_of_tile:
                                if tt not in owritten:
                                    owritten.add(tt)
                                    T.op('dve', lambda e: e.tensor_copy(out=Ob[:, tt * 128:(tt + 1) * 128], in_=po[:, r, osl, :]),
                                         reads=[('po', r, osl)], writes=[('O', tt)])
                                else:
                                    T.op('dve', lambda e: e.tensor_tensor(out=Ob[:, tt * 128:(tt + 1) * 128],
                                                                          in0=Ob[:, tt * 128:(tt + 1) * 128], in1=po[:, r, osl, :],
                                                                          op=ALU.add),
                                         reads=[('po', r, osl), ('O', tt)], writes=[('O', tt)])

                load_w(0)
                for h in range(KC):
                    proj_fm(0, AF.Silu, qs, 'qs')
                    proj_fm(4, AF.Silu, gs, 'gs')
                    proj_v()
                    proj_fm(1, AF.Sigmoid, Fb, ('F', 0))
                    proj_fm(2, AF.Sigmoid, Fb2, ('F', 1))
                    if h + 1 < KC:
                        load_w(h + 1)
                    prep(h, 0)
                    prep(h, 1)
                    if defer:
                        qeb_part(h, 0)
                        qeb_part(h, 1)
                    recur(h)
                    if defer:
                        oall = [('O', tt) for tt in range(NT)]
                        T.dma('sp', Osave_d[h], Ob[:], reads=oall, writes=[('Osave_d', h)], stream='o')
                        T.dma('sp', gs_d[h], gs[:], reads=['gs'], writes=[('gs_d', h)], stream='o')
                        for r in range(2):
                            T.dma('sp', Send_d.ap()[(h * 2 + r) * 128:(h * 2 + r + 1) * 128, :], Sst[:, r, :],
                                  reads=[('S', r)], writes=['Send_d'], stream='o')
                    else:
                        finalize_head(s, h, Ob, gs, ke, E1, kdT, pp, st)
                if defer:
                    T.dma('sp', Dtot_d.ap(), Dtot_sb[:], reads=['Dtot_sb'], writes=['Dtot_d'], stream='o')
                T.barrier()

    x1_d = nc.dram_tensor("x1_d", [2, TS, D], F32).ap()
    if debug:
        dbg['x1'] = nc.dram_tensor("dbg_x1", [2, TS, D], F32, kind="ExternalOutput").ap()
    G = 1024
    GT = G // 128

    def post_mixer(s, L, after_group=None):
        w_out_ap = hgrn_w_out[0] if L == 0 else diff_w_out[0]
        resid = x_in[s] if L == 0 else x1_d[s]
        for g in range(TS // G):
            t0 = g * G
            with ExitStack() as es0:
                yacc = es0.enter_context(nc.sbuf_tensor(uname("yacc"), [128, GT, D], F32))
                h2T = es0.enter_context(nc.sbuf_tensor(uname("h2T"), [128, KC, G], BF16))
                with ExitStack() as es:
                    onTg = es.enter_context(nc.sbuf_tensor(uname("onTg"), [128, KC, G], BF16))
                    wo = es.enter_context(nc.sbuf_tensor(uname("wo"), [128, 2, KC, 512], BF16))
                    hn2 = es.enter_context(nc.sbuf_tensor(uname("hn2"), [128, 2, D], BF16))
                    gam2 = es.enter_context(nc.sbuf_tensor(uname("gam2"), [128, D], F32))
                    ss2 = es.enter_context(nc.sbuf_tensor(uname("ss2"), [128, GT], F32))
                    ppa = es.enter_context(nc.psum_tensor(uname("ppa"), [128, 4, 512], F32))
                    pst2 = es.enter_context(nc.psum_tensor(uname("pst2"), [128, 2, KC, 128], BF16))
                    for tt in range(GT):
                        T.dma('sp', yacc[:, tt, :], resid[t0 + tt * 128:t0 + (tt + 1) * 128, :], writes=[('yacc', tt)], stream='x')
                    for kc in range(KC):
                        T.dma('sp', onTg[:, kc, :], onT_d[s, kc, :, t0:t0 + G], reads=[('onT_d', s, kc)], writes=['onTg'], stream='x')
                    T.dma('sp', gam2[:], norm_ffn[L, :].partition_broadcast(128), writes=['gam2'], stream='c')
                    k = 0
                    for cg in range(4):
                        wb = cg % 2
                        T.dma('pool', wo[:, wb, :, :], w_out_ap[:, cg * 512:(cg + 1) * 512].rearrange("(kc p) c -> p kc c", p=128),
                              writes=[('wo', wb)], stream='w')
                        for tt in range(GT):
                            slot = k % 4
                            k += 1

                            def mm(e):
                                for kc in range(KC):
                                    ins = e.matmul(ppa[:, slot, :], lhsT=onTg[:, kc, tt * 128:(tt + 1) * 128], rhs=wo[:, wb, kc, :],
                                                   start=(kc == 0), stop=(kc == KC - 1))
                                return ins
                            T.op('pe', mm, reads=['onTg', ('wo', wb)], writes=[('ppa', slot)])
                            T.op('dve', lambda e: e.tensor_tensor(out=yacc[:, tt, cg * 512:(cg + 1) * 512],
                                                                  in0=yacc[:, tt, cg * 512:(cg + 1) * 512], in1=ppa[:, slot, :], op=ALU.add),
                                 reads=[('ppa', slot), ('yacc', tt)], writes=[('yacc', tt)])
                    for tt in range(GT):
                        b = tt % 2
                        T.op('act', lambda e: e.activation(out=hn2[:, b, :], in_=yacc[:, tt, :], func=AF.Square,
                                                           accum_out=ss2[:, tt:tt + 1]),
                             reads=[('yacc', tt)], writes=[('hn2', b), ('ss2', tt)])
                        T.op('act', lambda e: e.activation(out=ss2[:, tt:tt + 1], in_=ss2[:, tt:tt + 1], func=AF.Ln,
                                                           scale=1.0 / D, bias=epsb[:]),
                             reads=[('ss2', tt), 'epsb'], writes=[('ss2', tt)])
                        T.op('act', lambda e: e.activation(out=ss2[:, tt:tt + 1], in_=ss2[:, tt:tt + 1], func=AF.Exp, scale=-0.5),
                             reads=[('ss2', tt)], writes=[('ss2', tt)])
                        T.op('dve', lambda e: e.scalar_tensor_tensor(out=hn2[:, b, :], in0=yacc[:, tt, :], scalar=ss2[:, tt:tt + 1],
                                                                     in1=gam2[:], op0=ALU.mult, op1=ALU.mult),
                             reads=[('yacc', tt), ('ss2', tt), 'gam2'], writes=[('hn2', b)])

                        def tr(e):
                            for kc in range(KC):
                                ins = e.transpose(out=pst2[:, b, kc, :], in_=hn2[:, b, kc * 128:(kc + 1) * 128], identity=ident[:])
                            return ins
                        T.op('pe', tr, reads=[('hn2', b), 'ident'], writes=[('pst2', b)])
                        T.op('act', lambda e: e.copy(out=h2T[:, :, tt * 128:(tt + 1) * 128], in_=pst2[:, b, :, :]),
                             reads=[('pst2', b)], writes=['h2T'])
                    T.barrier()
                with ExitStack() as es:
                    wg = es.enter_context(nc.sbuf_tensor(uname("wg"), [128, KC, 512], BF16))
                    wu = es.enter_context(nc.sbuf_tensor(uname("wu"), [128, KC, 512], BF16))
                    wd = es.enter_context(nc.sbuf_tensor(uname("wd"), [128, 4, D], BF16))
                    aT = es.enter_context(nc.sbuf_tensor(uname("aT"), [128, 2, 4, G], BF16))
                    sg = es.enter_context(nc.sbuf_tensor(uname("sg"), [128, 2, 512], F32))
                    pg = es.enter_context(nc.psum_tensor(uname("pg"), [128, 4, 512], F32))
                    pd = es.enter_context(nc.psum_tensor(uname("pd"), [128, 4, 512], F32))
                    kg = 0
                    kd_ = 0
                    ks = 0
                    for jb in range(NJ // 4):
                        ab = jb % 2
                        T.dma('pool', wg[:], ffn_w_gate[L, :, jb * 512:(jb + 1) * 512].rearrange("(kc p) c -> p kc c", p=128),
                              writes=['wg'], stream='w')
                        T.dma('pool', wu[:], ffn_w_up[L, :, jb * 512:(jb + 1) * 512].rearrange("(kc p) c -> p kc c", p=128),
                              writes=['wu'], stream='w')
                        T.dma('pool', wd[:], ffn_w_down[L, jb * 512:(jb + 1) * 512, :].rearrange("(j p) c -> p j c", p=128),
                              writes=['wd'], stream='w')
                        for c4 in range(4):
                            for tg in range(G // 512):
                                s1 = kg % 4
                                s2 = (kg + 1) % 4
                                kg += 2
                                sb = ks % 2
                                ks += 1

                                def mmg(e):
                                    for kc in range(KC):
                                        ins = e.matmul(pg[:, s1, :], lhsT=wg[:, kc, c4 * 128:(c4 + 1) * 128], rhs=h2T[:, kc, tg * 512:(tg + 1) * 512],
                                                       start=(kc == 0), stop=(kc == KC - 1))
                                    return ins

                                def mmu(e):
                                    for kc in range(KC):
                                        ins = e.matmul(pg[:, s2, :], lhsT=wu[:, kc, c4 * 128:(c4 + 1) * 128], rhs=h2T[:, kc, tg * 512:(tg + 1) * 512],
                                                       start=(kc == 0), stop=(kc == KC - 1))
                                    return ins
                                T.op('pe', mmg, reads=['wg'], writes=[('pg', s1)])
                                T.op('pe', mmu, reads=['wu'], writes=[('pg', s2)])
                                T.op('act', lambda e: e.activation(out=sg[:, sb, :], in_=pg[:, s1, :], func=AF.Silu),
                                     reads=[('pg', s1)], writes=[('sg', sb)])
                                T.op('dve', lambda e: e.tensor_tensor(out=aT[:, ab, c4, tg * 512:(tg + 1) * 512], in0=sg[:, sb, :],
                                                                      in1=pg[:, s2, :], op=ALU.mult),
                                     reads=[('sg', sb), ('pg', s2)], writes=[('aT', ab)])
                        for tt in range(GT):
                            for cg in range(4):
                                sd = kd_ % 4
                                kd_ += 1

                                def mmd(e):
                                    for c4 in range(4):
                                        ins = e.matmul(pd[:, sd, :], lhsT=aT[:, ab, c4, tt * 128:(tt + 1) * 128], rhs=wd[:, c4, cg * 512:(cg + 1) * 512],
                                                       start=(c4 == 0), stop=(c4 == 3))
                                    return ins
                                T.op('pe', mmd, reads=[('aT', ab), 'wd'], writes=[('pd', sd)])
                                T.op('dve', lambda e: e.tensor_tensor(out=yacc[:, tt, cg * 512:(cg + 1) * 512],
                                                                      in0=yacc[:, tt, cg * 512:(cg + 1) * 512], in1=pd[:, sd, :], op=ALU.add),
                                     reads=[('pd', sd), ('yacc', tt)], writes=[('yacc', tt)])
                    T.barrier()
                if after_group is not None:
                    after_group(s, g, t0, yacc, h2T)
                T.barrier()

    QT_d = nc.dram_tensor("QT_d", [2, 16, 128, TS], BF16).ap()
    kt_d = [[nc.dram_tensor(f"kt_d{s}_{i}", [512, TS], BF16) for i in range(4)] for s in range(2)]
    v_d = [[nc.dram_tensor(f"v_d{s}_{j}", [TS, 512], BF16) for j in range(4)] for s in range(2)]
    KTg = [nc.dram_tensor(f"KTg{i}", [NCORES * 512, TS], BF16) for i in range(4)]
    Vg = [nc.dram_tensor(f"Vg{j}", [NCORES * TS, 512], BF16) for j in range(4)]

    neglam = nc.alloc_sbuf_tensor("neglam", [128, 1], F32)
    swl = nc.alloc_sbuf_tensor("swl", [128, 256], F32)
    ebias = nc.alloc_sbuf_tensor("ebias", [128, 1], F32)
    lamt = nc.alloc_sbuf_tensor("lamt", [128, 4, 128], F32)
    lam2 = nc.alloc_sbuf_tensor("lam2", [128, 2, 128], F32)
    lams = nc.alloc_sbuf_tensor("lams", [128, 2], F32)
    for i, ap_ in enumerate((lq1, lk1, lq2, lk2)):
        T.dma('sp', lamt[:, i, :], ap_[0, :].partition_broadcast(128), writes=['lamt'], stream='c')
    T.dma('sp', swl[:], diff_subln[0, :].partition_broadcast(128), writes=['swl'], stream='c')
    T.op('dve', lambda e: e.tensor_tensor(out=lam2[:], in0=lamt[:, 0:4:2, :], in1=lamt[:, 1:4:2, :], op=ALU.mult), reads=['lamt'], writes=['lam2'])
    T.op('dve', lambda e: e.tensor_reduce(out=lams[:], in_=lam2[:], axis=mybir.AxisListType.X, op=ALU.add), reads=['lam2'], writes=['lams'])
    T.op('act', lambda e: e.activation(out=lams[:], in_=lams[:], func=AF.Exp), reads=['lams'], writes=['lams'])
    T.op('dve', lambda e: e.tensor_tensor(out=neglam[:], in0=lams[:, 1:2], in1=lams[:, 0:1], op=ALU.subtract), reads=['lams'], writes=['neglam'])
    T.op('dve', lambda e: e.tensor_scalar(out=neglam[:], in0=neglam[:], scalar1=-LAMBDA_INIT, scalar2=None, op0=ALU.add), reads=['neglam'], writes=['neglam'])
    T.op('dve', lambda e: e.tensor_scalar(out=swl[:], in0=swl[:], scalar1=1.0 - LAMBDA_INIT, scalar2=None, op0=ALU.mult), reads=['swl'], writes=['swl'])
    T.op('dve', lambda e: e.memset(ebias[:], 0.0), writes=['ebias'])

    def l0_after(s, g, t0, yacc, h2T):
        for tt in range(GT):
            T.dma('sp', x1_d[s, t0 + tt * 128:t0 + (tt + 1) * 128, :], yacc[:, tt, :], reads=[('yacc', tt)],
                  writes=[('x1_d', s, g)], stream='o')
            if debug:
                T.dma('sp', dbg['x1'][s, t0 + tt * 128:t0 + (tt + 1) * 128, :], yacc[:, tt, :], reads=[('yacc', tt)],
                      writes=[('dbg_x1', s, g)], stream='o')
        if stage < 2:
            return
        h1T = h2T
        with ExitStack() as es:
            hn1 = es.enter_context(nc.sbuf_tensor(uname("hn1"), [128, 2, D], BF16))
            gam1 = es.enter_context(nc.sbuf_tensor(uname("gam1"), [128, D], F32))
            ss1 = es.enter_context(nc.sbuf_tensor(uname("ss1"), [128, GT], F32))
            Wb = es.enter_context(nc.sbuf_tensor(uname("Wb"), [128, 2, KC, 512], BF16))
            rope = es.enter_context(nc.sbuf_tensor(uname("rope"), [128, GT, 2, 64], F32))
            xq = es.enter_context(nc.sbuf_tensor(uname("xq"), [128, 2, 512], F32))
            XC = es.enter_context(nc.sbuf_tensor(uname("XC"), [128, 2, 512], F32))
            XS = es.enter_context(nc.sbuf_tensor(uname("XS"), [128, 2, 512], F32))
            rq = es.enter_context(nc.sbuf_tensor(uname("rq"), [128, 2, 512], BF16))
            stq = es.enter_context(nc.sbuf_tensor(uname("stq"), [128, 2, 4, G], BF16))
            vst = es.enter_context(nc.sbuf_tensor(uname("vst"), [128, 2, GT, 512], BF16))
            pst1 = es.enter_context(nc.psum_tensor(uname("pst1"), [128, 2, KC, 128], BF16))
            ppq = es.enter_context(nc.psum_tensor(uname("ppq"), [128, 2, 512], F32))
            ptq = es.enter_context(nc.psum_tensor(uname("ptq"), [128, 2, 4, 128], BF16))
            T.dma('sp', gam1[:], norm_mixer[1, :].partition_broadcast(128), writes=['gam1'], stream='c')
            T.dma('sp', rope[:], c_rope[s, t0:t0 + G, :, :].rearrange("(tt p) a f -> p tt a f", p=128), writes=['rope'], stream='c')
            for tt in range(GT):
                b = tt % 2
                T.op('act', lambda e: e.activation(out=hn1[:, b, :], in_=yacc[:, tt, :], func=AF.Square,
                                                   accum_out=ss1[:, tt:tt + 1]),
                     reads=[('yacc', tt)], writes=[('hn1', b), ('ss1', tt)])
                T.op('act', lambda e: e.activation(out=ss1[:, tt:tt + 1], in_=ss1[:, tt:tt + 1], func=AF.Ln,
                                                   scale=1.0 / D, bias=epsb[:]),
                     reads=[('ss1', tt), 'epsb'], writes=[('ss1', tt)])
                T.op('act', lambda e: e.activation(out=ss1[:, tt:tt + 1], in_=ss1[:, tt:tt + 1], func=AF.Exp, scale=-0.5),
                     reads=[('ss1', tt)], writes=[('ss1', tt)])
                T.op('dve', lambda e: e.scalar_tensor_tensor(out=hn1[:, b, :], in0=yacc[:, tt, :], scalar=ss1[:, tt:tt + 1],
                                                             in1=gam1[:], op0=ALU.mult, op1=ALU.mult),
                     reads=[('yacc', tt), ('ss1', tt), 'gam1'], writes=[('hn1', b)])

                def tr(e):
                    for kc in range(KC):
                        ins = e.transpose(out=pst1[:, b, kc, :], in_=hn1[:, b, kc * 128:(kc + 1) * 128], identity=ident[:])
                    return ins
                T.op('pe', tr, reads=[('hn1', b), 'ident'], writes=[('pst1', b)])
                T.op('act', lambda e: e.copy(out=h1T[:, :, tt * 128:(tt + 1) * 128], in_=pst1[:, b, :, :]),
                     reads=[('pst1', b)], writes=['h1T'])
            k = 0
            for blk in range(12):
                wb = blk % 2
                T.dma('pool', Wb[:, wb, :, :], diff_w_in[0, :, blk * 512:(blk + 1) * 512].rearrange("(kc p) c -> p kc c", p=128),
                      writes=[('Wb', wb)], stream='w')
                for tt in range(GT):
                    sl = k % 2
                    k += 1

                    def mm(e):
                        for kc in range(KC):
                            ins = e.matmul(ppq[:, sl, :], lhsT=h1T[:, kc, tt * 128:(tt + 1) * 128], rhs=Wb[:, wb, kc, :],
                                           start=(kc == 0), stop=(kc == KC - 1))
                        return ins
                    T.op('pe', mm, reads=['h1T', ('Wb', wb)], writes=[('ppq', sl)])
                    if blk < 8:
                        T.op('act', lambda e: e.copy(out=xq[:, sl, :], in_=ppq[:, sl, :]), reads=[('ppq', sl)], writes=[('xq', sl)])
                        x4 = xq[:, sl, :].rearrange("p (m a f) -> p m a f", m=4, a=2)
                        c4_ = XC[:, sl, :].rearrange("p (m a f) -> p m a f", m=4, a=2)
                        s4_ = XS[:, sl, :].rearrange("p (m a f) -> p m a f", m=4, a=2)
                        r4_ = rq[:, sl, :].rearrange("p (m a f) -> p m a f", m=4, a=2)
                        cosb = rope[:, tt, 0, :].unsqueeze(1).unsqueeze(1).to_broadcast([128, 4, 2, 64])
                        sinb = rope[:, tt, 1, :].unsqueeze(1).unsqueeze(1).to_broadcast([128, 4, 2, 64])
                        T.op('pool', lambda e: e.tensor_tensor(out=c4_, in0=x4, in1=cosb, op=ALU.mult),
                             reads=[('xq', sl), 'rope'], writes=[('XC', sl)])
                        T.op('pool', lambda e: e.tensor_tensor(out=s4_, in0=x4, in1=sinb, op=ALU.mult),
                             reads=[('xq', sl), 'rope'], writes=[('XS', sl)])
                        T.op('dve', lambda e: e.tensor_tensor(out=r4_[:, :, 0, :], in0=c4_[:, :, 0, :], in1=s4_[:, :, 1, :], op=ALU.subtract),
                             reads=[('XC', sl), ('XS', sl)], writes=[('rq', sl)])
                        T.op('dve', lambda e: e.tensor_tensor(out=r4_[:, :, 1, :], in0=c4_[:, :, 1, :], in1=s4_[:, :, 0, :], op=ALU.add),
                             reads=[('XC', sl), ('XS', sl)], writes=[('rq', sl)])

                        def trq(e):
                            for m in range(4):
                                ins = e.transpose(out=ptq[:, sl, m, :], in_=rq[:, sl, m * 128:(m + 1) * 128], identity=ident[:])
                            return ins
                        T.op('pe', trq, reads=[('rq', sl), 'ident'], writes=[('ptq', sl)])
                        T.op('act', lambda e: e.activation(out=stq[:, wb, :, tt * 128:(tt + 1) * 128], in_=ptq[:, sl, :, :], func=AF.Copy,
                                                           scale=(QSCALE if blk < 4 else 1.0)),
                             reads=[('ptq', sl)], writes=[('stq', wb)])
                    else:
                        T.op('act', lambda e: e.copy(out=vst[:, wb, tt, :], in_=ppq[:, sl, :]), reads=[('ppq', sl)], writes=[('vst', wb)])
                if blk < 4:
                    for m in range(4):
                        T.dma('sp', QT_d[s, blk * 4 + m, :, t0:t0 + G], stq[:, wb, m, :], reads=[('stq', wb)],
                              writes=[('QT_d', s)], stream='o')
                elif blk < 8:
                    for m in range(4):
                        mp = (blk - 4) * 4 + m
                        T.dma('sp', kt_d[s][blk - 4].ap()[m * 128:(m + 1) * 128, t0:t0 + G], stq[:, wb, m, :], reads=[('stq', wb)],
                              writes=[('kt_d', s, blk - 4)], stream='o')
                else:
                    T.dma('sp', v_d[s][blk - 8].ap()[t0:t0 + G, :].rearrange("(tt p) c -> p tt c", p=128),
                          vst[:, wb, :, :], reads=[('vst', wb)], writes=[('v_d', s, blk - 8)], stream='o')
            T.barrier()

    def attention(s):
        nk = TS if s == 1 else NCORES * TS
        nkt = nk // 128
        with ExitStack() as es:
            Qh = es.enter_context(nc.sbuf_tensor(uname("Qh"), [128, 2, TS], BF16))
            Kh = es.enter_context(nc.sbuf_tensor(uname("Kh"), [128, 2, nk], BF16))
            Vh = es.enter_context(nc.sbuf_tensor(uname("Vh"), [128, nkt, 257], BF16))
            pT = es.enter_context(nc.sbuf_tensor(uname("pT"), [128, 3, 512], BF16))
            A1 = es.enter_context(nc.sbuf_tensor(uname("A1"), [128, 16, 256], F32))
            tmpA = es.enter_context(nc.sbuf_tensor(uname("tmpA"), [128, 2, 256], F32))
            rz = es.enter_context(nc.sbuf_tensor(uname("rz"), [128, 8], F32))
            ssn = es.enter_context(nc.sbuf_tensor(uname("ssn"), [128, 8], F32))
            onq = es.enter_context(nc.sbuf_tensor(uname("onq"), [128, 2, 256], BF16))
            ost = es.enter_context(nc.sbuf_tensor(uname("ost"), [128, 2, TS], BF16))
            psc = es.enter_context(nc.psum_tensor(uname("psc"), [128, 3, 512], F32))
            pav = es.enter_context(nc.psum_tensor(uname("pav"), [128, 4, 512], F32))
            pto = es.enter_context(nc.psum_tensor(uname("pto"), [128, 2, 2, 128], BF16))
            T.op('dve', lambda e: e.memset(Vh[:, :, 256:257], 1.0), writes=[('Vh', 0), ('Vh', 1)])
            ksc = 0
            kz = 0
            nvh = 2 if s == 1 else NCORES
            for hd in range(8):
                for m in range(2):
                    mp = 2 * hd + m
                    T.dma('pool', Qh[:, m, :], QT_d[s, mp, :, :], reads=[('QT_d', s)], writes=[('Qh', m)], stream='a')
                    if s == 1:
                        T.dma('pool', Kh[:, m, :], kt_d[1][mp // 4].ap()[(mp % 4) * 128:(mp % 4 + 1) * 128, :], reads=[('kt_d', 1, mp // 4)],
                              writes=[('Kh', m, 0), ('Kh', m, 1)], stream='a')
                    else:
                        for rk in range(NCORES):
                            T.dma('pool', Kh[:, m, rk * TS:(rk + 1) * TS],
                                  KTg[mp // 4].ap()[rk * 512 + (mp % 4) * 128:rk * 512 + (mp % 4 + 1) * 128, :], reads=[('KTg', mp // 4)],
                                  writes=[('Kh', m, rk // 4)], stream='a')
                if s == 1:
                    T.dma('pool', Vh[:, :, 0:256], v_d[1][hd // 2].ap()[:, (hd % 2) * 256:(hd % 2 + 1) * 256].rearrange("(kt p) e -> p kt e", p=128),
                          reads=[('v_d', 1, hd // 2)], writes=[('Vh', 0), ('Vh', 1)], stream='a')
                else:
                    for rk in range(NCORES):
                        T.dma('pool', Vh[:, rk * NT:(rk + 1) * NT, 0:256],
                              Vg[hd // 2].ap()[rk * TS:(rk + 1) * TS, (hd % 2) * 256:(hd % 2 + 1) * 256].rearrange("(kt p) e -> p kt e", p=128),
                              reads=[('Vg', hd // 2)], writes=[('Vh', rk // 4)], stream='a')
                items = [(m, qg, kt) for m in range(2) for qg in range(4) for kt in range(nkt)]
                slots_of = {}

                def emit_qk(idx):
                    nonlocal ksc
                    m, qg, kt = items[idx]
                    half = (kt * 2) // nkt
                    sl = ksc % 3
                    ksc += 1
                    slots_of[idx] = sl
                    T.op('pe', lambda e: e.matmul(psc[:, sl, :], lhsT=Kh[:, m, kt * 128:(kt + 1) * 128],
                                                  rhs=Qh[:, m, qg * 512:(qg + 1) * 512], start=True, stop=True),
                         reads=[('Kh', m, half), ('Qh', m)], writes=[('psc', sl)])
                    T.op('act', lambda e: e.activation(out=pT[:, sl, :], in_=psc[:, sl, :], func=AF.Exp),
                         reads=[('psc', sl)], writes=[('pT', sl)])

                def emit_av(idx):
                    m, qg, kt = items[idx]
                    half = (kt * 2) // nkt
                    sl = slots_of.pop(idx)

                    def av(e):
                        for qt in range(4):
                            ins = e.matmul(pav[:, qt, 0:257], lhsT=pT[:, sl, qt * 128:(qt + 1) * 128], rhs=Vh[:, kt, :],
                                           start=(kt == 0), stop=(kt == nkt - 1))
                        return ins
                    T.op('pe', av, reads=[('pT', sl), ('Vh', half)], writes=['pav'])

                def emit_evac(m, qg):
                    nonlocal kz
                    for qt in range(4):
                        zi = kz % 8
                        kz += 1
                        tq = qg * 4 + qt
                        T.op('dve', lambda e: e.reciprocal(out=rz[:, zi:zi + 1], in_=pav[:, qt, 256:257]), reads=['pav'], writes=[('rz', zi)])
                        if m == 0:
                            T.op('dve', lambda e: e.tensor_scalar(out=A1[:, tq, :], in0=pav[:, qt, 0:256], scalar1=rz[:, zi:zi + 1],
                                                                  scalar2=None, op0=ALU.mult),
                                 reads=['pav', ('rz', zi)], writes=[('A1', tq)])
                        else:
                            tb = qt % 2
                            T.op('dve', lambda e: e.tensor_scalar(out=tmpA[:, tb, :], in0=pav[:, qt, 0:256], scalar1=rz[:, zi:zi + 1],
                                                                  scalar2=neglam[:, 0:1], op0=ALU.mult, op1=ALU.mult),
                                 reads=['pav', ('rz', zi), 'neglam'], writes=[('tmpA', tb)])
                            T.op('dve', lambda e: e.tensor_tensor(out=tmpA[:, tb, :], in0=tmpA[:, tb, :], in1=A1[:, tq, :], op=ALU.add),
                                 reads=[('tmpA', tb), ('A1', tq)], writes=[('tmpA', tb)])
                            T.op('act', lambda e: e.activation(out=onq[:, tb, :], in_=tmpA[:, tb, :], func=AF.Square,
                                                               accum_out=ssn[:, zi:zi + 1]),
                                 reads=[('tmpA', tb)], writes=[('onq', tb), ('ssn', zi)])
                            T.op('act', lambda e: e.activation(out=ssn[:, zi:zi + 1], in_=ssn[:, zi:zi + 1], func=AF.Ln,
                                                               scale=1.0 / 256, bias=epsb[:]),
                                 reads=[('ssn', zi), 'epsb'], writes=[('ssn', zi)])
                            T.op('act', lambda e: e.activation(out=ssn[:, zi:zi + 1], in_=ssn[:, zi:zi + 1], func=AF.Exp, scale=-0.5),
                                 reads=[('ssn', zi)], writes=[('ssn', zi)])
                            T.op('dve', lambda e: e.scalar_tensor_tensor(out=onq[:, tb, :], in0=tmpA[:, tb, :], scalar=ssn[:, zi:zi + 1],
                                                                         in1=swl[:], op0=ALU.mult, op1=ALU.mult),
                                 reads=[('tmpA', tb), ('ssn', zi), 'swl'], writes=[('onq', tb)])

                            def tro(e):
                                for c2 in range(2):
                                    ins = e.transpose(out=pto[:, tb, c2, :], in_=onq[:, tb, c2 * 128:(c2 + 1) * 128], identity=ident[:])
                                return ins
                            T.op('pe', tro, reads=[('onq', tb), 'ident'], writes=[('pto', tb)])
                            T.op('act', lambda e: e.copy(out=ost[:, :, tq * 128:(tq + 1) * 128], in_=pto[:, tb, :, :]),
                                 reads=[('pto', tb)], writes=['ost'])

                emit_qk(0)
                for idx in range(len(items)):
                    if idx + 1 < len(items):
                        emit_qk(idx + 1)
                    emit_av(idx)
                    m_, qg_, kt_ = items[idx]
                    if kt_ == nkt - 1:
                        emit_evac(m_, qg_)
                for c2 in range(2):
                    T.dma('sp', onT_d[s, 2 * hd + c2], ost[:, c2, :], reads=['ost'], writes=[('onT_d', s, 2 * hd + c2)], stream='o')
            T.barrier()

    def final_after(s, g, t0, yacc, h2T):
        with ExitStack() as es:
            gamf = es.enter_context(nc.sbuf_tensor(uname("gamf"), [128, D], F32))
            ssf = es.enter_context(nc.sbuf_tensor(uname("ssf"), [128, GT], F32))
            junk = es.enter_context(nc.sbuf_tensor(uname("junk"), [128, 2, D], BF16))
            T.dma('sp', gamf[:], norm_final.partition_broadcast(128), writes=['gamf'], stream='c')
            for tt in range(GT):
                b = tt % 2
                T.op('act', lambda e: e.activation(out=junk[:, b, :], in_=yacc[:, tt, :], func=AF.Square,
                                                   accum_out=ssf[:, tt:tt + 1]),
                     reads=[('yacc', tt)], writes=[('junk', b), ('ssf', tt)])
                T.op('act', lambda e: e.activation(out=ssf[:, tt:tt + 1], in_=ssf[:, tt:tt + 1], func=AF.Ln,
                                                   scale=1.0 / D, bias=epsb[:]),
                     reads=[('ssf', tt), 'epsb'], writes=[('ssf', tt)])
                T.op('act', lambda e: e.activation(out=ssf[:, tt:tt + 1], in_=ssf[:, tt:tt + 1], func=AF.Exp, scale=-0.5),
                     reads=[('ssf', tt)], writes=[('ssf', tt)])
                T.op('dve', lambda e: e.scalar_tensor_tensor(out=yacc[:, tt, :], in0=yacc[:, tt, :], scalar=ssf[:, tt:tt + 1],
                                                             in1=gamf[:], op0=ALU.mult, op1=ALU.mult),
                     reads=[('yacc', tt), ('ssf', tt), 'gamf'], writes=[('yacc', tt)])
                T.dma('sp', y_out[s, t0 + tt * 128:t0 + (tt + 1) * 128, :], yacc[:, tt, :], reads=[('yacc', tt)],
                      writes=[('y', s, g, tt)], stream='y')
            T.barrier()

    def gather(src_t, dst_t, reads, writes):
        if fake_gather:
            n = src_t.ap().shape[0]
            for rk in range(NCORES):
                T.dma('sp', dst_t.ap()[rk * n:(rk + 1) * n, :], src_t.ap(), reads=reads, writes=writes, stream='g')
            return
        T._waits('pool', T._deps(reads, writes))
        sem = T._new('cc')
        ins = nc.gpsimd.collective_compute("AllGather", ALU.bypass, replica_groups=[list(range(NCORES))],
                                           ins=[src_t.ap().opt()], outs=[dst_t.ap().opt()])
        ins.then_inc(sem)
        T.ninst += 1
        T._record((sem, 1, 'cc'), reads, writes)

    def l0_fix():
        with ExitStack() as es:
            Ob = es.enter_context(nc.sbuf_tensor(uname("fOb"), [128, TS], F32))
            gsb = es.enter_context(nc.sbuf_tensor(uname("fgs"), [128, TS], BF16))
            qeb = es.enter_context(nc.sbuf_tensor(uname("fqeb"), [128, 2, TS], BF16))
            ke = es.enter_context(nc.sbuf_tensor(uname("fke"), [128, TS], BF16))
            E1 = es.enter_context(nc.sbuf_tensor(uname("fE1"), [128, TS], F32))
            kdT = es.enter_context(nc.sbuf_tensor(uname("fkdT"), [128, TS], BF16))
            Sg = es.enter_context(nc.sbuf_tensor(uname("fSg"), [128, 2, NCORES, 128], F32))
            DGt = es.enter_context(nc.sbuf_tensor(uname("fDGt"), [128, NCORES, 2 * KC], F32))
            acoef = es.enter_context(nc.sbuf_tensor(uname("facoef"), [128, 2, NCORES], F32))
            acc = es.enter_context(nc.sbuf_tensor(uname("facc"), [128, 2, 128], F32))
            accb = es.enter_context(nc.sbuf_tensor(uname("faccb"), [128, 2, 128], BF16))
            cm = es.enter_context(nc.sbuf_tensor(uname("fcm"), [128, 2, NCORES], F32))
            pp = es.enter_context(nc.psum_tensor(uname("fpp"), [128, 2, 512], F32))
            pc = es.enter_context(nc.psum_tensor(uname("fpc"), [128, 2, 512], F32))
            st = {'pp': 0}
            kpc = 0
            oall = [('O', tt) for tt in range(NT)]
            T.dma('sp', cm[:], c_cmask, writes=['cm'], stream='c')
            SGv = SG.ap().rearrange("(k x) e -> x k e", k=NCORES)
            T.dma('sp', DGt[:], DG.ap().rearrange("(k d) c -> d k c", k=NCORES)[:, :, 0:2 * KC], reads=['DG'], writes=['DGt'], stream='c')
            for h in range(KC):
                T.dma('sp', Ob[:], Osave_d[h], reads=[('Osave_d', h)], writes=oall, stream='x')
                T.dma('sp', gsb[:], gs_d[h], reads=[('gs_d', h)], writes=['gs'], stream='x')
                for r in range(2):
                    T.dma('sp', qeb[:, r, :], qeb_d[h, r], reads=[('qeb_d', h)], writes=[('qeb', r)], stream='x')
                    T.dma('sp', Sg[:, r, :, :], SGv[(h * 2 + r) * 128:(h * 2 + r + 1) * 128, :, :],
                          reads=['SG'], writes=[('Sg', r)], stream='x')
                    col = r * KC + h
                    T.op('dve', lambda e: e.tensor_scalar(out=acoef[:, r, :], in0=DGt[:, :, col], scalar1=-1.0, scalar2=None, op0=ALU.add),
                         reads=['DGt'], writes=[('acoef', r)])
                    T.op('dve', lambda e: e.tensor_tensor(out=acoef[:, r, :], in0=acoef[:, r, :], in1=cm[:, r, :], op=ALU.mult),
                         reads=[('acoef', r), 'cm'], writes=[('acoef', r)])
                    T.op('dve', lambda e: e.tensor_scalar(out=acoef[:, r, :], in0=acoef[:, r, :], scalar1=1.0, scalar2=None, op0=ALU.add),
                         reads=[('acoef', r)], writes=[('acoef', r)])
                    T.op('dve', lambda e: e.tensor_tensor(out=Sg[:, r, :, :], in0=Sg[:, r, :, :],
                                                          in1=cm[:, r, :].unsqueeze(2).to_broadcast([128, NCORES, 128]), op=ALU.mult),
                         reads=[('Sg', r), 'cm'], writes=[('Sg', r)])
                    order = list(range(NCORES)) if r == 0 else list(range(NCORES - 1, -1, -1))
                    for idx, cp in enumerate(order):
                        if idx == 0:
                            T.op('dve', lambda e: e.tensor_copy(out=acc[:, r, :], in_=Sg[:, r, cp, :]), reads=[('Sg', r)], writes=[('acc', r)])
                        else:
                            T.op('dve', lambda e: e.scalar_tensor_tensor(out=acc[:, r, :], in0=acc[:, r, :], scalar=acoef[:, r, cp:cp + 1],
                                                                         in1=Sg[:, r, cp, :], op0=ALU.mult, op1=ALU.add),
                                 reads=[('Sg', r), ('acc', r), ('acoef', r)], writes=[('acc', r)])
                    T.op('act', lambda e: e.copy(out=accb[:, r, :], in_=acc[:, r, :]), reads=[('acc', r)], writes=[('accb', r)])
                    for tg in range(4):
                        sl = kpc % 2
                        kpc += 1
                        T.op('pe', lambda e: e.matmul(pc[:, sl, :], lhsT=accb[:, r, :], rhs=qeb[:, r, tg * 512:(tg + 1) * 512],
                                                      start=True, stop=True),
                             reads=[('accb', r), ('qeb', r)], writes=[('pc', sl)])
                        T.op('dve', lambda e: e.tensor_tensor(out=Ob[:, tg * 512:(tg + 1) * 512], in0=Ob[:, tg * 512:(tg + 1) * 512],
                                                              in1=pc[:, sl, :], op=ALU.add),
                             reads=[('pc', sl)] + oall, writes=oall)
                finalize_head(0, h, Ob, gsb, ke, E1, kdT, pp, st)
            T.barrier()

    if slots is None:
        slots = [1] if debug else [0, 1]
    exch = (0 in slots) and not no_exchange
    if 0 in slots:
        l0_mixer(0)
        if exch:
            T.barrier()
            gather(Send_d, SG, ['Send_d'], ['SG'])
            gather(Dtot_d, DG, ['Dtot_d'], ['DG'])
            T.barrier(['pool'])
    if 1 in slots:
        l0_mixer(1)
    if exch:
        l0_fix()
    for s in slots:
        if stage >= 1:
            post_mixer(s, 0, l0_after)
        if s == 0 and stage >= 3:
            T.barrier()
            for i in range(4):
                gather(kt_d[0][i], KTg[i], [('kt_d', 0, i)], [('KTg', i)])
                gather(v_d[0][i], Vg[i], [('v_d', 0, i)], [('Vg', i)])
            T.barrier(['pool'])
    if stage >= 3:
        for s in reversed(slots):
            attention(s)
            post_mixer(s, 1, final_after)
    T.p2_free = P2_FREE
    T.final_wait('sp')
    return nc, T


def host_consts():
    ident = np.eye(128, dtype=np.float32)
    p = np.arange(128) % 32
    t = np.arange(32)
    m = np.zeros((128, 2, 32), np.float32)
    m[:, 0, :] = (p[:, None] <= t[None, :])
    m[:, 1, :] = (p[:, None] >= t[None, :])
    return ident, m


def make_in_maps(inputs):
    ident, m = host_consts()
    xp = np.asarray(inputs['x_prompt'], np.float32)
    xs = np.asarray(inputs['x_sample'], np.float32)
    inv_freq = (10000.0 ** (-np.arange(0, 128, 2, dtype=np.float32) / 128)).astype(np.float32)
    maps = []
    shared = {k: np.ascontiguousarray(np.asarray(v, np.float32)) for k, v in inputs.items()
              if k not in ('x_prompt', 'x_sample')}
    for c in range(NCORES):
        x = np.stack([xp[0, c * TS:(c + 1) * TS, :], xs[c]], axis=0)
        rope = np.zeros((2, TS, 2, 64), np.float32)
        for slot, pos0 in ((0, c * TS), (1, 0)):
            ang = (np.arange(pos0, pos0 + TS, dtype=np.float32)[:, None] * inv_freq[None, :]).astype(np.float32)
            rope[slot, :, 0, :] = np.cos(ang)
            rope[slot, :, 1, :] = np.sin(ang)
        cm = np.zeros((128, 2, NCORES), np.float32)
        cm[:, 0, :c] = 1.0
        cm[:, 1, c + 1:] = 1.0
        d = dict(shared)
        d.update({'x': np.ascontiguousarray(x), 'c_ident': ident, 'c_masks': m, 'c_rope': rope, 'c_cmask': cm})
        maps.append(d)
    return maps


_NC_CACHE = {}


def kernel(**inputs):
    if 'nc' not in _NC_CACHE:
        _NC_CACHE['nc'] = build_nc()[0]
    nc = _NC_CACHE['nc']
    maps = make_in_maps(inputs)
    res = run_bass_kernel_spmd(nc, maps, core_ids=list(range(NCORES)))
    ys = [np.asarray(r['y'], dtype=np.float32) for r in res.results]
    y_prompt = np.concatenate([y[0] for y in ys], axis=0)[None]
    y_sample = np.stack([y[1] for y in ys], axis=0)
    return (y_prompt, y_sample)
```
